# Optimizing a Trainium2 kernel written in Bass

```python
import math
import jax
import jax.numpy as jnp
from jax import lax
import numpy as np

D_MODEL = 1024
BATCH = 4
SEQ = 4096
DEPTH = 2

GRID_W = 64
CTX_LEN = 256
HEAD_DIM = 64
N_BRANCH = 4
BRANCH_WIDTH = 512
NA_HEADS = 8
NA_WIN_ROWS = 8
NA_WIN_COLS = 16
SWA_HEADS = 8
SWA_KV_HEADS = 2
SWA_WINDOW = 128
SWA_BLOCK = 128
S5_WIDTH = 512
S5_GROUP = 16
S5_GROUPS = S5_WIDTH // S5_GROUP
S5_STATE = 64
S5_DT_MIN = 0.001
S5_DT_MAX = 0.1
MLA_HEADS = 8
MLA_Q_LORA = 256
MLA_KV_LORA = 128
MLA_NOPE = 64
MLA_ROPE = 32
MLA_V = 64
MLA_BLOCK = 128
D_FF = 2816
MACARON_WEIGHT = 0.5
ROPE_BASE = 10000.0
EPS = 1e-6
N_MOD = 9
IN_SPLITS = (NA_HEADS * HEAD_DIM, NA_HEADS * HEAD_DIM, NA_HEADS * HEAD_DIM,
             SWA_HEADS * HEAD_DIM, SWA_KV_HEADS * HEAD_DIM, SWA_KV_HEADS * HEAD_DIM,
             S5_WIDTH, MLA_Q_LORA, MLA_KV_LORA, MLA_ROPE, N_BRANCH * D_MODEL)
IN_COLS = sum(IN_SPLITS)

kernel_name = 'hybrid_na_swa_s5_mla_diffusion_block'


def rms_norm(x, w):
    xf = x.astype(jnp.float32)
    y = xf * lax.rsqrt(jnp.mean(xf * xf, axis=-1, keepdims=True) + EPS)
    return (y * w.astype(jnp.float32)).astype(x.dtype)


def modulate(x, shift, scale):
    return x * (1.0 + scale) + shift


def swiglu(x, w_gate, w_up, w_down):
    return (jax.nn.silu(x @ w_gate) * (x @ w_up)) @ w_down


def macaron_half_ffn(h, shift, scale, gate, norm_w, w_gate, w_up, w_down):
    n = modulate(rms_norm(h, norm_w), shift, scale)
    return h + MACARON_WEIGHT * gate * swiglu(n, w_gate, w_up, w_down)


def split_heads(t, n_heads):
    return t.reshape(t.shape[:-1] + (n_heads, t.shape[-1] // n_heads))


def split_columns(t):
    parts, off = [], 0
    for width in IN_SPLITS:
        parts.append(t[..., off:off + width])
        off += width
    return parts


def rope_2d(x, rows, cols):
    dim = x.shape[-1]
    half_axis = dim // 2
    n_freq = half_axis // 2
    inv_freq = ROPE_BASE ** (-jnp.arange(n_freq, dtype=jnp.float32) / n_freq)

    def rotate(xa, pos):
        ang = pos.astype(jnp.float32)[:, None] * inv_freq[None, :]
        cos = jnp.cos(ang)[None, :, None, :]
        sin = jnp.sin(ang)[None, :, None, :]
        x1, x2 = xa[..., :n_freq], xa[..., n_freq:]
        return jnp.concatenate([x1 * cos - x2 * sin, x2 * cos + x1 * sin], axis=-1)

    xf = x.astype(jnp.float32)
    out = jnp.concatenate([rotate(xf[..., :half_axis], rows), rotate(xf[..., half_axis:], cols)], axis=-1)
    return out.astype(x.dtype)


def softmax_with_sink(s, sink):
    m = jnp.maximum(jnp.max(s, axis=-1, keepdims=True), sink)
    e = jnp.exp(s - m)
    return e / (jnp.sum(e, axis=-1, keepdims=True) + jnp.exp(sink - m))


def ctx_attention(q, k, v, sink=None):
    B, C, Hq, dq = q.shape
    Hk = k.shape[2]
    G = Hq // Hk
    qg = q.reshape(B, C, Hk, G, dq)
    s = jnp.einsum('bqkgd,bckd->bkgqc', qg, k).astype(jnp.float32) * dq ** -0.5
    if sink is None:
        p = jax.nn.softmax(s, axis=-1)
    else:
        p = softmax_with_sink(s, sink.astype(jnp.float32).reshape(Hk, G)[None, :, :, None, None])
    o = jnp.einsum('bkgqc,bckd->bqkgd', p.astype(v.dtype), v)
    return o.reshape(B, C, Hq * v.shape[-1])


def neighbourhood_attention(q, k, v, kc, vc, rpb):
    B, T, H, d = q.shape
    rows_n = T // GRID_W
    kr = min(NA_WIN_ROWS, rows_n)
    n_loc = kr * NA_WIN_COLS
    scale = d ** -0.5
    qg = q.reshape(B, rows_n, GRID_W, H, d)
    kg = k.reshape(B, rows_n, GRID_W, H, d)
    vg = v.reshape(B, rows_n, GRID_W, H, d)
    col = jnp.arange(GRID_W)
    c0 = jnp.clip(col - NA_WIN_COLS // 2, 0, GRID_W - NA_WIN_COLS)
    col_idx = c0[:, None] + jnp.arange(NA_WIN_COLS)[None, :]
    col_bias_idx = col_idx - col[:, None] + (NA_WIN_COLS - 1)
    rpb32 = rpb.astype(jnp.float32)

    def one_row(args):
        r, q_r = args
        r0 = jnp.clip(r - kr // 2, 0, rows_n - kr)
        k_win = lax.dynamic_slice_in_dim(kg, r0, kr, axis=1)[:, :, col_idx]
        v_win = lax.dynamic_slice_in_dim(vg, r0, kr, axis=1)[:, :, col_idx]
        row_bias_idx = r0 + jnp.arange(kr) - r + (NA_WIN_ROWS - 1)
        bias = rpb32[:, row_bias_idx[None, :, None], col_bias_idx[:, None, :]]
        s_loc = jnp.einsum('bqhd,brqjhd->bhqrj', q_r, k_win).astype(jnp.float32) * scale + bias[None]
        s_ctx = jnp.einsum('bqhd,bchd->bhqc', q_r, kc).astype(jnp.float32) * scale
        p = jax.nn.softmax(jnp.concatenate([s_loc.reshape(B, H, GRID_W, n_loc), s_ctx], axis=-1), axis=-1).astype(v.dtype)
        p_loc = p[..., :n_loc].reshape(B, H, GRID_W, kr, NA_WIN_COLS)
        return (jnp.einsum('bhqrj,brqjhd->bqhd', p_loc, v_win)
                + jnp.einsum('bhqc,bchd->bqhd', p[..., n_loc:], vc))

    out = lax.map(one_row, (jnp.arange(rows_n), jnp.moveaxis(qg, 1, 0)))
    return jnp.moveaxis(out, 0, 1).reshape(B, T, H * d)


def sliding_window_attention(q, k, v, kc, vc, sink):
    B, T, Hq, d = q.shape
    Hk = k.shape[2]
    G = Hq // Hk
    blk = SWA_BLOCK
    nb = T // blk
    scale = d ** -0.5
    qb = q.reshape(B, nb, blk, Hk, G, d)
    pad = ((0, 0), (blk, blk), (0, 0), (0, 0))
    kp = jnp.pad(k, pad).reshape(B, nb + 2, blk, Hk, d)
    vp = jnp.pad(v, pad).reshape(B, nb + 2, blk, Hk, d)
    kband = jnp.concatenate([kp[:, :-2], kp[:, 1:-1], kp[:, 2:]], axis=2)
    vband = jnp.concatenate([vp[:, :-2], vp[:, 1:-1], vp[:, 2:]], axis=2)
    qpos = jnp.arange(nb)[:, None] * blk + jnp.arange(blk)[None, :]
    kpos = (jnp.arange(nb)[:, None] - 1) * blk + jnp.arange(3 * blk)[None, :]
    valid = ((jnp.abs(qpos[:, :, None] - kpos[:, None, :]) <= SWA_WINDOW)
             & (kpos[:, None, :] >= 0) & (kpos[:, None, :] < T))
    s_loc = jnp.einsum('bnqkgd,bnckd->bnkgqc', qb, kband).astype(jnp.float32) * scale
    s_loc = jnp.where(valid[None, :, None, None], s_loc, -jnp.inf)
    s_ctx = jnp.einsum('bnqkgd,bckd->bnkgqc', qb, kc).astype(jnp.float32) * scale
    sink_b = sink.astype(jnp.float32).reshape(Hk, G)[None, None, :, :, None, None]
    p = softmax_with_sink(jnp.concatenate([s_loc, s_ctx], axis=-1), sink_b).astype(v.dtype)
    n_loc = 3 * blk
    o = (jnp.einsum('bnkgqc,bnckd->bnqkgd', p[..., :n_loc], vband)
         + jnp.einsum('bnkgqc,bckd->bnqkgd', p[..., n_loc:], vc))
    return o.reshape(B, T, Hq * d)


def mla_query(cq, p, rows, cols):
    q = split_heads(rms_norm(cq, p['mla_q_norm']) @ p['mla_w_uq'], MLA_HEADS)
    q_nope, q_rope = q[..., :MLA_NOPE], q[..., MLA_NOPE:]
    if rows is not None:
        q_rope = rope_2d(q_rope, rows, cols)
    return jnp.concatenate([q_nope, q_rope], axis=-1)


def mla_keys_values(ckv, k_rope, p, rows, cols):
    kv = split_heads(rms_norm(ckv, p['mla_kv_norm']) @ p['mla_w_ukv'], MLA_HEADS)
    k_nope, v = kv[..., :MLA_NOPE], kv[..., MLA_NOPE:]
    kr = k_rope[..., None, :]
    if rows is not None:
        kr = rope_2d(kr, rows, cols)
    k = jnp.concatenate([k_nope, jnp.broadcast_to(kr, k_nope.shape[:-1] + (MLA_ROPE,))], axis=-1)
    return k, v


def mla_dense_attention(q, k, v, kc, vc):
    B, T, H, dq = q.shape
    dv = v.shape[-1]
    scale = dq ** -0.5
    k_all = jnp.concatenate([k, kc], axis=1)
    v_all = jnp.concatenate([v, vc], axis=1)
    nb = T // MLA_BLOCK
    qb = jnp.moveaxis(q.reshape(B, nb, MLA_BLOCK, H, dq), 1, 0)

    def one_block(q_blk):
        s = jnp.einsum('bqhd,bkhd->bhqk', q_blk, k_all).astype(jnp.float32) * scale
        pr = jax.nn.softmax(s, axis=-1).astype(v.dtype)
        return jnp.einsum('bhqk,bkhd->bqhd', pr, v_all)

    o = lax.map(one_block, qb)
    return jnp.moveaxis(o, 0, 1).reshape(B, T, H * dv)


def linear_recurrence(e1, e2):
    a1, b1 = e1
    a2, b2 = e2
    return a1 * a2, a2 * b1 + b2


def s5_discretise(lam_re, lam_im, log_dt, b_re, b_im, c_re, c_im):
    lam = lax.complex(lam_re.astype(jnp.float32), lam_im.astype(jnp.float32))
    dt = jnp.exp(log_dt.astype(jnp.float32))[:, None]
    lam_bar = jnp.exp(lam * dt)
    b = lax.complex(b_re.astype(jnp.float32), b_im.astype(jnp.float32))
    b_bar = ((lam_bar - 1.0) / lam)[..., None] * b
    c = lax.complex(c_re.astype(jnp.float32), c_im.astype(jnp.float32))
    return lam_bar, b_bar, c


def s5_states(u, lam_bar, b_bar, h0, reverse):
    bu = jnp.einsum('gpc,btgc->btgp', b_bar, u.astype(jnp.complex64))
    if h0 is not None:
        first = -1 if reverse else 0
        bu = bu.at[:, first].add(lam_bar * h0)
    a = jnp.broadcast_to(lam_bar, bu.shape)
    _, h = lax.associative_scan(linear_recurrence, (a, bu), reverse=reverse, axis=1)
    return h


def s5_readout(c, h):
    B, T = h.shape[0], h.shape[1]
    return jnp.einsum('gcp,btgp->btgc', c, h).real.reshape(B, T, S5_WIDTH)


def s5_glu(y, w, b):
    g = jax.nn.gelu(y)
    return g * jax.nn.sigmoid(g @ w + b)


def s5_mixer(u_lat, u_ctx, p, need_ctx):
    B, T, _ = u_lat.shape
    C = u_ctx.shape[1]
    ul = u_lat.astype(jnp.float32).reshape(B, T, S5_GROUPS, S5_GROUP)
    uc = u_ctx.astype(jnp.float32).reshape(B, C, S5_GROUPS, S5_GROUP)
    d_skip = p['s5_d'].astype(jnp.float32)
    y_lat = d_skip * u_lat.astype(jnp.float32)
    y_ctx = d_skip * u_ctx.astype(jnp.float32) if need_ctx else None
    for direction in range(2):
        reverse = direction == 1
        lam_bar, b_bar, c = s5_discretise(
            p['s5_lambda_re'][direction], p['s5_lambda_im'][direction], p['s5_log_dt'][direction],
            p['s5_b_re'][direction], p['s5_b_im'][direction], p['s5_c_re'][direction], p['s5_c_im'][direction])
        hc = s5_states(uc, lam_bar, b_bar, None, reverse)
        h_final = hc[:, 0] if reverse else hc[:, -1]
        hl = s5_states(ul, lam_bar, b_bar, h_final, reverse)
        y_lat = y_lat + s5_readout(c, hl)
        if need_ctx:
            y_ctx = y_ctx + s5_readout(c, hc)
    out_lat = s5_glu(y_lat.astype(u_lat.dtype), p['s5_glu_w'], p['s5_glu_b'])
    out_ctx = s5_glu(y_ctx.astype(u_ctx.dtype), p['s5_glu_w'], p['s5_glu_b']) if need_ctx else None
    return out_lat, out_ctx


def merge_branches(ys, gates, w_branch, w_out):
    g = jax.nn.sigmoid(gates.reshape(gates.shape[:-1] + (N_BRANCH, gates.shape[-1] // N_BRANCH)))
    proj = jnp.einsum('btnw,nwd->btnd', jnp.stack(ys, axis=-2), w_branch)
    return jnp.sum(g * proj, axis=-2) @ w_out


def hybrid_token_mixer(n_lat, n_ctx, p, rows, cols, need_ctx):
    (na_q, na_k, na_v, sw_q, sw_k, sw_v, s5_u, mla_cq, mla_ckv, mla_kr, gates) = split_columns(n_lat @ p['w_in'])
    (na_qc, na_kc, na_vc, sw_qc, sw_kc, sw_vc, s5_uc, mla_cqc, mla_ckvc, mla_krc, gates_c) = split_columns(n_ctx @ p['w_in'])
    na_kc_h, na_vc_h = split_heads(na_kc, NA_HEADS), split_heads(na_vc, NA_HEADS)
    y_na = neighbourhood_attention(split_heads(na_q, NA_HEADS), split_heads(na_k, NA_HEADS),
                                   split_heads(na_v, NA_HEADS), na_kc_h, na_vc_h, p['na_rpb'])
    sw_kc_h, sw_vc_h = split_heads(sw_kc, SWA_KV_HEADS), split_heads(sw_vc, SWA_KV_HEADS)
    y_sw = sliding_window_attention(rope_2d(split_heads(sw_q, SWA_HEADS), rows, cols),
                                    rope_2d(split_heads(sw_k, SWA_KV_HEADS), rows, cols),
                                    split_heads(sw_v, SWA_KV_HEADS), sw_kc_h, sw_vc_h, p['swa_sink'])
    y_s5, y_s5_c = s5_mixer(s5_u, s5_uc, p, need_ctx)
    k_mc, v_mc = mla_keys_values(mla_ckvc, mla_krc, p, None, None)
    k_m, v_m = mla_keys_values(mla_ckv, mla_kr, p, rows, cols)
    y_mla = mla_dense_attention(mla_query(mla_cq, p, rows, cols), k_m, v_m, k_mc, v_mc)
    out_lat = merge_branches([y_na, y_sw, y_s5, y_mla], gates, p['w_branch'], p['w_out'])
    if not need_ctx:
        return out_lat, None
    y_na_c = ctx_attention(split_heads(na_qc, NA_HEADS), na_kc_h, na_vc_h)
    y_sw_c = ctx_attention(split_heads(sw_qc, SWA_HEADS), sw_kc_h, sw_vc_h, p['swa_sink'])
    y_mla_c = ctx_attention(mla_query(mla_cqc, p, None, None), k_mc, v_mc)
    out_ctx = merge_branches([y_na_c, y_sw_c, y_s5_c, y_mla_c], gates_c, p['w_branch'], p['w_out'])
    return out_lat, out_ctx


def setup_inputs(seed: int = 0) -> dict:
    key = jax.random.key(seed)
    ks = iter(jax.random.split(key, 48))

    def nrm(shape, scale=1.0):
        return jax.random.normal(next(ks), shape, jnp.float32) * scale

    def gain(shape):
        return 1.0 + nrm(shape, 0.01)

    D = D_MODEL
    G, P, Cg = S5_GROUPS, S5_STATE, S5_GROUP
    lam_im = jnp.pi * jnp.arange(P, dtype=jnp.float32) + nrm((DEPTH, 2, G, P), 0.01)
    log_dt = jax.random.uniform(next(ks), (DEPTH, 2, G), jnp.float32,
                                math.log(S5_DT_MIN), math.log(S5_DT_MAX))
    return {
        'x': nrm((BATCH, SEQ, D)),
        'c': nrm((BATCH, D)),
        'ctx': nrm((BATCH, CTX_LEN, D)),
        'c_ctx': nrm((D,)),
        'ada_w': nrm((DEPTH, D, N_MOD * D), 0.5 * D ** -0.5),
        'ada_b': nrm((DEPTH, N_MOD * D), 0.02),
        'ffn1_norm': gain((DEPTH, D)),
        'ffn1_w_gate': nrm((DEPTH, D, D_FF), D ** -0.5),
        'ffn1_w_up': nrm((DEPTH, D, D_FF), D ** -0.5),
        'ffn1_w_down': nrm((DEPTH, D_FF, D), D_FF ** -0.5),
        'mix_norm': gain((DEPTH, D)),
        'w_in': nrm((DEPTH, D, IN_COLS), D ** -0.5),
        'na_rpb': nrm((DEPTH, NA_HEADS, 2 * NA_WIN_ROWS - 1, 2 * NA_WIN_COLS - 1), 0.02),
        'swa_sink': nrm((DEPTH, SWA_HEADS), 0.5),
        's5_lambda_re': -0.5 + nrm((DEPTH, 2, G, P), 0.01),
        's5_lambda_im': lam_im,
        's5_log_dt': log_dt,
        's5_b_re': nrm((DEPTH, 2, G, P, Cg), (2 * Cg) ** -0.5),
        's5_b_im': nrm((DEPTH, 2, G, P, Cg), (2 * Cg) ** -0.5),
        's5_c_re': nrm((DEPTH, 2, G, Cg, P), P ** -0.5),
        's5_c_im': nrm((DEPTH, 2, G, Cg, P), P ** -0.5),
        's5_d': nrm((DEPTH, S5_WIDTH)),
        's5_glu_w': nrm((DEPTH, S5_WIDTH, S5_WIDTH), S5_WIDTH ** -0.5),
        's5_glu_b': nrm((DEPTH, S5_WIDTH), 0.02),
        'mla_q_norm': gain((DEPTH, MLA_Q_LORA)),
        'mla_w_uq': nrm((DEPTH, MLA_Q_LORA, MLA_HEADS * (MLA_NOPE + MLA_ROPE)), MLA_Q_LORA ** -0.5),
        'mla_kv_norm': gain((DEPTH, MLA_KV_LORA)),
        'mla_w_ukv': nrm((DEPTH, MLA_KV_LORA, MLA_HEADS * (MLA_NOPE + MLA_V)), MLA_KV_LORA ** -0.5),
        'w_branch': nrm((DEPTH, N_BRANCH, BRANCH_WIDTH, D), BRANCH_WIDTH ** -0.5),
        'w_out': nrm((DEPTH, D, D), D ** -0.5),
        'ffn2_norm': gain((DEPTH, D)),
        'ffn2_w_gate': nrm((DEPTH, D, D_FF), D ** -0.5),
        'ffn2_w_up': nrm((DEPTH, D, D_FF), D ** -0.5),
        'ffn2_w_down': nrm((DEPTH, D_FF, D), D_FF ** -0.5),
        'final_norm': gain((D,)),
    }


def reference(x, c, ctx, c_ctx, ada_w, ada_b, ffn1_norm, ffn1_w_gate, ffn1_w_up, ffn1_w_down,
              mix_norm, w_in, na_rpb, swa_sink, s5_lambda_re, s5_lambda_im, s5_log_dt,
              s5_b_re, s5_b_im, s5_c_re, s5_c_im, s5_d, s5_glu_w, s5_glu_b,
              mla_q_norm, mla_w_uq, mla_kv_norm, mla_w_ukv, w_branch, w_out,
              ffn2_norm, ffn2_w_gate, ffn2_w_up, ffn2_w_down, final_norm):
    B, T, D = x.shape
    pos = jnp.arange(T)
    rows = pos // GRID_W
    cols = pos % GRID_W
    h, hc = x, ctx
    silu_c, silu_cc = jax.nn.silu(c), jax.nn.silu(c_ctx)
    for l in range(DEPTH):
        need_ctx = l < DEPTH - 1
        mod = (silu_c @ ada_w[l] + ada_b[l]).reshape(B, N_MOD, 1, D)
        modc = (silu_cc @ ada_w[l] + ada_b[l]).reshape(N_MOD, D)
        h = macaron_half_ffn(h, mod[:, 0], mod[:, 1], mod[:, 2], ffn1_norm[l],
                             ffn1_w_gate[l], ffn1_w_up[l], ffn1_w_down[l])
        hc = macaron_half_ffn(hc, modc[0], modc[1], modc[2], ffn1_norm[l],
                              ffn1_w_gate[l], ffn1_w_up[l], ffn1_w_down[l])
        p = {'w_in': w_in[l], 'na_rpb': na_rpb[l], 'swa_sink': swa_sink[l],
             's5_lambda_re': s5_lambda_re[l], 's5_lambda_im': s5_lambda_im[l], 's5_log_dt': s5_log_dt[l],
             's5_b_re': s5_b_re[l], 's5_b_im': s5_b_im[l], 's5_c_re': s5_c_re[l], 's5_c_im': s5_c_im[l],
             's5_d': s5_d[l], 's5_glu_w': s5_glu_w[l], 's5_glu_b': s5_glu_b[l],
             'mla_q_norm': mla_q_norm[l], 'mla_w_uq': mla_w_uq[l],
             'mla_kv_norm': mla_kv_norm[l], 'mla_w_ukv': mla_w_ukv[l],
             'w_branch': w_branch[l], 'w_out': w_out[l]}
        n_lat = modulate(rms_norm(h, mix_norm[l]), mod[:, 3], mod[:, 4])
        n_ctx = modulate(rms_norm(hc, mix_norm[l]), modc[3], modc[4])
        y_lat, y_ctx = hybrid_token_mixer(n_lat, n_ctx, p, rows, cols, need_ctx)
        h = h + mod[:, 5] * y_lat
        h = macaron_half_ffn(h, mod[:, 6], mod[:, 7], mod[:, 8], ffn2_norm[l],
                             ffn2_w_gate[l], ffn2_w_up[l], ffn2_w_down[l])
        if need_ctx:
            hc = hc + modc[5] * y_ctx
            hc = macaron_half_ffn(hc, modc[6], modc[7], modc[8], ffn2_norm[l],
                                  ffn2_w_gate[l], ffn2_w_up[l], ffn2_w_down[l])
    return rms_norm(h, final_norm)
```

```python
import math
from contextlib import ExitStack
import numpy as np
import concourse.bass as bass
import concourse.mybir as mybir
from concourse.bass_utils import run_bass_kernel_spmd

F32 = mybir.dt.float32
BF16 = mybir.dt.bfloat16
AF = mybir.ActivationFunctionType
ALU = mybir.AluOpType

ENGS = ("pe", "act", "dve", "pool", "sp")
D = 1024
DFF = 2816
CTX = 256
NMOD = 9
INC = 7328
EPS = 1e-6


class Res:
    __slots__ = ("name", "writers", "readers")

    def __init__(self, name=""):
        self.name = name
        self.writers = {}
        self.readers = {}


class Prog:
    def __init__(self, nc, n_dma_sems=14):
        self.nc = nc
        self.q = {e: [] for e in ENGS}
        self.cnt = {e: 0 for e in ENGS}
        self.known = {}
        self.sems = {}
        self.n_dma_sems = n_dma_sems
        self.dma_n = {e: 0 for e in ENGS}
        self.dma_sems = {}
        self._ctx = []

    def open(self):
        nc = self.nc
        for e in ENGS:
            cm = nc.semaphore("s_" + e)
            self.sems[e] = cm.__enter__()
            self._ctx.append(cm)
        for e in ("sp", "pool", "act"):
            lst = []
            for i in range(self.n_dma_sems):
                cm = nc.semaphore("d_%s%d" % (e, i))
                lst.append(cm.__enter__())
                self._ctx.append(cm)
            self.dma_sems[e] = lst

    def close(self):
        for cm in reversed(self._ctx):
            cm.__exit__(None, None, None)
        self._ctx = []

    def _need(self, eng, deps):
        for key, (sem, val) in deps.items():
            k = (eng, key)
            if self.known.get(k, 0) >= val:
                continue
            self.known[k] = val
            self.q[eng].append(("wait", sem, val))

    def _collect(self, eng, reads, writes):
        deps = {}

        def add(d, same_ok):
            for key, (sem, val) in d.items():
                if key == eng and not same_ok:
                    continue
                if key not in deps or deps[key][1] < val:
                    deps[key] = (sem, val)
        for r in reads:
            add(r.writers, True)
        for w in writes:
            add(w.writers, False)
            add(w.readers, False)
        return deps

    def _commit(self, ev_key, ev, reads, writes):
        for w in writes:
            w.writers = {ev_key: ev}
            w.readers = {}
        for r in reads:
            if r in writes:
                continue
            old = r.readers.get(ev_key)
            if old is None or old[1] < ev[1]:
                r.readers[ev_key] = ev

    def op(self, eng, fn, reads=(), writes=(), chain=False):
        deps = self._collect(eng, reads, writes)
        if chain and eng in deps:
            del deps[eng]
        self._need(eng, deps)
        self.cnt[eng] += 1
        ev = (self.sems[eng], self.cnt[eng])
        self.q[eng].append(("op", fn, self.sems[eng]))
        self._commit(eng, ev, reads, writes)

    def dma(self, eng, out, in_, reads=(), writes=(), accum=False, **kw):
        if accum:
            deps = self._collect(eng, reads, ())
            for w in writes:
                for d_ in (w.writers, w.readers):
                    for key_, (sem_, val_) in d_.items():
                        if d_ is w.writers and key_.startswith("dma_"):
                            continue
                        if key_ not in deps or deps[key_][1] < val_:
                            deps[key_] = (sem_, val_)
        else:
            deps = self._collect(eng, reads, writes)
        n = self.dma_n[eng]
        self.dma_n[eng] += 1
        slot = n % self.n_dma_sems
        sem = self.dma_sems[eng][slot]
        rnd = n // self.n_dma_sems
        key = "dma_%s_%d" % (eng, slot)
        if rnd > 0:
            deps[key] = (sem, 16 * rnd)
        if eng in deps:
            del deps[eng]
        self._need(eng, deps)
        ev = (sem, 16 * (rnd + 1))
        self.q[eng].append(("dma", out, in_, sem, kw))
        if accum:
            for w in writes:
                w.writers[key] = ev
            self._commit(key, ev, reads, ())
        else:
            self._commit(key, ev, reads, writes)

    def barrier(self):
        deps = {}
        for e in ENGS:
            if self.cnt[e] > 0:
                deps[e] = (self.sems[e], self.cnt[e])
        for e in ("sp", "pool", "act"):
            n = self.dma_n[e]
            for slot in range(min(n, self.n_dma_sems)):
                last = ((n - 1 - slot) // self.n_dma_sems)
                deps["dma_%s_%d" % (e, slot)] = (self.dma_sems[e][slot], 16 * (last + 1))
        for e in ENGS:
            d = {k: v for k, v in deps.items() if k != e}
            self._need(e, d)

    def wait_all(self, eng, ress):
        deps = {}
        for r in ress:
            for d in (r.writers, r.readers):
                for key, (sem, val) in d.items():
                    if key == eng:
                        continue
                    if key not in deps or deps[key][1] < val:
                        deps[key] = (sem, val)
        self._need(eng, deps)

    def emit(self):
        nc = self.nc
        with nc.allow_non_contiguous_dma(reason="small strided vector loads"), nc.Block() as block:
            def make(ename):
                def body(e):
                    for it in self.q[ename]:
                        if it[0] == "wait":
                            e.wait_ge(it[1], it[2])
                        elif it[0] == "op":
                            it[1](e).then_inc(it[2], 1)
                        else:
                            _, out, in_, sem, kw = it
                            e.dma_start(out=out, in_=in_, **kw).then_inc(sem, 16)
                return body
            block.tensor(make("pe"))
            block.scalar(make("act"))
            block.vector(make("dve"))
            block.gpsimd(make("pool"))
            block.sync(make("sp"))


class Rot:
    def __init__(self, n, name=""):
        self.n = n
        self.i = 0
        self.res = [Res("%s%d" % (name, j)) for j in range(n)]

    def next(self):
        j = self.i % self.n
        self.i += 1
        return j, self.res[j]


class K:
    def __init__(self, TL, depth, debug=False, phases=None, nlayers=None):
        self.TL = TL
        self.T = TL + CTX
        self.depth = depth
        self.debug = debug
        self.phases = phases
        self.nlayers = depth if nlayers is None else nlayers
        self.nc = bass.Bass("TRN2", target_bir_lowering=False)
        self.P = Prog(self.nc)
        self.dr = {}
        self.rr = {}

    def din(self, name, shape, dt=F32):
        t = self.nc.dram_tensor(name, list(shape), dt, kind="ExternalInput").ap()
        self.dr[name] = t
        self.rr[name] = Res(name)
        return t

    def dscr(self, name, shape, dt, out=False):
        kind = "ExternalOutput" if (out or self.debug) else "Internal"
        t = self.nc.dram_tensor(name, list(shape), dt, kind=kind).ap()
        self.dr[name] = t
        self.rr[name] = Res(name)
        return t

    def chunks(self, with_ctx=True):
        out = [(c * 256, 256) for c in range(self.TL // 256)]
        if with_ctx:
            out.append((self.TL, CTX))
        return out

    def declare(self):
        L, T, TL = self.depth, self.T, self.TL
        d = self.din
        d("x", [TL, D]); d("c", [D]); d("ctx", [CTX, D]); d("c_ctx", [D])
        d("ada_w", [L, D, NMOD * D]); d("ada_b", [L, NMOD * D])
        for f in ("ffn1", "ffn2"):
            d(f + "_norm", [L, D]); d(f + "_w_gate", [L, D, DFF]); d(f + "_w_up", [L, D, DFF]); d(f + "_w_down", [L, DFF, D])
        d("mix_norm", [L, D]); d("w_in", [L, D, INC]); d("w_in_sw", [L, D, 672])
        d("na_rpb", [L, 8, 15, 31]); d("swa_sink", [L, 8])
        d("s5_lambda_re", [L, 2, 32, 64]); d("s5_lambda_im", [L, 2, 32, 64]); d("s5_log_dt", [L, 2, 32])
        d("s5_b_re", [L, 2, 32, 64, 16]); d("s5_b_im", [L, 2, 32, 64, 16])
        d("s5_c_re", [L, 2, 32, 16, 64]); d("s5_c_im", [L, 2, 32, 16, 64])
        d("s5_d", [L, 512]); d("s5_glu_w", [L, 512, 512]); d("s5_glu_b", [L, 512])
        d("mla_q_norm", [L, 256]); d("mla_w_uq", [L, 256, 768]); d("mla_w_uq_sw", [L, 256, 768])
        d("mla_kv_norm", [L, 128]); d("mla_w_ukv", [L, 128, 1024])
        d("w_branch", [L, 4, 512, D]); d("w_out", [L, D, D]); d("final_norm", [D])
        d("k_ident", [128, 128]); d("k_cos64", [128, T]); d("k_sin64", [128, T])
        d("k_cos64q", [128, T]); d("k_sin64q", [128, T])
        d("k_cos32q", [96, T]); d("k_sin32q", [96, T]); d("k_cos32k", [32, T]); d("k_sin32k", [32, T])
        d("k_mprev", [128, 512]); d("k_mnext", [128, 512])
        d("k_G", [31, 4096]); d("k_cmask", [15, 4096])
        s = self.dscr
        s("hT", [8, 128, T], F32)
        s("qna", [4, 128, T], BF16); s("kna", [4, 128, T], BF16); s("vna", [T, 512], BF16)
        s("qsw", [4, 128, T], BF16); s("ksw", [128, T], BF16); s("vsw", [T, 128], BF16)
        s("u16", [4, 128, T], BF16)
        s("qml", [8, 96, T], BF16); s("kml", [8, 96, T], BF16); s("vml", [T, 512], BF16)
        s("gat", [4, 8, 128, T], BF16)
        s("ymx", [4, 4, 128, T], BF16)
        s("ys5", [4, 128, T], F32)
        s("ebd", [8, 15, 64, 64], F32)
        s("out", [TL, D], F32, out=True)
        if self.debug:
            s("dbg_mod", [self.depth, 128, NMOD * 16], F32)

    def sb(self, es, name, shape, dt):
        self._uid = getattr(self, "_uid", 0) + 1
        return es.enter_context(self.nc.sbuf_tensor("%s_%d" % (name, self._uid), list(shape), dt))

    def ps(self, es, name, shape, dt=F32):
        self._uid = getattr(self, "_uid", 0) + 1
        return es.enter_context(self.nc.psum_tensor("%s_%d" % (name, self._uid), list(shape), dt))

    def phase_init(self):
        P, nc, T, TL = self.P, self.nc, self.T, self.TL
        with ExitStack() as es:
            ident = self.sb(es, "i_ident", [128, 128], F32)
            xin = [self.sb(es, "i_x%d" % i, [128, D], F32) for i in range(2)]
            xo = [self.sb(es, "i_o%d" % i, [128, 8, 128], F32) for i in range(2)]
            pst = [self.ps(es, "i_ps%d" % i, [128, 8, 128]) for i in range(2)]
            r_id = Res()
            r_in = Rot(2); r_o = Rot(2); r_ps = Rot(2)
            P.dma("sp", ident[:], self.dr["k_ident"][:, :], writes=[r_id])
            for tt in range(T // 128):
                t0 = tt * 128
                src = self.dr["x"][t0:t0 + 128, :] if t0 < TL else self.dr["ctx"][t0 - TL:t0 - TL + 128, :]
                i, ri = r_in.next(); j, ro = r_o.next(); p, rp = r_ps.next()
                P.dma("sp", xin[i][:], src, writes=[ri])
                for k in range(8):
                    P.op("pe", lambda e, i=i, p=p, k=k: e.transpose(out=pst[p][:, k, :], in_=xin[i][:, k * 128:(k + 1) * 128], identity=ident[:]),
                         reads=[ri, r_id], writes=[rp], chain=True)
                P.op("act", lambda e, j=j, p=p: e.activation(out=xo[j][:, 0:4, :], in_=pst[p][:, 0:4, :], func=AF.Copy), reads=[rp], writes=[ro])
                P.op("dve", lambda e, j=j, p=p: e.tensor_copy(out=xo[j][:, 4:8, :], in_=pst[p][:, 4:8, :]), reads=[rp], writes=[ro])
                P.dma("sp", self.dr["hT"][:, :, t0:t0 + 128].rearrange("k p t -> p k t"), xo[j][:], reads=[ro], writes=[self.rr["hT"]])

    def phase_mod(self, l):
        P, nc = self.P, self.nc
        mod, rmod = self.mod, self.rmod
        with ExitStack() as es:
            sc = self.sb(es, "m_sc", [128, 8, 2], F32)
            wt = [self.sb(es, "m_w%d" % i, [128, NMOD * D], F32) for i in range(2)]
            bt = self.sb(es, "m_b", [128, 72], F32)
            pm = self.ps(es, "m_ps", [128, 72, 2])
            rsc, rb, rpm = Res(), Res(), Res()
            rw = Rot(2)
            c0, rc0 = self.load_vec(es, "m_c0", self.dr["c"], 8)
            c1, rc1 = self.load_vec(es, "m_c1", self.dr["c_ctx"], 8)
            P.op("dve", lambda e: e.tensor_copy(out=sc[:, :, 0], in_=c0[:]), reads=[rc0], writes=[rsc])
            P.op("dve", lambda e: e.tensor_copy(out=sc[:, :, 1], in_=c1[:]), reads=[rc1], writes=[rsc])
            P.dma("sp", bt[:], self.dr["ada_b"][l].rearrange("(j p) -> p j", p=128), writes=[rb])
            P.op("act", lambda e: e.activation(out=sc[:], in_=sc[:], func=AF.Silu), reads=[rsc], writes=[rsc])
            wi = []
            for k in range(8):
                i, r = rw.next()
                P.dma("sp", wt[i][:], self.dr["ada_w"][l, k * 128:(k + 1) * 128, :], writes=[r])
                wi.append((i, r))
                for j in range(72):
                    P.op("pe", lambda e, i=i, j=j, k=k: e.matmul(pm[:, j, :], lhsT=wt[i][:, j * 128:(j + 1) * 128], rhs=sc[:, k, :], start=True, stop=True),
                         reads=[r, rsc], writes=[rpm], chain=True)
                if k == 0:
                    P.op("dve", lambda e: e.tensor_copy(out=self.macc[:], in_=pm[:]), reads=[rpm], writes=[self.rmacc])
                else:
                    P.op("dve", lambda e: e.tensor_tensor(out=self.macc[:], in0=pm[:], in1=self.macc[:], op=ALU.add), reads=[rpm, self.rmacc], writes=[self.rmacc])
            for s in range(2):
                P.op("dve", lambda e, s=s: e.tensor_tensor(out=mod[:, :, :, s], in0=self.macc[:, :, s].rearrange("p (m k) -> p m k", k=8),
                                                           in1=bt[:].rearrange("p (m k) -> p m k", k=8), op=ALU.add),
                     reads=[self.rmacc, rb], writes=[rmod])

    def load_vec(self, es, name, src_ap, nk):
        t = self.sb(es, name, [128, nk], F32)
        r = Res(name)
        self.P.dma("sp", t[:], src_ap.rearrange("(k p) -> p k", p=128), writes=[r])
        return t, r

    def make_AS(self, es, pref, normw, rn, mi_shift, mi_scale):
        P = self.P
        A = self.sb(es, pref + "_A", [128, 8, 2], F32)
        rA = Res()
        for s in range(2):
            P.op("dve", lambda e, s=s: e.scalar_tensor_tensor(out=A[:, :, s], in0=self.mod[:, mi_scale, :, s], scalar=1.0, in1=normw[:], op0=ALU.add, op1=ALU.mult),
                 reads=[self.rmod, rn], writes=[rA])
        return A, rA

    def norm_chunk(self, es_tiles, hc, rh, n, s, A, rA, mi_shift, nT, rnT):
        P = self.P
        sq, rsq, pss, rpss, rstd, rrstd, ones, rones, tmp, rtmp = es_tiles
        P.op("act", lambda e: e.activation(out=sq[:, :, :n], in_=hc[:, :, :n], func=AF.Square), reads=[rh], writes=[rsq])
        for k in range(8):
            P.op("pe", lambda e, k=k: e.matmul(pss[:, :n], lhsT=ones[:], rhs=sq[:, k, :n], start=(k == 0), stop=(k == 7)),
                 reads=[rsq, rones], writes=[rpss], chain=True)
        P.op("act", lambda e: e.activation(out=rstd[:, :n], in_=pss[:, :n], func=AF.Sqrt, bias=self.epsb[:, 0:1], scale=1.0 / D), reads=[rpss, self.reps], writes=[rrstd])
        P.op("dve", lambda e: e.reciprocal(out=rstd[:, :n], in_=rstd[:, :n]), reads=[rrstd], writes=[rrstd])
        for k in range(8):
            j, rt = rtmp.next()
            P.op("dve", lambda e, k=k, j=j: e.scalar_tensor_tensor(out=tmp[j][:, :n], in0=hc[:, k, :n], scalar=A[:, k, s:s + 1], in1=rstd[:, :n], op0=ALU.mult, op1=ALU.mult),
                 reads=[rh, rA, rrstd], writes=[rt])
            P.op("act", lambda e, k=k, j=j: e.activation(out=nT[:, k, :n], in_=tmp[j][:, :n], func=AF.Identity, bias=self.mod[:, mi_shift, k, s:s + 1], scale=1.0),
                 reads=[rt, self.rmod], writes=[rnT])

    def norm_tiles(self, es, pref):
        sq = self.sb(es, pref + "_sq", [128, 8, 512], BF16)
        pss = self.ps(es, pref + "_pss", [128, 512])
        rstd = self.sb(es, pref + "_rstd", [128, 512], F32)
        tmp = [self.sb(es, pref + "_tmp%d" % i, [128, 512], F32) for i in range(2)]
        return (sq, Res(), pss, Res(), rstd, Res(), self.ones, self.rones, tmp, Rot(2))

    def phase_ffn(self, l, which, mi0, with_ctx):
        P, nc = self.P, self.nc
        with ExitStack() as es:
            wg = self.sb(es, "f_wg", [128, 8, DFF], BF16)
            wu = self.sb(es, "f_wu", [128, 8, DFF], BF16)
            wd = self.sb(es, "f_wd", [128, 22, D], BF16)
            rwg, rwu, rwd = Res(), Res(), Res()
            for k in range(8):
                P.dma("pool", wg[:, k, :], self.dr[which + "_w_gate"][l, k * 128:(k + 1) * 128, :], writes=[rwg], accum=True)
                P.dma("pool", wu[:, k, :], self.dr[which + "_w_up"][l, k * 128:(k + 1) * 128, :], writes=[rwu], accum=True)
            for k in range(22):
                P.dma("pool", wd[:, k, :], self.dr[which + "_w_down"][l, k * 128:(k + 1) * 128, :], writes=[rwd], accum=True)
            normw, rn = self.load_vec(es, "f_nw", self.dr[which + "_norm"][l], 8)
            A, rA = self.make_AS(es, "f", normw, rn, mi0, mi0 + 1)
            G = self.sb(es, "f_G", [128, 8, 2], F32)
            rG = Res()
            P.op("dve", lambda e: e.tensor_scalar(out=G[:], in0=self.mod[:, mi0 + 2, :, :], scalar1=0.5, scalar2=None, op0=ALU.mult), reads=[self.rmod], writes=[rG])
            hc = self.sb(es, "f_h", [128, 8, 512], F32)
            rh = Res()
            nT = self.sb(es, "f_nT", [128, 8, 512], BF16)
            rnT = Res()
            hid = self.sb(es, "f_hid", [128, 22, 512], BF16)
            rhid = Res()
            sg = [self.sb(es, "f_sg%d" % i, [128, 512], F32) for i in range(2)]
            rsg = Rot(2)
            nt = self.norm_tiles(es, "f")
            pg = [self.ps(es, "f_pg%d" % i, [128, 512]) for i in range(2)]
            pu = [self.ps(es, "f_pu%d" % i, [128, 512]) for i in range(2)]
            pd = [self.ps(es, "f_pd%d" % i, [128, 512]) for i in range(2)]
            rpg, rpu, rpd = Rot(2), Rot(2), Rot(2)
            hT = self.dr["hT"]; rhT = self.rr["hT"]
            for (t0, n) in self.chunks(with_ctx):
                s = 0 if t0 < self.TL else 1
                P.dma("sp", hc[:, :, :n], hT[:, :, t0:t0 + n].rearrange("k p t -> p k t"), reads=[rhT], writes=[rh])
                self.norm_chunk(nt, hc, rh, n, s, A, rA, mi0, nT, rnT)
                for m in range(22):
                    ig, rg_ = rpg.next(); iu, ru_ = rpu.next(); isg, rs_ = rsg.next()
                    for k in range(8):
                        P.op("pe", lambda e, ig=ig, m=m, k=k: e.matmul(pg[ig][:, :n], lhsT=wg[:, k, m * 128:(m + 1) * 128], rhs=nT[:, k, :n], start=(k == 0), stop=(k == 7)),
                             reads=[rwg, rnT], writes=[rg_], chain=True)
                    for k in range(8):
                        P.op("pe", lambda e, iu=iu, m=m, k=k: e.matmul(pu[iu][:, :n], lhsT=wu[:, k, m * 128:(m + 1) * 128], rhs=nT[:, k, :n], start=(k == 0), stop=(k == 7)),
                             reads=[rwu, rnT], writes=[ru_], chain=True)
                    P.op("act", lambda e, ig=ig, isg=isg: e.activation(out=sg[isg][:, :n], in_=pg[ig][:, :n], func=AF.Silu), reads=[rg_], writes=[rs_])
                    P.op("dve", lambda e, iu=iu, isg=isg, m=m: e.tensor_tensor(out=hid[:, m, :n], in0=pu[iu][:, :n], in1=sg[isg][:, :n], op=ALU.mult),
                         reads=[ru_, rs_], writes=[rhid])
                for mo in range(8):
                    ip, rp_ = rpd.next()
                    for k in range(22):
                        P.op("pe", lambda e, ip=ip, mo=mo, k=k: e.matmul(pd[ip][:, :n], lhsT=wd[:, k, mo * 128:(mo + 1) * 128], rhs=hid[:, k, :n], start=(k == 0), stop=(k == 21)),
                             reads=[rwd, rhid], writes=[rp_], chain=True)
                    P.op("dve", lambda e, ip=ip, mo=mo, s=s, n=n: e.scalar_tensor_tensor(out=hc[:, mo, :n], in0=pd[ip][:, :n], scalar=G[:, mo, s:s + 1], in1=hc[:, mo, :n], op0=ALU.mult, op1=ALU.add),
                         reads=[rp_, rG, rh], writes=[rh])
                P.dma("sp", hT[:, :, t0:t0 + n].rearrange("k p t -> p k t"), hc[:, :, :n], reads=[rh], writes=[rhT])

    def phase_final(self):
        P, nc, TL = self.P, self.nc, self.TL
        with ExitStack() as es:
            fw, rfw = self.load_vec(es, "z_fw", self.dr["final_norm"], 8)
            ident = self.sb(es, "z_ident", [128, 128], F32)
            rid = Res()
            P.dma("sp", ident[:], self.dr["k_ident"][:, :], writes=[rid])
            hc = self.sb(es, "z_h", [128, 8, 512], F32)
            rh = Res()
            sq = self.sb(es, "z_sq", [128, 8, 512], BF16); rsq = Res()
            rstd = self.sb(es, "z_rstd", [128, 512], F32); rrs = Res()
            y = self.sb(es, "z_y", [128, 8, 512], F32); ry = Res()
            pss = self.ps(es, "z_pss", [128, 512]); rpss = Res()
            pt = [self.ps(es, "z_pt%d" % i, [128, 8, 128]) for i in range(2)]; rpt = Rot(2)
            o = [self.sb(es, "z_o%d" % i, [128, D], F32) for i in range(2)]; ro = Rot(2)
            hT = self.dr["hT"]; rhT = self.rr["hT"]
            for (t0, n) in self.chunks(False):
                P.dma("sp", hc[:, :, :n], hT[:, :, t0:t0 + n].rearrange("k p t -> p k t"), reads=[rhT], writes=[rh])
                P.op("act", lambda e: e.activation(out=sq[:, :, :n], in_=hc[:, :, :n], func=AF.Square), reads=[rh], writes=[rsq])
                for k in range(8):
                    P.op("pe", lambda e, k=k: e.matmul(pss[:, :n], lhsT=self.ones[:], rhs=sq[:, k, :n], start=(k == 0), stop=(k == 7)), reads=[rsq, self.rones], writes=[rpss], chain=True)
                P.op("act", lambda e: e.activation(out=rstd[:, :n], in_=pss[:, :n], func=AF.Sqrt, bias=self.epsb[:, 0:1], scale=1.0 / D), reads=[rpss, self.reps], writes=[rrs])
                P.op("dve", lambda e: e.reciprocal(out=rstd[:, :n], in_=rstd[:, :n]), reads=[rrs], writes=[rrs])
                for k in range(8):
                    P.op("dve", lambda e, k=k: e.scalar_tensor_tensor(out=y[:, k, :n], in0=hc[:, k, :n], scalar=fw[:, k:k + 1], in1=rstd[:, :n], op0=ALU.mult, op1=ALU.mult),
                         reads=[rh, rfw, rrs], writes=[ry])
                for tt in range(n // 128):
                    ip, rp_ = rpt.next(); io, ro_ = ro.next()
                    for k in range(8):
                        P.op("pe", lambda e, ip=ip, k=k, tt=tt: e.transpose(out=pt[ip][:, k, :], in_=y[:, k, tt * 128:(tt + 1) * 128], identity=ident[:]),
                             reads=[ry, rid], writes=[rp_], chain=True)
                    P.op("act", lambda e, ip=ip, io=io: e.activation(out=o[io][:, 0:512], in_=pt[ip][:, 0:4, :].rearrange("p k t -> p (k t)"), func=AF.Copy), reads=[rp_], writes=[ro_])
                    P.op("dve", lambda e, ip=ip, io=io: e.tensor_copy(out=o[io][:, 512:1024], in_=pt[ip][:, 4:8, :].rearrange("p k t -> p (k t)")), reads=[rp_], writes=[ro_])
                    P.dma("sp", self.dr["out"][t0 + tt * 128:t0 + (tt + 1) * 128, :], o[io][:], reads=[ro_], writes=[self.rr["out"]])

    def build(self):
        nc, P = self.nc, self.P
        self.declare()
        with ExitStack() as es:
            P.open()
            self.mod = self.sb(es, "g_mod", [128, NMOD, 8, 2], F32); self.rmod = Res()
            self.macc = self.sb(es, "g_macc", [128, 72, 2], F32); self.rmacc = Res()
            self.ones = self.sb(es, "g_ones", [128, 128], BF16); self.rones = Res()
            self.epsb = self.sb(es, "g_eps", [128, 1], F32); self.reps = Res()
            P.op("dve", lambda e: e.memset(self.ones[:], 1.0), writes=[self.rones])
            P.op("dve", lambda e: e.memset(self.epsb[:], EPS), writes=[self.reps])
            self.hpi = self.sb(es, "g_hpi", [128, 1], F32); self.rhpi = Res()
            P.op("dve", lambda e: e.memset(self.hpi[:], math.pi / 2), writes=[self.rhpi])
            ph = self.phases

            def on(name):
                return ph is None or name in ph
            if on("init"):
                self.phase_init()
                P.barrier()
            for l in range(self.nlayers):
                last = (l == self.depth - 1)
                if on("mod"):
                    self.phase_mod(l)
                    if self.debug:
                        P.dma("sp", self.dr["dbg_mod"][l], self.mod[:].rearrange("p a b c -> p (a b c)"), reads=[self.rmod], writes=[self.rr["dbg_mod"]])
                    P.barrier()
                if on("ffn1"):
                    self.phase_ffn(l, "ffn1", 0, True)
                    P.barrier()
                if on("mix"):
                    self.phase_mix(l, not last)
                    P.barrier()
                if on("ffn2"):
                    self.phase_ffn(l, "ffn2", 6, not last)
                    P.barrier()
            if on("final"):
                self.phase_final()
            P.wait_all("sp", list(self.rr.values()))
            P.emit()
            P.close()
        return nc

    def phase_mix(self, l, need_ctx):
        ph = self.phases

        def on(name):
            return ph is None or name in ph
        if on("proj"):
            self.phase_proj(l); self.P.barrier()
        if on("mla"):
            self.phase_mla(l, need_ctx); self.P.barrier()
        if on("swa"):
            self.phase_swa(l, need_ctx); self.P.barrier()
        if on("na"):
            self.phase_na(l, need_ctx); self.P.barrier()
        if on("s5"):
            self.phase_s5(l, need_ctx); self.P.barrier()
        if on("merge"):
            self.phase_merge(l, need_ctx)


def host_consts(TL):
    T = TL + CTX
    pos = np.arange(TL)
    rows = (pos // 64).astype(np.float32)
    cols = (pos % 64).astype(np.float32)

    def tab(dim):
        half = dim // 2
        nf = half // 2
        inv = (10000.0 ** (-np.arange(nf, dtype=np.float32) / nf)).astype(np.float32)
        cos = np.ones((dim, T), np.float32)
        sin = np.zeros((dim, T), np.float32)
        for dd in range(dim):
            p = rows if dd < half else cols
            j = dd % nf
            ang = (p * inv[j]).astype(np.float32)
            sign = -1.0 if (dd % half) < nf else 1.0
            cos[dd, :TL] = np.cos(ang)
            sin[dd, :TL] = sign * np.sin(ang)
        return cos, sin
    c64, s64 = tab(64)
    c32, s32 = tab(32)
    k = {}
    k["k_ident"] = np.eye(128, dtype=np.float32)
    k["k_cos64"] = np.concatenate([c64, c64], 0)
    k["k_sin64"] = np.concatenate([s64, s64], 0)
    k["k_cos64q"] = (k["k_cos64"] * np.float32(0.125)).astype(np.float32)
    k["k_sin64q"] = (k["k_sin64"] * np.float32(0.125)).astype(np.float32)
    sq = np.float32(96.0 ** -0.5)
    k["k_cos32q"] = np.concatenate([np.ones((64, T), np.float32), c32 * sq], 0).astype(np.float32)
    k["k_sin32q"] = np.concatenate([np.zeros((64, T), np.float32), s32 * sq], 0).astype(np.float32)
    k["k_cos32k"] = c32
    k["k_sin32k"] = s32
    jl = np.arange(128)[:, None]
    il = np.arange(128)[None, :]
    k["k_mprev"] = np.tile((jl >= il).astype(np.float32), (1, 4))
    k["k_mnext"] = np.tile((jl <= il).astype(np.float32), (1, 4))
    kc = np.arange(64)[:, None]
    qc = np.arange(64)[None, :]
    c0 = np.clip(qc - 8, 0, 48)
    win = ((kc >= c0) & (kc < c0 + 16))
    G = np.zeros((31, 64, 64), np.float32)
    for dc in range(31):
        G[dc] = ((kc - qc + 15) == dc) & win
    k["k_G"] = G.reshape(31, 4096)
    k["k_cmask"] = np.tile(win.astype(np.float32).reshape(1, 4096), (15, 1))
    return k


def perm_swap(n_heads, dim):
    half = dim // 2
    nf = half // 2
    idx = np.arange(n_heads * dim)
    out = idx.copy()
    for h in range(n_heads):
        for dd in range(dim):
            partner = dd + nf if (dd % half) < nf else dd - nf
            out[h * dim + dd] = h * dim + partner
    return out


def host_layout(inputs, b, TL):
    m = {}
    m["x"] = np.ascontiguousarray(inputs["x"][b, :TL])
    m["c"] = np.ascontiguousarray(inputs["c"][b])
    m["ctx"] = np.ascontiguousarray(inputs["ctx"][b])
    for n in ("c_ctx", "ada_w", "ada_b", "ffn1_norm", "ffn1_w_gate", "ffn1_w_up", "ffn1_w_down", "mix_norm", "w_in",
              "na_rpb", "swa_sink", "s5_lambda_re", "s5_lambda_im", "s5_log_dt", "s5_b_re", "s5_b_im", "s5_c_re", "s5_c_im",
              "s5_d", "s5_glu_w", "s5_glu_b", "mla_q_norm", "mla_w_uq", "mla_kv_norm", "mla_w_ukv", "w_branch", "w_out",
              "ffn2_norm", "ffn2_w_gate", "ffn2_w_up", "ffn2_w_down", "final_norm"):
        m[n] = np.ascontiguousarray(inputs[n])
    w_in = inputs["w_in"]
    p64q = perm_swap(8, 64)
    p64k = perm_swap(2, 64)
    p32 = perm_swap(1, 32)
    m["w_in_sw"] = np.ascontiguousarray(np.concatenate([w_in[:, :, 1536:2048][:, :, p64q], w_in[:, :, 2048:2176][:, :, p64k],
                                                        w_in[:, :, 3200:3232][:, :, p32]], axis=2))
    wuq = inputs["mla_w_uq"]
    pq = np.arange(768)
    for h in range(8):
        pq[h * 96 + 64:h * 96 + 96] = h * 96 + 64 + p32
    m["mla_w_uq_sw"] = np.ascontiguousarray(wuq[:, :, pq])
    return m


_CACHE = {}


def kernel(**inputs):
    TL = inputs["x"].shape[1]
    depth = inputs["ada_w"].shape[0]
    B = inputs["x"].shape[0]
    inputs = {k: np.asarray(v) for k, v in inputs.items()}
    kb = K(TL, depth)
    nc = kb.build()
    consts = host_consts(TL)
    in_maps = []
    for core in range(8):
        m = host_layout(inputs, core % B, TL)
        m.update(consts)
        in_maps.append(m)
    res = run_bass_kernel_spmd(nc, in_maps, core_ids=list(range(8)))
    out = np.stack([res.results[b]["out"] for b in range(B)], axis=0)
    return out.astype(np.float32)


L = 256
CH = 256


def phase_s5(self, l, need_ctx):
    P, T, TL = self.P, self.T, self.TL
    dr, rr = self.dr, self.rr
    NCH = TL // L
    with ExitStack() as es:
        ident = self.sb(es, "s_id", [128, 128], F32); rid = Res()
        P.dma("sp", ident[:], dr["k_ident"][:, :], writes=[rid])
        dvec, rdv = self.load_vec(es, "s_d", dr["s5_d"][l], 4)
        uT = self.sb(es, "s_u", [128, T], BF16); ru = Res()
        names = ["lr", "li", "ldt", "dt", "a", "th", "r", "c", "s", "cc", "ss", "cs", "nr", "ni", "den", "kr", "ki", "t0", "t1"]
        pr = {nm: self.sb(es, "s_p_" + nm, [128, 4], F32) for nm in names}
        rp = Res()
        wc = self.sb(es, "s_wc", [128, 9, 4], F32); ws = self.sb(es, "s_ws", [128, 9, 4], F32)
        BDr = self.sb(es, "s_BDr", [128, 4, 128], F32); BDi = self.sb(es, "s_BDi", [128, 4, 128], F32); rBD = Res()
        CDr = self.sb(es, "s_CDr", [128, 4, 128], F32); CDi = self.sb(es, "s_CDi", [128, 4, 128], F32); rCD = Res()
        BBr = self.sb(es, "s_BBr", [128, 4, 128], F32); BBi = self.sb(es, "s_BBi", [128, 4, 128], F32); rBB = Res()
        tB = self.sb(es, "s_tB", [128, 128], F32); rtB = Res()
        WBr = self.sb(es, "s_WBr", [128, 4, 128], BF16); WBi = self.sb(es, "s_WBi", [128, 4, 128], BF16)
        WCr = self.sb(es, "s_WCr", [128, 4, 128], BF16); WCi = self.sb(es, "s_WCi", [128, 4, 128], BF16); rWt = Res()
        cE = self.sb(es, "s_cE", [128, 4, L], F32); sE = self.sb(es, "s_sE", [128, 4, L], F32); rE = Res()
        tE = self.sb(es, "s_tE", [128, L], F32); rtE = Res()
        init_re = self.sb(es, "s_ire", [128, 4], F32); init_im = self.sb(es, "s_iim", [128, 4], F32); rinit = Res()
        tI = self.sb(es, "s_tI", [128, 2], F32); rtI = Res()
        tt = [[self.sb(es, "s_t%d_%d" % (j, i), [128, L], F32) for i in range(2)] for j in range(4)]
        rtt = [Rot(2) for j in range(4)]
        vv = [[self.sb(es, "s_v%d_%d" % (j, i), [128, L], F32) for i in range(2)] for j in range(2)]
        rvv = [Rot(2) for j in range(2)]
        gg = [[self.sb(es, "s_g%d_%d" % (j, i), [128, L], F32) for i in range(2)] for j in range(2)]
        rgg = [Rot(2) for j in range(2)]
        mm_ = [[self.sb(es, "s_m%d_%d" % (j, i), [128, L], F32) for i in range(2)] for j in range(4)]
        rmm = [Rot(2) for j in range(4)]
        hh = [[self.sb(es, "s_h%d_%d" % (j, i), [128, L], BF16) for i in range(2)] for j in range(2)]
        rhh = [Rot(2) for j in range(2)]
        ych = [self.sb(es, "s_y%d" % i, [128, L], F32) for i in range(2)]; rych = Rot(2)
        yprev = [self.sb(es, "s_yp%d" % i, [128, L], F32) for i in range(2)]; ryp = Rot(2)
        pbu = [self.ps(es, "s_pbu%d" % i, [128, 2, L]) for i in range(2)]; rpbu = Rot(2)
        py = [self.ps(es, "s_py%d" % i, [128, 512]) for i in range(2)]; rpy = Rot(2)
        ptr = [self.ps(es, "s_ptr%d" % i, [128, 512]) for i in range(2)]; rptr = Rot(2)

        P.op("pool", lambda e: e.memset(BDr[:], 0.0), writes=[rBD]); P.op("pool", lambda e: e.memset(BDi[:], 0.0), writes=[rBD])
        P.op("pool", lambda e: e.memset(CDr[:], 0.0), writes=[rCD]); P.op("pool", lambda e: e.memset(CDi[:], 0.0), writes=[rCD])

        def tt_op(eng, out, a, b, op, reads, writes):
            P.op(eng, lambda e: e.tensor_tensor(out=out, in0=a, in1=b, op=op), reads=reads, writes=writes)

        for d in range(2):
            for fc in range(4):
                for ti in range(4):
                    for gl in range(2):
                        g = (fc * 4 + ti) * 2 + gl
                        ps_ = slice(gl * 64, gl * 64 + 64)
                        P.dma("sp", pr["lr"][ps_, ti:ti + 1], dr["s5_lambda_re"][l, d, g, :].rearrange("(p o) -> p o", o=1), writes=[rp], accum=True)
                        P.dma("sp", pr["li"][ps_, ti:ti + 1], dr["s5_lambda_im"][l, d, g, :].rearrange("(p o) -> p o", o=1), writes=[rp], accum=True)
                        P.dma("sp", pr["ldt"][ps_, ti:ti + 1], dr["s5_log_dt"][l, d, g:g + 1].partition_broadcast(64), writes=[rp], accum=True)
                        fo = (ti * 2 + gl) * 16
                        P.dma("sp", BDr[ps_, ti, fo:fo + 16], dr["s5_b_re"][l, d, g], writes=[rBD], accum=True)
                        P.dma("sp", BDi[ps_, ti, fo:fo + 16], dr["s5_b_im"][l, d, g], writes=[rBD], accum=True)
                        P.dma("sp", CDr[fo:fo + 16, ti, ps_], dr["s5_c_re"][l, d, g], writes=[rCD], accum=True)
                        P.dma("sp", CDi[fo:fo + 16, ti, ps_], dr["s5_c_im"][l, d, g], writes=[rCD], accum=True)
                R_ = [rp]
                p = pr
                P.op("act", lambda e: e.activation(out=p["dt"][:], in_=p["ldt"][:], func=AF.Exp), reads=R_, writes=R_)
                tt_op("dve", p["a"][:], p["lr"][:], p["dt"][:], ALU.mult, R_, R_)
                tt_op("dve", p["th"][:], p["li"][:], p["dt"][:], ALU.mult, R_, R_)
                P.op("act", lambda e: e.activation(out=p["r"][:], in_=p["a"][:], func=AF.Exp), reads=R_, writes=R_)
                P.op("act", lambda e: e.activation(out=p["s"][:], in_=p["th"][:], func=AF.Sin, scale=1.0 / 32), reads=R_, writes=R_)
                P.op("act", lambda e: e.activation(out=p["c"][:], in_=p["th"][:], func=AF.Sin, scale=1.0 / 32, bias=self.hpi[:, 0:1]), reads=R_ + [self.rhpi], writes=R_)
                for _ in range(5):
                    tt_op("dve", p["cc"][:], p["c"][:], p["c"][:], ALU.mult, R_, R_)
                    tt_op("dve", p["ss"][:], p["s"][:], p["s"][:], ALU.mult, R_, R_)
                    tt_op("dve", p["cs"][:], p["c"][:], p["s"][:], ALU.mult, R_, R_)
                    tt_op("dve", p["c"][:], p["cc"][:], p["ss"][:], ALU.subtract, R_, R_)
                    P.op("dve", lambda e: e.tensor_scalar(out=p["s"][:], in0=p["cs"][:], scalar1=2.0, scalar2=None, op0=ALU.mult), reads=R_, writes=R_)
                P.op("dve", lambda e: e.tensor_copy(out=wc[:, 0, :], in_=p["c"][:]), reads=R_, writes=R_)
                P.op("dve", lambda e: e.tensor_copy(out=ws[:, 0, :], in_=p["s"][:]), reads=R_, writes=R_)
                for k in range(8):
                    tt_op("dve", p["cc"][:], wc[:, k, :], wc[:, k, :], ALU.mult, R_, R_)
                    tt_op("dve", p["ss"][:], ws[:, k, :], ws[:, k, :], ALU.mult, R_, R_)
                    tt_op("dve", p["cs"][:], wc[:, k, :], ws[:, k, :], ALU.mult, R_, R_)
                    tt_op("dve", wc[:, k + 1, :], p["cc"][:], p["ss"][:], ALU.subtract, R_, R_)
                    P.op("dve", lambda e, k=k: e.tensor_scalar(out=ws[:, k + 1, :], in0=p["cs"][:], scalar1=2.0, scalar2=None, op0=ALU.mult), reads=R_, writes=R_)
                tt_op("dve", p["nr"][:], p["r"][:], p["c"][:], ALU.mult, R_, R_)
                P.op("dve", lambda e: e.tensor_scalar(out=p["nr"][:], in0=p["nr"][:], scalar1=-1.0, scalar2=None, op0=ALU.add), reads=R_, writes=R_)
                tt_op("dve", p["ni"][:], p["r"][:], p["s"][:], ALU.mult, R_, R_)
                tt_op("dve", p["cc"][:], p["lr"][:], p["lr"][:], ALU.mult, R_, R_)
                tt_op("dve", p["ss"][:], p["li"][:], p["li"][:], ALU.mult, R_, R_)
                tt_op("dve", p["den"][:], p["cc"][:], p["ss"][:], ALU.add, R_, R_)
                P.op("dve", lambda e: e.reciprocal(out=p["den"][:], in_=p["den"][:]), reads=R_, writes=R_)
                tt_op("dve", p["t0"][:], p["nr"][:], p["lr"][:], ALU.mult, R_, R_)
                tt_op("dve", p["t1"][:], p["ni"][:], p["li"][:], ALU.mult, R_, R_)
                tt_op("dve", p["kr"][:], p["t0"][:], p["t1"][:], ALU.add, R_, R_)
                tt_op("dve", p["kr"][:], p["kr"][:], p["den"][:], ALU.mult, R_, R_)
                tt_op("dve", p["t0"][:], p["ni"][:], p["lr"][:], ALU.mult, R_, R_)
                tt_op("dve", p["t1"][:], p["nr"][:], p["li"][:], ALU.mult, R_, R_)
                tt_op("dve", p["ki"][:], p["t0"][:], p["t1"][:], ALU.subtract, R_, R_)
                tt_op("dve", p["ki"][:], p["ki"][:], p["den"][:], ALU.mult, R_, R_)
                for ti in range(4):
                    P.op("dve", lambda e, ti=ti: e.tensor_scalar(out=tB[:], in0=BDi[:, ti, :], scalar1=p["ki"][:, ti:ti + 1], scalar2=None, op0=ALU.mult), reads=[rBD, rp], writes=[rtB])
                    P.op("dve", lambda e, ti=ti: e.scalar_tensor_tensor(out=BBr[:, ti, :], in0=BDr[:, ti, :], scalar=p["kr"][:, ti:ti + 1], in1=tB[:], op0=ALU.mult, op1=ALU.subtract),
                         reads=[rBD, rp, rtB], writes=[rBB])
                    P.op("dve", lambda e, ti=ti: e.tensor_scalar(out=tB[:], in0=BDr[:, ti, :], scalar1=p["ki"][:, ti:ti + 1], scalar2=None, op0=ALU.mult), reads=[rBD, rp, rBB], writes=[rtB])
                    P.op("dve", lambda e, ti=ti: e.scalar_tensor_tensor(out=BBi[:, ti, :], in0=BDi[:, ti, :], scalar=p["kr"][:, ti:ti + 1], in1=tB[:], op0=ALU.mult, op1=ALU.add),
                         reads=[rBD, rp, rtB], writes=[rBB])
                for ti in range(4):
                    for (src, rsrc, dst, sc) in ((BBr, rBB, WBr, 1.0), (BBi, rBB, WBi, 1.0), (CDr, rCD, WCr, 1.0), (CDi, rCD, WCi, -1.0)):
                        ip, rpt = rptr.next()
                        P.op("pe", lambda e, ip=ip, src=src, ti=ti: e.transpose(out=ptr[ip][:, 0:128], in_=src[:, ti, :], identity=ident[:]), reads=[rsrc, rid], writes=[rpt], chain=True)
                        P.op("act", lambda e, ip=ip, dst=dst, ti=ti, sc=sc: e.activation(out=dst[:, ti, :], in_=ptr[ip][:, 0:128], func=AF.Copy, scale=sc), reads=[rpt], writes=[rWt])
                P.op("pool", lambda e: e.memset(cE[:, :, 0:1], 1.0), writes=[rE])
                P.op("pool", lambda e: e.memset(sE[:, :, 0:1], 0.0), writes=[rE])
                for k in range(8):
                    m = 1 << k
                    for ti in range(4):
                        P.op("dve", lambda e, ti=ti, m=m, k=k: e.tensor_scalar(out=tE[:, 0:m], in0=sE[:, ti, 0:m], scalar1=ws[:, k, ti:ti + 1], scalar2=None, op0=ALU.mult), reads=[rE, rp], writes=[rtE])
                        P.op("dve", lambda e, ti=ti, m=m, k=k: e.scalar_tensor_tensor(out=cE[:, ti, m:2 * m], in0=cE[:, ti, 0:m], scalar=wc[:, k, ti:ti + 1], in1=tE[:, 0:m], op0=ALU.mult, op1=ALU.subtract),
                             reads=[rE, rp, rtE], writes=[rE])
                        P.op("dve", lambda e, ti=ti, m=m, k=k: e.tensor_scalar(out=tE[:, 0:m], in0=sE[:, ti, 0:m], scalar1=wc[:, k, ti:ti + 1], scalar2=None, op0=ALU.mult), reads=[rE, rp], writes=[rtE])
                        P.op("dve", lambda e, ti=ti, m=m, k=k: e.scalar_tensor_tensor(out=sE[:, ti, m:2 * m], in0=cE[:, ti, 0:m], scalar=ws[:, k, ti:ti + 1], in1=tE[:, 0:m], op0=ALU.mult, op1=ALU.add),
                             reads=[rE, rp, rtE], writes=[rE])
                P.dma("sp", uT[:], dr["u16"][fc], reads=[rr["u16"]], writes=[ru])
                P.op("dve", lambda e: e.memset(init_re[:], 0.0), writes=[rinit])
                P.op("dve", lambda e: e.memset(init_im[:], 0.0), writes=[rinit])
                order = [NCH] + (list(range(NCH)) if d == 0 else list(range(NCH - 1, -1, -1)))
                for ci in order:
                    t0 = ci * L
                    iy, ry = rpy.next()
                    for ti in range(4):
                        ib, rb = rpbu.next()
                        P.op("pe", lambda e, ib=ib, ti=ti, t0=t0: e.matmul(pbu[ib][:, 0, :], lhsT=WBr[:, ti, :], rhs=uT[:, t0:t0 + L], start=True, stop=True), reads=[rWt, ru], writes=[rb], chain=True)
                        P.op("pe", lambda e, ib=ib, ti=ti, t0=t0: e.matmul(pbu[ib][:, 1, :], lhsT=WBi[:, ti, :], rhs=uT[:, t0:t0 + L], start=True, stop=True), reads=[rWt, ru], writes=[rb], chain=True)
                        if d == 0:
                            cv, sv = cE[:, ti, :], sE[:, ti, :]
                        else:
                            cv, sv = cE[:, ti, ::-1], sE[:, ti, ::-1]
                        bre, bim = pbu[ib][:, 0, :], pbu[ib][:, 1, :]
                        ids = [rtt[j].next() for j in range(4)]
                        tl = [tt[j][ids[j][0]] for j in range(4)]
                        rl = [ids[j][1] for j in range(4)]
                        tt_op("dve", tl[0][:], bre, cv, ALU.mult, [rb, rE], [rl[0]])
                        tt_op("dve", tl[1][:], bim, sv, ALU.mult, [rb, rE], [rl[1]])
                        tt_op("dve", tl[2][:], bim, cv, ALU.mult, [rb, rE], [rl[2]])
                        tt_op("dve", tl[3][:], bre, sv, ALU.mult, [rb, rE], [rl[3]])
                        (i0, rv0), (i1, rv1) = rvv[0].next(), rvv[1].next()
                        vre, vim = vv[0][i0], vv[1][i1]
                        tt_op("pool", vre[:], tl[0][:], tl[1][:], ALU.add, [rl[0], rl[1]], [rv0])
                        tt_op("pool", vim[:], tl[2][:], tl[3][:], ALU.subtract, [rl[2], rl[3]], [rv1])
                        (j0, rg0), (j1, rg1) = rgg[0].next(), rgg[1].next()
                        gre, gim = gg[0][j0], gg[1][j1]
                        rbc = pr["r"][:, ti:ti + 1].to_broadcast([128, L])
                        if d == 0:
                            P.op("dve", lambda e, gre=gre, vre=vre, ti=ti, rbc=rbc: e.tensor_tensor_scan(out=gre[:], data0=rbc, data1=vre[:], initial=init_re[:, ti:ti + 1], op0=ALU.mult, op1=ALU.add),
                                 reads=[rv0, rp, rinit], writes=[rg0])
                            P.op("dve", lambda e, gim=gim, vim=vim, ti=ti, rbc=rbc: e.tensor_tensor_scan(out=gim[:], data0=rbc, data1=vim[:], initial=init_im[:, ti:ti + 1], op0=ALU.mult, op1=ALU.add),
                                 reads=[rv1, rp, rinit], writes=[rg1])
                            last = L - 1
                        else:
                            P.op("dve", lambda e, gre=gre, vre=vre, ti=ti, rbc=rbc: e.tensor_tensor_scan(out=gre[:, ::-1], data0=rbc, data1=vre[:, ::-1], initial=init_re[:, ti:ti + 1], op0=ALU.mult, op1=ALU.add),
                                 reads=[rv0, rp, rinit], writes=[rg0])
                            P.op("dve", lambda e, gim=gim, vim=vim, ti=ti, rbc=rbc: e.tensor_tensor_scan(out=gim[:, ::-1], data0=rbc, data1=vim[:, ::-1], initial=init_im[:, ti:ti + 1], op0=ALU.mult, op1=ALU.add),
                                 reads=[rv1, rp, rinit], writes=[rg1])
                            last = 0
                        P.op("dve", lambda e, gim=gim, ti=ti, last=last: e.tensor_scalar(out=tI[:, 0:1], in0=gim[:, last:last + 1], scalar1=ws[:, 8, ti:ti + 1], scalar2=None, op0=ALU.mult), reads=[rg1, rp], writes=[rtI])
                        P.op("dve", lambda e, gim=gim, ti=ti, last=last: e.tensor_scalar(out=tI[:, 1:2], in0=gim[:, last:last + 1], scalar1=wc[:, 8, ti:ti + 1], scalar2=None, op0=ALU.mult), reads=[rg1, rp], writes=[rtI])
                        P.op("dve", lambda e, gre=gre, ti=ti, last=last: e.scalar_tensor_tensor(out=init_re[:, ti:ti + 1], in0=gre[:, last:last + 1], scalar=wc[:, 8, ti:ti + 1], in1=tI[:, 0:1], op0=ALU.mult, op1=ALU.subtract),
                             reads=[rg0, rp, rtI], writes=[rinit])
                        P.op("dve", lambda e, gre=gre, ti=ti, last=last: e.scalar_tensor_tensor(out=init_im[:, ti:ti + 1], in0=gre[:, last:last + 1], scalar=ws[:, 8, ti:ti + 1], in1=tI[:, 1:2], op0=ALU.mult, op1=ALU.add),
                             reads=[rg0, rp, rtI], writes=[rinit])
                        mids = [rmm[j].next() for j in range(4)]
                        ml = [mm_[j][mids[j][0]] for j in range(4)]
                        rml = [mids[j][1] for j in range(4)]
                        tt_op("pool", ml[0][:], gre[:], cv, ALU.mult, [rg0, rE], [rml[0]])
                        tt_op("pool", ml[1][:], gim[:], sv, ALU.mult, [rg1, rE], [rml[1]])
                        tt_op("pool", ml[2][:], gre[:], sv, ALU.mult, [rg0, rE], [rml[2]])
                        tt_op("pool", ml[3][:], gim[:], cv, ALU.mult, [rg1, rE], [rml[3]])
                        (k0, rh0), (k1, rh1) = rhh[0].next(), rhh[1].next()
                        hre, him = hh[0][k0], hh[1][k1]
                        tt_op("dve", hre[:], ml[0][:], ml[1][:], ALU.subtract, [rml[0], rml[1]], [rh0])
                        tt_op("dve", him[:], ml[2][:], ml[3][:], ALU.add, [rml[2], rml[3]], [rh1])
                        P.op("pe", lambda e, iy=iy, ti=ti, hre=hre: e.matmul(py[iy][:, :L], lhsT=WCr[:, ti, :], rhs=hre[:], start=(ti == 0), stop=False), reads=[rWt, rh0], writes=[ry], chain=True)
                        P.op("pe", lambda e, iy=iy, ti=ti, him=him: e.matmul(py[iy][:, :L], lhsT=WCi[:, ti, :], rhs=him[:], start=False, stop=(ti == 3)), reads=[rWt, rh1], writes=[ry], chain=True)
                    io, ro = rych.next()
                    if d == 0:
                        P.op("dve", lambda e, io=io, iy=iy, t0=t0, fc=fc: e.scalar_tensor_tensor(out=ych[io][:], in0=uT[:, t0:t0 + L], scalar=dvec[:, fc:fc + 1], in1=py[iy][:, :L], op0=ALU.mult, op1=ALU.add),
                             reads=[ru, rdv, ry], writes=[ro])
                    else:
                        ipv, rpv = ryp.next()
                        P.dma("sp", yprev[ipv][:], dr["ys5"][fc, :, t0:t0 + L], reads=[rr["ys5"]], writes=[rpv])
                        tt_op("dve", ych[io][:], py[iy][:, :L], yprev[ipv][:], ALU.add, [ry, rpv], [ro])
                    P.dma("sp", dr["ys5"][fc, :, t0:t0 + L], ych[io][:], reads=[ro], writes=[rr["ys5"]])
    self.P.barrier()
    with ExitStack() as es:
        Wg = self.sb(es, "sg_w", [128, 4, 512], BF16); rWg = Res()
        for k in range(4):
            P.dma("pool", Wg[:, k, :], dr["s5_glu_w"][l, k * 128:(k + 1) * 128, :], writes=[rWg], accum=True)
        gb, rgb = self.load_vec(es, "sg_b", dr["s5_glu_b"][l], 4)
        y = self.sb(es, "sg_y", [128, 4, L], F32); ry_ = Res()
        x2 = self.sb(es, "sg_x2", [128, 4, L], F32); rx2 = Res()
        g32 = self.sb(es, "sg_g32", [128, 4, L], F32); rg32 = Res()
        g16 = self.sb(es, "sg_g16", [128, 4, L], BF16); rg16 = Res()
        sz = [self.sb(es, "sg_sz%d" % i, [128, L], F32) for i in range(2)]; rsz = Rot(2)
        st = [self.sb(es, "sg_st%d" % i, [128, L], BF16) for i in range(2)]; rst = Rot(2)
        pz = [self.ps(es, "sg_pz%d" % i, [128, 512]) for i in range(2)]; rpz = Rot(2)
        for (t0, n) in self.chunks(need_ctx):
            P.dma("sp", y[:, :, :n], dr["ys5"][:, :, t0:t0 + n].rearrange("k p t -> p k t"), reads=[rr["ys5"]], writes=[ry_])
            P.op("dve", lambda e: e.tensor_tensor(out=x2[:], in0=y[:], in1=y[:], op=ALU.mult), reads=[ry_], writes=[rx2])
            P.op("dve", lambda e: e.tensor_scalar(out=x2[:], in0=x2[:], scalar1=0.044715, scalar2=1.0, op0=ALU.mult, op1=ALU.add), reads=[rx2], writes=[rx2])
            P.op("dve", lambda e: e.tensor_tensor(out=x2[:], in0=x2[:], in1=y[:], op=ALU.mult), reads=[rx2, ry_], writes=[rx2])
            P.op("act", lambda e: e.activation(out=x2[:], in_=x2[:], func=AF.Sigmoid, scale=2.0 * math.sqrt(2.0 / math.pi)), reads=[rx2], writes=[rx2])
            P.op("dve", lambda e: e.tensor_tensor(out=g32[:], in0=x2[:], in1=y[:], op=ALU.mult), reads=[rx2, ry_], writes=[rg32])
            P.op("act", lambda e: e.activation(out=g16[:], in_=g32[:], func=AF.Copy), reads=[rg32], writes=[rg16])
            for m in range(4):
                ip, rp_ = rpz.next(); isz, rs_ = rsz.next(); ist, rt_ = rst.next()
                for k in range(4):
                    P.op("pe", lambda e, ip=ip, m=m, k=k: e.matmul(pz[ip][:, :n], lhsT=Wg[:, k, m * 128:(m + 1) * 128], rhs=g16[:, k, :n], start=(k == 0), stop=(k == 3)),
                         reads=[rWg, rg16], writes=[rp_], chain=True)
                P.op("act", lambda e, ip=ip, isz=isz, m=m: e.activation(out=sz[isz][:, :n], in_=pz[ip][:, :n], func=AF.Sigmoid, bias=gb[:, m:m + 1], scale=1.0), reads=[rp_, rgb], writes=[rs_])
                P.op("dve", lambda e, isz=isz, ist=ist, m=m: e.tensor_tensor(out=st[ist][:, :n], in0=sz[isz][:, :n], in1=g32[:, m, :n], op=ALU.mult), reads=[rs_, rg32], writes=[rt_])
                P.dma("sp", dr["ymx"][2, m, :, t0:t0 + n], st[ist][:, :n], reads=[rt_], writes=[rr["ymx"]])


def install(K):

    def phase_proj(self, l):
        P, nc, T, TL = self.P, self.nc, self.T, self.TL
        dr, rr = self.dr, self.rr
        with ExitStack() as es:
            W = self.sb(es, "p_w", [128, 8, 7328], BF16); rW = Res()
            Wsw = self.sb(es, "p_wsw", [128, 8, 672], BF16); rWsw = Res()
            for k in range(8):
                P.dma("pool", W[:, k, :], dr["w_in"][l, k * 128:(k + 1) * 128, :], writes=[rW], accum=True)
                P.dma("pool", Wsw[:, k, :], dr["w_in_sw"][l, k * 128:(k + 1) * 128, :], writes=[rWsw], accum=True)
            wuq = self.sb(es, "p_wuq", [128, 2, 768], BF16); wuqs = self.sb(es, "p_wuqs", [128, 2, 768], BF16); rwuq = Res()
            for j in range(2):
                P.dma("pool", wuq[:, j, :], dr["mla_w_uq"][l, j * 128:(j + 1) * 128, :], writes=[rwuq], accum=True)
                P.dma("pool", wuqs[:, j, :], dr["mla_w_uq_sw"][l, j * 128:(j + 1) * 128, :], writes=[rwuq], accum=True)
            wukv = self.sb(es, "p_wukv", [128, 1024], BF16); rwukv = Res()
            P.dma("pool", wukv[:], dr["mla_w_ukv"][l], writes=[rwukv])
            normw, rn = self.load_vec(es, "p_nw", dr["mix_norm"][l], 8)
            qn, rqn = self.load_vec(es, "p_qn", dr["mla_q_norm"][l], 2)
            kvn, rkvn = self.load_vec(es, "p_kvn", dr["mla_kv_norm"][l], 1)
            A, rA = self.make_AS(es, "p", normw, rn, 3, 4)
            hc = self.sb(es, "p_h", [128, 8, 512], F32); rh = Res()
            nT = self.sb(es, "p_nT", [128, 8, 512], BF16); rnT = Res()
            nt = self.norm_tiles(es, "p")
            (sq, rsq, pss, rpss, rstd, rrstd, ones, rones, tmpn, rtmpn) = nt
            c64 = self.sb(es, "p_c64", [128, CH], F32); s64 = self.sb(es, "p_s64", [128, CH], F32)
            c64q = self.sb(es, "p_c64q", [128, CH], F32); s64q = self.sb(es, "p_s64q", [128, CH], F32)
            c32q = self.sb(es, "p_c32q", [96, CH], F32); s32q = self.sb(es, "p_s32q", [96, CH], F32)
            c32k = self.sb(es, "p_c32k", [32, CH], F32); s32k = self.sb(es, "p_s32k", [32, CH], F32)
            rtab = Res()
            cq = self.sb(es, "p_cq", [128, 2, CH], F32); rcq = Res()
            cqn = self.sb(es, "p_cqn", [128, 2, CH], BF16); rcqn = Res()
            ckv = self.sb(es, "p_ckv", [128, CH], F32); rckv = Res()
            ckvn = self.sb(es, "p_ckvn", [128, CH], BF16); rckvn = Res()
            NST = 6
            stg = [self.sb(es, "p_st%d" % i, [128, 512], BF16) for i in range(NST)]; rstg = Rot(NST)
            t1 = [self.sb(es, "p_t1%d" % i, [128, CH], F32) for i in range(2)]; rt1 = Rot(2)
            t2 = [self.sb(es, "p_t2%d" % i, [128, CH], F32) for i in range(2)]; rt2 = Rot(2)
            NPS = 6
            pp = [self.ps(es, "p_ps%d" % i, [128, 512]) for i in range(NPS)]; rpp = Rot(NPS)
            tog = [0]

            def mm(ps_ap, terms, rps, reads):
                nterm = len(terms)
                for i, (lt, rh_) in enumerate(terms):
                    P.op("pe", lambda e, lt=lt, rh_=rh_, i=i: e.matmul(ps_ap, lhsT=lt, rhs=rh_, start=(i == 0), stop=(i == nterm - 1)),
                         reads=reads, writes=[rps], chain=True)

            def evac(out_ap, in_ap, rin, rout, scale=1.0, func=None):
                tog[0] ^= 1
                if func is not None or tog[0]:
                    P.op("act", lambda e: e.activation(out=out_ap, in_=in_ap, func=(func or AF.Copy), scale=scale), reads=[rin], writes=[rout])
                else:
                    P.op("dve", lambda e: e.tensor_scalar(out=out_ap, in0=in_ap, scalar1=scale, scalar2=None, op0=ALU.mult), reads=[rin], writes=[rout])

            for (t0, n) in self.chunks(True):
                s = 0 if t0 < TL else 1
                P.dma("sp", hc[:, :, :n], dr["hT"][:, :, t0:t0 + n].rearrange("k p t -> p k t"), reads=[rr["hT"]], writes=[rh])
                for (tt, nm) in ((c64, "k_cos64"), (s64, "k_sin64"), (c64q, "k_cos64q"), (s64q, "k_sin64q"), (c32q, "k_cos32q"), (s32q, "k_sin32q"),
                                 (c32k, "k_cos32k"), (s32k, "k_sin32k")):
                    P.dma("sp", tt[:, :n], dr[nm][:, t0:t0 + n], writes=[rtab], accum=True)
                self.norm_chunk(nt, hc, rh, n, s, A, rA, 3, nT, rnT)

                def fm(col0, M, Wt=W, rWt=rW):
                    ip, rp = rpp.next()
                    mm(pp[ip][:M, :n], [(Wt[:, k, col0:col0 + M], nT[:, k, :n]) for k in range(8)], rp, [rWt, rnT])
                    return pp[ip], rp

                def out_fm(dst_ap, rdst, ps, rp, M, scale=1.0, func=None):
                    i, rs = rstg.next()
                    evac(stg[i][:M, :n], ps[:M, :n], rp, rs, scale, func)
                    P.dma("sp", dst_ap, stg[i][:M, :n], reads=[rs], writes=[rdst])

                def rope_out(dst_ap, rdst, ps1, rp1, ps2, rp2, cs, sn, lo, hi):
                    i1, r1 = rt1.next(); i2, r2 = rt2.next(); i, rs = rstg.next()
                    P.op("dve", lambda e: e.tensor_tensor(out=t1[i1][lo:hi, :n], in0=ps1[lo:hi, :n], in1=cs[lo:hi, :n], op=ALU.mult), reads=[rp1, rtab], writes=[r1])
                    P.op("dve", lambda e: e.tensor_tensor(out=t2[i2][lo:hi, :n], in0=ps2[lo:hi, :n], in1=sn[lo:hi, :n], op=ALU.mult), reads=[rp2, rtab], writes=[r2])
                    P.op("pool", lambda e: e.tensor_tensor(out=stg[i][lo:hi, :n], in0=t1[i1][lo:hi, :n], in1=t2[i2][lo:hi, :n], op=ALU.add), reads=[r1, r2], writes=[rs])
                    return i, rs

                for j in range(4):
                    ps, rp = fm(128 * j, 128); out_fm(dr["qna"][j, :, t0:t0 + n], rr["qna"], ps, rp, 128, 0.125)
                    ps, rp = fm(512 + 128 * j, 128); out_fm(dr["kna"][j, :, t0:t0 + n], rr["kna"], ps, rp, 128)
                for j in range(4):
                    ps1, rp1 = fm(1536 + 128 * j, 128); ps2, rp2 = fm(128 * j, 128, Wsw, rWsw)
                    i, rs = rope_out(None, None, ps1, rp1, ps2, rp2, c64q, s64q, 0, 128)
                    P.dma("sp", dr["qsw"][j, :, t0:t0 + n], stg[i][:, :n], reads=[rs], writes=[rr["qsw"]])
                ps1, rp1 = fm(2048, 128); ps2, rp2 = fm(512, 128, Wsw, rWsw)
                i, rs = rope_out(None, None, ps1, rp1, ps2, rp2, c64, s64, 0, 128)
                P.dma("sp", dr["ksw"][:, t0:t0 + n], stg[i][:, :n], reads=[rs], writes=[rr["ksw"]])
                for j in range(4):
                    ps, rp = fm(2304 + 128 * j, 128); out_fm(dr["u16"][j, :, t0:t0 + n], rr["u16"], ps, rp, 128)
                for j in range(2):
                    ps, rp = fm(2816 + 128 * j, 128)
                    P.op("act", lambda e, j=j, ps=ps: e.activation(out=cq[:, j, :n], in_=ps[:, :n], func=AF.Copy), reads=[rp], writes=[rcq])
                ps, rp = fm(3072, 128)
                P.op("dve", lambda e, ps=ps: e.tensor_copy(out=ckv[:, :n], in_=ps[:, :n]), reads=[rp], writes=[rckv])
                ps1, rp1 = fm(3200, 32); ps2, rp2 = fm(640, 32, Wsw, rWsw)
                i, rs = rope_out(None, None, ps1, rp1, ps2, rp2, c32k, s32k, 0, 32)
                for h in range(8):
                    P.dma("sp", dr["kml"][h, 64:96, t0:t0 + n], stg[i][0:32, :n], reads=[rs], writes=[rr["kml"]])
                for j in range(32):
                    ps, rp = fm(3232 + 128 * j, 128)
                    out_fm(dr["gat"][j // 8, j % 8, :, t0:t0 + n], rr["gat"], ps, rp, 128, 1.0, AF.Sigmoid)
                for sub in range(n // 128):
                    tk = slice(sub * 128, sub * 128 + 128)
                    i, rs = rstg.next()
                    for hf in range(2):
                        ip, rp = rpp.next()
                        mm(pp[ip][:, :256], [(nT[:, k, tk], W[:, k, 1024 + 256 * hf:1024 + 256 * hf + 256]) for k in range(8)], rp, [rW, rnT])
                        evac(stg[i][:, 256 * hf:256 * hf + 256], pp[ip][:, :256], rp, rs)
                    P.dma("sp", dr["vna"][t0 + sub * 128:t0 + sub * 128 + 128, :], stg[i][:, :], reads=[rs], writes=[rr["vna"]])
                    i, rs = rstg.next(); ip, rp = rpp.next()
                    mm(pp[ip][:, :128], [(nT[:, k, tk], W[:, k, 2176:2304]) for k in range(8)], rp, [rW, rnT])
                    evac(stg[i][:, :128], pp[ip][:, :128], rp, rs)
                    P.dma("sp", dr["vsw"][t0 + sub * 128:t0 + sub * 128 + 128, :], stg[i][:, :128], reads=[rs], writes=[rr["vsw"]])
                P.op("act", lambda e: e.activation(out=sq[:, 0:2, :n], in_=cq[:, :, :n], func=AF.Square), reads=[rcq], writes=[rsq])
                mm(pss[:, :n], [(ones[:], sq[:, j, :n]) for j in range(2)], rpss, [rsq, rones])
                P.op("act", lambda e: e.activation(out=rstd[:, :n], in_=pss[:, :n], func=AF.Sqrt, bias=self.epsb[:, 0:1], scale=1.0 / 256), reads=[rpss, self.reps], writes=[rrstd])
                P.op("dve", lambda e: e.reciprocal(out=rstd[:, :n], in_=rstd[:, :n]), reads=[rrstd], writes=[rrstd])
                for j in range(2):
                    P.op("dve", lambda e, j=j: e.scalar_tensor_tensor(out=cqn[:, j, :n], in0=cq[:, j, :n], scalar=qn[:, j:j + 1], in1=rstd[:, :n], op0=ALU.mult, op1=ALU.mult),
                         reads=[rcq, rqn, rrstd], writes=[rcqn])
                for h in range(8):
                    ip1, rp1 = rpp.next(); ip2, rp2 = rpp.next()
                    mm(pp[ip1][:96, :n], [(wuq[:, j, 96 * h:96 * h + 96], cqn[:, j, :n]) for j in range(2)], rp1, [rwuq, rcqn])
                    mm(pp[ip2][:96, :n], [(wuqs[:, j, 96 * h:96 * h + 96], cqn[:, j, :n]) for j in range(2)], rp2, [rwuq, rcqn])
                    i, rs = rope_out(None, None, pp[ip1], rp1, pp[ip2], rp2, c32q, s32q, 64, 96)
                    P.op("act", lambda e, i=i, ip1=ip1: e.activation(out=stg[i][0:64, :n], in_=pp[ip1][0:64, :n], func=AF.Copy, scale=96.0 ** -0.5), reads=[rp1], writes=[rs])
                    P.dma("sp", dr["qml"][h, :, t0:t0 + n], stg[i][0:96, :n], reads=[rs], writes=[rr["qml"]])
                P.op("act", lambda e: e.activation(out=sq[:, 0, :n], in_=ckv[:, :n], func=AF.Square), reads=[rckv], writes=[rsq])
                mm(pss[:, :n], [(ones[:], sq[:, 0, :n])], rpss, [rsq, rones])
                P.op("act", lambda e: e.activation(out=rstd[:, :n], in_=pss[:, :n], func=AF.Sqrt, bias=self.epsb[:, 0:1], scale=1.0 / 128), reads=[rpss, self.reps], writes=[rrstd])
                P.op("dve", lambda e: e.reciprocal(out=rstd[:, :n], in_=rstd[:, :n]), reads=[rrstd], writes=[rrstd])
                P.op("dve", lambda e: e.scalar_tensor_tensor(out=ckvn[:, :n], in0=ckv[:, :n], scalar=kvn[:, 0:1], in1=rstd[:, :n], op0=ALU.mult, op1=ALU.mult),
                     reads=[rckv, rkvn, rrstd], writes=[rckvn])
                for h in range(8):
                    ip, rp = rpp.next()
                    mm(pp[ip][:64, :n], [(wukv[:, 128 * h:128 * h + 64], ckvn[:, :n])], rp, [rwukv, rckvn])
                    out_fm(dr["kml"][h, 0:64, t0:t0 + n], rr["kml"], pp[ip], rp, 64)
                wv = wukv[:].rearrange("p (h c) -> p h c", c=128)
                for sub in range(n // 128):
                    tk = slice(sub * 128, sub * 128 + 128)
                    i, rs = rstg.next()
                    for hf in range(2):
                        ip, rp = rpp.next()
                        mm(pp[ip][:, :256].rearrange("p (h c) -> p h c", c=64), [(ckvn[:, tk], wv[:, 4 * hf:4 * hf + 4, 64:128])], rp, [rwukv, rckvn])
                        evac(stg[i][:, 256 * hf:256 * hf + 256], pp[ip][:, :256], rp, rs)
                    P.dma("sp", dr["vml"][t0 + sub * 128:t0 + sub * 128 + 128, :], stg[i][:, :], reads=[rs], writes=[rr["vml"]])

    def attn_tiles(self, es, pref):
        st = {}
        st["ps"] = [self.ps(es, pref + "_s%d" % i, [128, 512]) for i in range(3)]; st["rps"] = Rot(3)
        st["po"] = [self.ps(es, pref + "_o%d" % i, [128, 512]) for i in range(2)]; st["rpo"] = Rot(2)
        st["e"] = [self.sb(es, pref + "_e%d" % i, [128, 256], BF16) for i in range(4)]; st["re"] = Rot(4)
        st["rec"] = [self.sb(es, pref + "_r%d" % i, [128, 256], F32) for i in range(2)]; st["rrec"] = Rot(2)
        st["y"] = [self.sb(es, pref + "_y%d" % i, [64, 256], BF16) for i in range(3)]; st["ry"] = Rot(3)
        return st

    def attn_job(self, st, qap, N, tiles, rin, sink=None):
        P = self.P
        io, ro = st["rpo"].next()
        po = st["po"][io]
        nt = len(tiles)
        for ti, (kap, vap, mask, rmask) in enumerate(tiles):
            ip, rp = st["rps"].next(); ie, re_ = st["re"].next()
            ps = st["ps"][ip]; et = st["e"][ie]
            P.op("pe", lambda e, ps=ps, kap=kap: e.matmul(ps[:, :N], lhsT=kap, rhs=qap, start=True, stop=True), reads=rin, writes=[rp], chain=True)
            P.op("act", lambda e, ps=ps, et=et: e.activation(out=et[:, :N], in_=ps[:, :N], func=AF.Exp), reads=[rp], writes=[re_])
            if mask is not None:
                P.op("dve", lambda e, et=et, mask=mask: e.tensor_tensor(out=et[:, :N], in0=et[:, :N], in1=mask, op=ALU.mult), reads=[re_, rmask], writes=[re_])
            P.op("pe", lambda e, po=po, vap=vap, et=et, ti=ti: e.matmul(po[:, :N], lhsT=vap, rhs=et[:, :N], start=(ti == 0), stop=(ti == nt - 1)),
                 reads=rin + [re_], writes=[ro], chain=True)
        ir, rrc = st["rrec"].next(); iy, ry = st["ry"].next()
        rec = st["rec"][ir]; y = st["y"][iy]
        if sink is not None:
            esink, rsink, blocks = sink
            for (c0, c1, sc) in blocks:
                P.op("dve", lambda e, c0=c0, c1=c1, sc=sc: e.tensor_scalar(out=rec[64:128, c0:c1], in0=po[64:128, c0:c1], scalar1=esink[64:128, sc:sc + 1], scalar2=None, op0=ALU.add),
                     reads=[ro, rsink], writes=[rrc])
            P.op("dve", lambda e: e.reciprocal(out=rec[0:64, :N], in_=rec[64:128, :N]), reads=[rrc], writes=[rrc])
        else:
            P.op("dve", lambda e: e.reciprocal(out=rec[0:64, :N], in_=po[64:128, :N]), reads=[ro], writes=[rrc])
        P.op("dve", lambda e: e.tensor_tensor(out=y[0:64, :N], in0=po[0:64, :N], in1=rec[0:64, :N], op=ALU.mult), reads=[ro, rrc], writes=[ry])
        return y, ry

    def load_vaug(self, vaug, rv, src, h, dv_off, ntile, first):
        P = self.P
        if first:
            P.op("pool", lambda e: e.memset(vaug[:, :, 64:128], 1.0), writes=[rv])
        P.dma("sp", vaug[:, :, 0:64], src[:, dv_off:dv_off + 64].rearrange("(t p) c -> p t c", p=128), writes=[rv])

    def phase_mla(self, l, need_ctx):
        P, T, TL = self.P, self.T, self.TL
        dr, rr = self.dr, self.rr
        NT = T // 128
        with ExitStack() as es:
            st = attn_tiles(self, es, "ml")
            kT = self.sb(es, "ml_k", [96, T], BF16); qT = self.sb(es, "ml_q", [96, T], BF16)
            vaug = self.sb(es, "ml_v", [128, NT, 128], BF16)
            rk, rq, rv = Res(), Res(), Res()
            for h in range(8):
                P.dma("sp", kT[:], dr["kml"][h], reads=[rr["kml"]], writes=[rk])
                P.dma("sp", qT[:], dr["qml"][h], reads=[rr["qml"]], writes=[rq])
                load_vaug(self, vaug, rv, dr["vml"], h, 64 * h, NT, h == 0)
                for (t0, n) in self.chunks(need_ctx):
                    kt = range(NT) if t0 < TL else range(TL // 128, NT)
                    tiles = [(kT[:, j * 128:(j + 1) * 128], vaug[:, j, :], None, None) for j in kt]
                    y, ry = attn_job(self, st, qT[:, t0:t0 + n], n, tiles, [rk, rq, rv])
                    P.dma("sp", dr["ymx"][3, h // 2, (h % 2) * 64:(h % 2) * 64 + 64, t0:t0 + n], y[0:64, :n], reads=[ry], writes=[rr["ymx"]])

    def phase_swa(self, l, need_ctx):
        P, T, TL = self.P, self.T, self.TL
        dr, rr = self.dr, self.rr
        NT = T // 128
        NB = TL // 128
        with ExitStack() as es:
            st = attn_tiles(self, es, "sw")
            kT = self.sb(es, "sw_k", [64, T], BF16)
            qT = self.sb(es, "sw_q", [64, 4, T], BF16)
            vaug = self.sb(es, "sw_v", [128, NT, 128], BF16)
            mp = self.sb(es, "sw_mp", [128, 512], F32); mn = self.sb(es, "sw_mn", [128, 512], F32); rm = Res()
            sk = self.sb(es, "sw_sk", [128, 8], F32); rsk = Res()
            P.dma("sp", mp[:], dr["k_mprev"][:, :], writes=[rm]); P.dma("sp", mn[:], dr["k_mnext"][:, :], writes=[rm])
            P.dma("sp", sk[:], dr["swa_sink"][l, :].partition_broadcast(128), writes=[rsk])
            P.op("act", lambda e: e.activation(out=sk[:], in_=sk[:], func=AF.Exp), reads=[rsk], writes=[rsk])
            rk, rq, rv = Res(), Res(), Res()
            for g in range(2):
                P.dma("sp", kT[:], dr["ksw"][64 * g:64 * g + 64, :], reads=[rr["ksw"]], writes=[rk])
                for hh in range(4):
                    h = 4 * g + hh
                    P.dma("sp", qT[:, hh, :], dr["qsw"][h // 2, (h % 2) * 64:(h % 2) * 64 + 64, :], reads=[rr["qsw"]], writes=[rq])
                load_vaug(self, vaug, rv, dr["vsw"], g, 64 * g, NT, g == 0)
                ctx_tiles = [(kT[:, j * 128:(j + 1) * 128], vaug[:, j, :], None, None) for j in range(NB, NT)]
                for hp in range(2):
                    h0 = 4 * g + 2 * hp
                    blocks = [(0, 128, h0), (128, 256, h0 + 1)]
                    jobs = [(nb * 128, nb) for nb in range(NB)]
                    if need_ctx:
                        jobs += [(TL, -1), (TL + 128, -1)]
                    for (q0, nb) in jobs:
                        tiles = []
                        if nb >= 0:
                            if nb > 0:
                                tiles.append((kT[:, (nb - 1) * 128:nb * 128], vaug[:, nb - 1, :], mp[:, 0:256], rm))
                            tiles.append((kT[:, nb * 128:(nb + 1) * 128], vaug[:, nb, :], None, None))
                            if nb < NB - 1:
                                tiles.append((kT[:, (nb + 1) * 128:(nb + 2) * 128], vaug[:, nb + 1, :], mn[:, 0:256], rm))
                        tiles += ctx_tiles
                        y, ry = attn_job(self, st, qT[:, 2 * hp:2 * hp + 2, q0:q0 + 128], 256, tiles, [rk, rq, rv], sink=(sk, rsk, blocks))
                        for hb in range(2):
                            h = h0 + hb
                            P.dma("sp", dr["ymx"][1, h // 2, (h % 2) * 64:(h % 2) * 64 + 64, q0:q0 + 128], y[0:64, hb * 128:hb * 128 + 128], reads=[ry], writes=[rr["ymx"]])

    def phase_na(self, l, need_ctx):
        P, nc, T, TL = self.P, self.nc, self.T, self.TL
        dr, rr = self.dr, self.rr
        NT = T // 128
        R = TL // 64
        assert R >= 10
        with ExitStack() as es:
            with ExitStack() as es2:
                rpT = self.sb(es2, "na_rpT", [31, 8, 15], F32); rrp = Res()
                Gt = self.sb(es2, "na_G", [31, 4096], F32); cm = self.sb(es2, "na_cm", [15, 4096], F32); rG = Res()
                eb = self.sb(es2, "na_eb", [15, 4096], F32); reb = Res()
                pb = [self.ps(es2, "na_pb%d" % i, [128, 512]) for i in range(2)]; rpb = Rot(2)
                P.dma("sp", rpT[:], dr["na_rpb"][l].rearrange("h r c -> c h r"), writes=[rrp])
                P.dma("sp", Gt[:], dr["k_G"][:, :], writes=[rG]); P.dma("sp", cm[:], dr["k_cmask"][:, :], writes=[rG])
                for h in range(8):
                    for cc in range(16):
                        ip, rp = rpb.next()
                        P.op("pe", lambda e, ip=ip, h=h, cc=cc: e.matmul(pb[ip][:15, :256], lhsT=rpT[:, h, :], rhs=Gt[:, cc * 256:(cc + 1) * 256], start=True, stop=True),
                             reads=[rrp, rG], writes=[rp], chain=True)
                        P.op("act", lambda e, ip=ip, cc=cc: e.activation(out=eb[:, cc * 256:(cc + 1) * 256], in_=pb[ip][:15, :256], func=AF.Exp), reads=[rp], writes=[reb])
                    P.op("dve", lambda e: e.tensor_tensor(out=eb[:], in0=eb[:], in1=cm[:], op=ALU.mult), reads=[reb, rG], writes=[reb])
                    P.dma("sp", dr["ebd"][h].rearrange("r a b -> r (a b)"), eb[:], reads=[reb], writes=[rr["ebd"]])
            P.barrier()
            st = attn_tiles(self, es, "na")
            kT = self.sb(es, "na_k", [64, T], BF16); qT = self.sb(es, "na_q", [64, T], BF16)
            vaug = self.sb(es, "na_v", [128, NT, 128], BF16)
            EB = self.sb(es, "na_EB", [128, 5, 5, 128], F32); rEB = Res()
            P.op("pool", lambda e: e.memset(EB[:], 0.0), writes=[rEB])
            rk, rq, rv = Res(), Res(), Res()
            types = {0: (0, 0), 2: (1, 0), 4: (2, None), 6: (3, 2), 8: (4, 2)}
            for h in range(8):
                P.dma("sp", kT[:], dr["kna"][h // 2, (h % 2) * 64:(h % 2) * 64 + 64, :], reads=[rr["kna"]], writes=[rk])
                P.dma("sp", qT[:], dr["qna"][h // 2, (h % 2) * 64:(h % 2) * 64 + 64, :], reads=[rr["qna"]], writes=[rq])
                load_vaug(self, vaug, rv, dr["vna"], h, 64 * h, NT, h == 0)
                for qrel, (ty, r0r) in types.items():
                    for kt in range(5):
                        for jl in range(2):
                            for ql in range(2):
                                j = 2 * kt + jl
                                r0rel = ql if r0r is None else r0r
                                if not (r0rel <= j <= r0rel + 7):
                                    continue
                                drr = j - (qrel + ql) + 7
                                assert 0 <= drr <= 14
                                P.dma("sp", EB[jl * 64:(jl + 1) * 64, ty, kt, ql * 64:(ql + 1) * 64], dr["ebd"][h, drr], reads=[rr["ebd"]], writes=[rEB], accum=True)
                ctx_tiles = [(kT[:, j * 128:(j + 1) * 128], vaug[:, j, :], None, None) for j in range(TL // 128, NT)]
                for r in range(0, R, 2):
                    ks = min(max(r - 4, 0), R - 10)
                    ty = types[r - ks][0]
                    tiles = [(kT[:, (ks + 2 * kt) * 64:(ks + 2 * kt) * 64 + 128], vaug[:, (ks + 2 * kt) // 2, :], EB[:, ty, kt, :], rEB) for kt in range(5)]
                    tiles += ctx_tiles
                    y, ry = attn_job(self, st, qT[:, r * 64:r * 64 + 128], 128, tiles, [rk, rq, rv])
                    P.dma("sp", dr["ymx"][0, h // 2, (h % 2) * 64:(h % 2) * 64 + 64, r * 64:r * 64 + 128], y[0:64, :128], reads=[ry], writes=[rr["ymx"]])
                if need_ctx:
                    y, ry = attn_job(self, st, qT[:, TL:TL + 256], 256, ctx_tiles, [rk, rq, rv])
                    P.dma("sp", dr["ymx"][0, h // 2, (h % 2) * 64:(h % 2) * 64 + 64, TL:TL + 256], y[0:64, :256], reads=[ry], writes=[rr["ymx"]])

    def phase_merge(self, l, need_ctx):
        P, T, TL = self.P, self.T, self.TL
        dr, rr = self.dr, self.rr
        with ExitStack() as es:
            wb = self.sb(es, "g_wb", [128, 4, 4, 1024], BF16); rwb = Res()
            wo = self.sb(es, "g_wo", [128, 8, 1024], BF16); rwo = Res()
            for nb in range(4):
                for k in range(4):
                    P.dma("pool", wb[:, nb, k, :], dr["w_branch"][l, nb, k * 128:(k + 1) * 128, :], writes=[rwb], accum=True)
            for k in range(8):
                P.dma("pool", wo[:, k, :], dr["w_out"][l, k * 128:(k + 1) * 128, :], writes=[rwo], accum=True)
            yT = self.sb(es, "g_y", [128, 4, 4, CH], BF16); ryT = Res()
            gt = self.sb(es, "g_g", [128, 4, 8, CH], BF16); rgt = Res()
            hc = self.sb(es, "g_h", [128, 8, CH], F32); rh = Res()
            mg = self.sb(es, "g_m", [128, 8, CH], BF16); rmg = Res()
            acc = [self.sb(es, "g_a%d" % i, [128, CH], F32) for i in range(2)]; racc = Rot(2)
            tmp = [self.sb(es, "g_t%d" % i, [128, CH], F32) for i in range(2)]; rtmp = Rot(2)
            pp = [self.ps(es, "g_p%d" % i, [128, 512]) for i in range(4)]; rpp = Rot(4)
            po = [self.ps(es, "g_po%d" % i, [128, 512]) for i in range(2)]; rpo = Rot(2)
            for (t0, n) in self.chunks(need_ctx):
                s = 0 if t0 < TL else 1
                P.dma("sp", hc[:, :, :n], dr["hT"][:, :, t0:t0 + n].rearrange("k p t -> p k t"), reads=[rr["hT"]], writes=[rh])
                for nb in range(4):
                    P.dma("sp", yT[:, nb, :, :n], dr["ymx"][nb, :, :, t0:t0 + n].rearrange("k p t -> p k t"), reads=[rr["ymx"]], writes=[ryT], accum=True)
                    P.dma("sp", gt[:, nb, :, :n], dr["gat"][nb, :, :, t0:t0 + n].rearrange("k p t -> p k t"), reads=[rr["gat"]], writes=[rgt], accum=True)
                for m in range(8):
                    ia, ra = racc.next()
                    for nb in range(4):
                        ip, rp = rpp.next()
                        for k in range(4):
                            P.op("pe", lambda e, ip=ip, nb=nb, k=k, m=m: e.matmul(pp[ip][:, :n], lhsT=wb[:, nb, k, m * 128:(m + 1) * 128], rhs=yT[:, nb, k, :n], start=(k == 0), stop=(k == 3)),
                                 reads=[rwb, ryT], writes=[rp], chain=True)
                        if nb == 0:
                            P.op("dve", lambda e, ip=ip, ia=ia, nb=nb, m=m: e.tensor_tensor(out=acc[ia][:, :n], in0=pp[ip][:, :n], in1=gt[:, nb, m, :n], op=ALU.mult), reads=[rp, rgt], writes=[ra])
                        else:
                            it, rt = rtmp.next()
                            P.op("dve", lambda e, ip=ip, it=it, nb=nb, m=m: e.tensor_tensor(out=tmp[it][:, :n], in0=pp[ip][:, :n], in1=gt[:, nb, m, :n], op=ALU.mult), reads=[rp, rgt], writes=[rt])
                            if nb < 3:
                                P.op("pool", lambda e, ia=ia, it=it: e.tensor_tensor(out=acc[ia][:, :n], in0=acc[ia][:, :n], in1=tmp[it][:, :n], op=ALU.add), reads=[ra, rt], writes=[ra])
                            else:
                                P.op("pool", lambda e, ia=ia, it=it, m=m: e.tensor_tensor(out=mg[:, m, :n], in0=acc[ia][:, :n], in1=tmp[it][:, :n], op=ALU.add), reads=[ra, rt], writes=[rmg])
                for mo in range(8):
                    ip, rp = rpo.next()
                    for k in range(8):
                        P.op("pe", lambda e, ip=ip, mo=mo, k=k: e.matmul(po[ip][:, :n], lhsT=wo[:, k, mo * 128:(mo + 1) * 128], rhs=mg[:, k, :n], start=(k == 0), stop=(k == 7)),
                             reads=[rwo, rmg], writes=[rp], chain=True)
                    P.op("dve", lambda e, ip=ip, mo=mo, s=s, n=n: e.scalar_tensor_tensor(out=hc[:, mo, :n], in0=po[ip][:, :n], scalar=self.mod[:, 5, mo, s:s + 1], in1=hc[:, mo, :n], op0=ALU.mult, op1=ALU.add),
                         reads=[rp, self.rmod, rh], writes=[rh])
                P.dma("sp", dr["hT"][:, :, t0:t0 + n].rearrange("k p t -> p k t"), hc[:, :, :n], reads=[rh], writes=[rr["hT"]])

    K.phase_proj = phase_proj
    K.phase_mla = phase_mla
    K.phase_swa = phase_swa
    K.phase_na = phase_na
    K.phase_merge = phase_merge
    K.phase_s5 = phase_s5


install(K)
```

```python
import math
from contextlib import ExitStack
import numpy as np
import concourse.bass as bass
import concourse.mybir as mybir
from concourse.bass_utils import run_bass_kernel_spmd

F32 = mybir.dt.float32
BF16 = mybir.dt.bfloat16
AF = mybir.ActivationFunctionType
ALU = mybir.AluOpType

ENGS = ("pe", "act", "dve", "pool", "sp")
D = 1024
DFF = 2816
CTX = 256
NMOD = 9
INC = 7328
EPS = 1e-6


class Res:
    __slots__ = ("name", "writers", "readers")

    def __init__(self, name=""):
        self.name = name
        self.writers = {}
        self.readers = {}


class Prog:
    def __init__(self, nc, n_dma_sems=14):
        self.nc = nc
        self.q = {e: [] for e in ENGS}
        self.cnt = {e: 0 for e in ENGS}
        self.known = {}
        self.sems = {}
        self.n_dma_sems = n_dma_sems
        self.dma_n = {e: 0 for e in ENGS}
        self.dma_sems = {}
        self._ctx = []

    def open(self):
        nc = self.nc
        for e in ENGS:
            cm = nc.semaphore("s_" + e)
            self.sems[e] = cm.__enter__()
            self._ctx.append(cm)
        for e in ("sp", "pool", "act"):
            lst = []
            for i in range(self.n_dma_sems):
                cm = nc.semaphore("d_%s%d" % (e, i))
                lst.append(cm.__enter__())
                self._ctx.append(cm)
            self.dma_sems[e] = lst

    def close(self):
        for cm in reversed(self._ctx):
            cm.__exit__(None, None, None)
        self._ctx = []

    def _need(self, eng, deps):
        for key, (sem, val) in deps.items():
            k = (eng, key)
            if self.known.get(k, 0) >= val:
                continue
            self.known[k] = val
            self.q[eng].append(("wait", sem, val))

    def _collect(self, eng, reads, writes):
        deps = {}

        def add(d, same_ok):
            for key, (sem, val) in d.items():
                if key == eng and not same_ok:
                    continue
                if key not in deps or deps[key][1] < val:
                    deps[key] = (sem, val)
        for r in reads:
            add(r.writers, True)
        for w in writes:
            add(w.writers, False)
            add(w.readers, False)
        return deps

    def _commit(self, ev_key, ev, reads, writes):
        for w in writes:
            w.writers = {ev_key: ev}
            w.readers = {}
        for r in reads:
            if r in writes:
                continue
            old = r.readers.get(ev_key)
            if old is None or old[1] < ev[1]:
                r.readers[ev_key] = ev

    def op(self, eng, fn, reads=(), writes=(), chain=False):
        deps = self._collect(eng, reads, writes)
        if chain and eng in deps:
            del deps[eng]
        self._need(eng, deps)
        self.cnt[eng] += 1
        ev = (self.sems[eng], self.cnt[eng])
        self.q[eng].append(("op", fn, self.sems[eng]))
        self._commit(eng, ev, reads, writes)

    def dma(self, eng, out, in_, reads=(), writes=(), accum=False, **kw):
        if accum:
            deps = self._collect(eng, reads, ())
            for w in writes:
                for d_ in (w.writers, w.readers):
                    for key_, (sem_, val_) in d_.items():
                        if d_ is w.writers and key_.startswith("dma_"):
                            continue
                        if key_ not in deps or deps[key_][1] < val_:
                            deps[key_] = (sem_, val_)
        else:
            deps = self._collect(eng, reads, writes)
        n = self.dma_n[eng]
        self.dma_n[eng] += 1
        slot = n % self.n_dma_sems
        sem = self.dma_sems[eng][slot]
        rnd = n // self.n_dma_sems
        key = "dma_%s_%d" % (eng, slot)
        if rnd > 0:
            deps[key] = (sem, 16 * rnd)
        if eng in deps:
            del deps[eng]
        self._need(eng, deps)
        ev = (sem, 16 * (rnd + 1))
        self.q[eng].append(("dma", out, in_, sem, kw))
        if accum:
            for w in writes:
                w.writers[key] = ev
            self._commit(key, ev, reads, ())
        else:
            self._commit(key, ev, reads, writes)

    def barrier(self):
        deps = {}
        for e in ENGS:
            if self.cnt[e] > 0:
                deps[e] = (self.sems[e], self.cnt[e])
        for e in ("sp", "pool", "act"):
            n = self.dma_n[e]
            for slot in range(min(n, self.n_dma_sems)):
                last = ((n - 1 - slot) // self.n_dma_sems)
                deps["dma_%s_%d" % (e, slot)] = (self.dma_sems[e][slot], 16 * (last + 1))
        for e in ENGS:
            d = {k: v for k, v in deps.items() if k != e}
            self._need(e, d)

    def wait_all(self, eng, ress):
        deps = {}
        for r in ress:
            for d in (r.writers, r.readers):
                for key, (sem, val) in d.items():
                    if key == eng:
                        continue
                    if key not in deps or deps[key][1] < val:
                        deps[key] = (sem, val)
        self._need(eng, deps)

    def emit(self):
        nc = self.nc
        with nc.allow_non_contiguous_dma(reason="small strided vector loads"), nc.Block() as block:
            def make(ename):
                def body(e):
                    for it in self.q[ename]:
                        if it[0] == "wait":
                            e.wait_ge(it[1], it[2])
                        elif it[0] == "op":
                            it[1](e).then_inc(it[2], 1)
                        else:
                            _, out, in_, sem, kw = it
                            e.dma_start(out=out, in_=in_, **kw).then_inc(sem, 16)
                return body
            block.tensor(make("pe"))
            block.scalar(make("act"))
            block.vector(make("dve"))
            block.gpsimd(make("pool"))
            block.sync(make("sp"))


class Rot:
    def __init__(self, n, name=""):
        self.n = n
        self.i = 0
        self.res = [Res("%s%d" % (name, j)) for j in range(n)]

    def next(self):
        j = self.i % self.n
        self.i += 1
        return j, self.res[j]


class K:
    def __init__(self, TL, depth, debug=False, phases=None, nlayers=None):
        self.TL = TL
        self.T = TL + CTX
        self.depth = depth
        self.debug = debug
        self.phases = phases
        self.nlayers = depth if nlayers is None else nlayers
        self.nc = bass.Bass("TRN2", target_bir_lowering=False)
        self.P = Prog(self.nc)
        self.dr = {}
        self.rr = {}

    def din(self, name, shape, dt=F32):
        t = self.nc.dram_tensor(name, list(shape), dt, kind="ExternalInput").ap()
        self.dr[name] = t
        self.rr[name] = Res(name)
        return t

    def dscr(self, name, shape, dt, out=False):
        kind = "ExternalOutput" if (out or self.debug) else "Internal"
        t = self.nc.dram_tensor(name, list(shape), dt, kind=kind).ap()
        self.dr[name] = t
        self.rr[name] = Res(name)
        return t

    def chunks(self, with_ctx=True):
        out = [(c * 256, 256) for c in range(self.TL // 256)]
        if with_ctx:
            out.append((self.TL, CTX))
        return out

    def declare(self):
        L, T, TL = self.depth, self.T, self.TL
        d = self.din
        d("x", [TL, D]); d("c", [D]); d("ctx", [CTX, D]); d("c_ctx", [D])
        d("ada_w", [L, D, NMOD * D]); d("ada_b", [L, NMOD * D])
        for f in ("ffn1", "ffn2"):
            d(f + "_norm", [L, D]); d(f + "_w_gate", [L, D, DFF]); d(f + "_w_up", [L, D, DFF]); d(f + "_w_down", [L, DFF, D])
        d("mix_norm", [L, D]); d("w_in", [L, D, INC]); d("w_in_sw", [L, D, 672])
        d("na_rpb", [L, 8, 15, 31]); d("swa_sink", [L, 8])
        d("s5_lambda_re", [L, 2, 32, 64]); d("s5_lambda_im", [L, 2, 32, 64]); d("s5_log_dt", [L, 2, 32])
        d("s5_b_re", [L, 2, 32, 64, 16]); d("s5_b_im", [L, 2, 32, 64, 16])
        d("s5_c_re", [L, 2, 32, 16, 64]); d("s5_c_im", [L, 2, 32, 16, 64])
        d("s5_d", [L, 512]); d("s5_glu_w", [L, 512, 512]); d("s5_glu_b", [L, 512])
        d("mla_q_norm", [L, 256]); d("mla_w_uq", [L, 256, 768]); d("mla_w_uq_sw", [L, 256, 768])
        d("mla_kv_norm", [L, 128]); d("mla_w_ukv", [L, 128, 1024])
        d("w_branch", [L, 4, 512, D]); d("w_out", [L, D, D]); d("final_norm", [D])
        d("k_ident", [128, 128]); d("k_cos64", [128, T]); d("k_sin64", [128, T])
        d("k_cos64q", [128, T]); d("k_sin64q", [128, T])
        d("k_cos32q", [96, T]); d("k_sin32q", [96, T]); d("k_cos32k", [32, T]); d("k_sin32k", [32, T])
        d("k_mprev", [128, 512]); d("k_mnext", [128, 512])
        d("k_G", [31, 4096]); d("k_cmask", [15, 4096])
        s = self.dscr
        s("hT", [8, 128, T], F32)
        s("qna", [4, 128, T], BF16); s("kna", [4, 128, T], BF16); s("vna", [T, 512], BF16)
        s("qsw", [4, 128, T], BF16); s("ksw", [128, T], BF16); s("vsw", [T, 128], BF16)
        s("u16", [4, 128, T], BF16)
        s("qml", [8, 96, T], BF16); s("kml", [8, 96, T], BF16); s("vml", [T, 512], BF16)
        s("gat", [4, 8, 128, T], BF16)
        s("ymx", [4, 4, 128, T], BF16)
        s("ys5", [4, 128, T], F32)
        s("ebd", [8, 15, 64, 64], F32)
        s("out", [TL, D], F32, out=True)
        if self.debug:
            s("dbg_mod", [self.depth, 128, NMOD * 16], F32)

    def sb(self, es, name, shape, dt):
        self._uid = getattr(self, "_uid", 0) + 1
        return es.enter_context(self.nc.sbuf_tensor("%s_%d" % (name, self._uid), list(shape), dt))

    def ps(self, es, name, shape, dt=F32):
        self._uid = getattr(self, "_uid", 0) + 1
        return es.enter_context(self.nc.psum_tensor("%s_%d" % (name, self._uid), list(shape), dt))

    def phase_init(self):
        P, nc, T, TL = self.P, self.nc, self.T, self.TL
        with ExitStack() as es:
            ident = self.sb(es, "i_ident", [128, 128], F32)
            xin = [self.sb(es, "i_x%d" % i, [128, D], F32) for i in range(2)]
            xo = [self.sb(es, "i_o%d" % i, [128, 8, 128], F32) for i in range(2)]
            pst = [self.ps(es, "i_ps%d" % i, [128, 8, 128]) for i in range(2)]
            r_id = Res()
            r_in = Rot(2); r_o = Rot(2); r_ps = Rot(2)
            P.dma("sp", ident[:], self.dr["k_ident"][:, :], writes=[r_id])
            for tt in range(T // 128):
                t0 = tt * 128
                src = self.dr["x"][t0:t0 + 128, :] if t0 < TL else self.dr["ctx"][t0 - TL:t0 - TL + 128, :]
                i, ri = r_in.next(); j, ro = r_o.next(); p, rp = r_ps.next()
                P.dma("sp", xin[i][:], src, writes=[ri])
                for k in range(8):
                    P.op("pe", lambda e, i=i, p=p, k=k: e.transpose(out=pst[p][:, k, :], in_=xin[i][:, k * 128:(k + 1) * 128], identity=ident[:]),
                         reads=[ri, r_id], writes=[rp], chain=True)
                P.op("act", lambda e, j=j, p=p: e.activation(out=xo[j][:, 0:4, :], in_=pst[p][:, 0:4, :], func=AF.Copy), reads=[rp], writes=[ro])
                P.op("dve", lambda e, j=j, p=p: e.tensor_copy(out=xo[j][:, 4:8, :], in_=pst[p][:, 4:8, :]), reads=[rp], writes=[ro])
                P.dma("sp", self.dr["hT"][:, :, t0:t0 + 128].rearrange("k p t -> p k t"), xo[j][:], reads=[ro], writes=[self.rr["hT"]])

    def phase_mod(self, l):
        P, nc = self.P, self.nc
        mod, rmod = self.mod, self.rmod
        with ExitStack() as es:
            sc = self.sb(es, "m_sc", [128, 8, 2], F32)
            wt = [self.sb(es, "m_w%d" % i, [128, NMOD * D], F32) for i in range(2)]
            bt = self.sb(es, "m_b", [128, 72], F32)
            pm = self.ps(es, "m_ps", [128, 72, 2])
            rsc, rb, rpm = Res(), Res(), Res()
            rw = Rot(2)
            c0, rc0 = self.load_vec(es, "m_c0", self.dr["c"], 8)
            c1, rc1 = self.load_vec(es, "m_c1", self.dr["c_ctx"], 8)
            P.op("dve", lambda e: e.tensor_copy(out=sc[:, :, 0], in_=c0[:]), reads=[rc0], writes=[rsc])
            P.op("dve", lambda e: e.tensor_copy(out=sc[:, :, 1], in_=c1[:]), reads=[rc1], writes=[rsc])
            P.dma("sp", bt[:], self.dr["ada_b"][l].rearrange("(j p) -> p j", p=128), writes=[rb])
            P.op("act", lambda e: e.activation(out=sc[:], in_=sc[:], func=AF.Silu), reads=[rsc], writes=[rsc])
            wi = []
            for k in range(8):
                i, r = rw.next()
                P.dma("sp", wt[i][:], self.dr["ada_w"][l, k * 128:(k + 1) * 128, :], writes=[r])
                wi.append((i, r))
                for j in range(72):
                    P.op("pe", lambda e, i=i, j=j, k=k: e.matmul(pm[:, j, :], lhsT=wt[i][:, j * 128:(j + 1) * 128], rhs=sc[:, k, :], start=True, stop=True),
                         reads=[r, rsc], writes=[rpm], chain=True)
                if k == 0:
                    P.op("dve", lambda e: e.tensor_copy(out=self.macc[:], in_=pm[:]), reads=[rpm], writes=[self.rmacc])
                else:
                    P.op("dve", lambda e: e.tensor_tensor(out=self.macc[:], in0=pm[:], in1=self.macc[:], op=ALU.add), reads=[rpm, self.rmacc], writes=[self.rmacc])
            for s in range(2):
                P.op("dve", lambda e, s=s: e.tensor_tensor(out=mod[:, :, :, s], in0=self.macc[:, :, s].rearrange("p (m k) -> p m k", k=8),
                                                           in1=bt[:].rearrange("p (m k) -> p m k", k=8), op=ALU.add),
                     reads=[self.rmacc, rb], writes=[rmod])

    def load_vec(self, es, name, src_ap, nk):
        t = self.sb(es, name, [128, nk], F32)
        r = Res(name)
        self.P.dma("sp", t[:], src_ap.rearrange("(k p) -> p k", p=128), writes=[r])
        return t, r

    def make_AS(self, es, pref, normw, rn, mi_shift, mi_scale):
        P = self.P
        A = self.sb(es, pref + "_A", [128, 8, 2], F32)
        rA = Res()
        for s in range(2):
            P.op("dve", lambda e, s=s: e.scalar_tensor_tensor(out=A[:, :, s], in0=self.mod[:, mi_scale, :, s], scalar=1.0, in1=normw[:], op0=ALU.add, op1=ALU.mult),
                 reads=[self.rmod, rn], writes=[rA])
        return A, rA

    def norm_chunk(self, es_tiles, hc, rh, n, s, A, rA, mi_shift, nT, rnT):
        P = self.P
        sq, rsq, pss, rpss, rstd, rrstd, ones, rones, tmp, rtmp = es_tiles
        P.op("act", lambda e: e.activation(out=sq[:, :, :n], in_=hc[:, :, :n], func=AF.Square), reads=[rh], writes=[rsq])
        for k in range(8):
            P.op("pe", lambda e, k=k: e.matmul(pss[:, :n], lhsT=ones[:], rhs=sq[:, k, :n], start=(k == 0), stop=(k == 7)),
                 reads=[rsq, rones], writes=[rpss], chain=True)
        P.op("act", lambda e: e.activation(out=rstd[:, :n], in_=pss[:, :n], func=AF.Sqrt, bias=self.epsb[:, 0:1], scale=1.0 / D), reads=[rpss, self.reps], writes=[rrstd])
        P.op("dve", lambda e: e.reciprocal(out=rstd[:, :n], in_=rstd[:, :n]), reads=[rrstd], writes=[rrstd])
        for k in range(8):
            j, rt = rtmp.next()
            P.op("dve", lambda e, k=k, j=j: e.scalar_tensor_tensor(out=tmp[j][:, :n], in0=hc[:, k, :n], scalar=A[:, k, s:s + 1], in1=rstd[:, :n], op0=ALU.mult, op1=ALU.mult),
                 reads=[rh, rA, rrstd], writes=[rt])
            P.op("act", lambda e, k=k, j=j: e.activation(out=nT[:, k, :n], in_=tmp[j][:, :n], func=AF.Identity, bias=self.mod[:, mi_shift, k, s:s + 1], scale=1.0),
                 reads=[rt, self.rmod], writes=[rnT])

    def norm_tiles(self, es, pref):
        sq = self.sb(es, pref + "_sq", [128, 8, 512], BF16)
        pss = self.ps(es, pref + "_pss", [128, 512])
        rstd = self.sb(es, pref + "_rstd", [128, 512], F32)
        tmp = [self.sb(es, pref + "_tmp%d" % i, [128, 512], F32) for i in range(2)]
        return (sq, Res(), pss, Res(), rstd, Res(), self.ones, self.rones, tmp, Rot(2))

    def phase_ffn(self, l, which, mi0, with_ctx):
        P, nc = self.P, self.nc
        with ExitStack() as es:
            wg = self.sb(es, "f_wg", [128, 8, DFF], BF16)
            wu = self.sb(es, "f_wu", [128, 8, DFF], BF16)
            wd = self.sb(es, "f_wd", [128, 22, D], BF16)
            rwg, rwu, rwd = Res(), Res(), Res()
            for k in range(8):
                P.dma("pool", wg[:, k, :], self.dr[which + "_w_gate"][l, k * 128:(k + 1) * 128, :], writes=[rwg], accum=True)
                P.dma("pool", wu[:, k, :], self.dr[which + "_w_up"][l, k * 128:(k + 1) * 128, :], writes=[rwu], accum=True)
            for k in range(22):
                P.dma("pool", wd[:, k, :], self.dr[which + "_w_down"][l, k * 128:(k + 1) * 128, :], writes=[rwd], accum=True)
            normw, rn = self.load_vec(es, "f_nw", self.dr[which + "_norm"][l], 8)
            A, rA = self.make_AS(es, "f", normw, rn, mi0, mi0 + 1)
            G = self.sb(es, "f_G", [128, 8, 2], F32)
            rG = Res()
            P.op("dve", lambda e: e.tensor_scalar(out=G[:], in0=self.mod[:, mi0 + 2, :, :], scalar1=0.5, scalar2=None, op0=ALU.mult), reads=[self.rmod], writes=[rG])
            hc = self.sb(es, "f_h", [128, 8, 512], F32)
            rh = Res()
            nT = self.sb(es, "f_nT", [128, 8, 512], BF16)
            rnT = Res()
            hid = self.sb(es, "f_hid", [128, 22, 512], BF16)
            rhid = Res()
            sg = [self.sb(es, "f_sg%d" % i, [128, 512], F32) for i in range(2)]
            rsg = Rot(2)
            nt = self.norm_tiles(es, "f")
            pg = [self.ps(es, "f_pg%d" % i, [128, 512]) for i in range(2)]
            pu = [self.ps(es, "f_pu%d" % i, [128, 512]) for i in range(2)]
            pd = [self.ps(es, "f_pd%d" % i, [128, 512]) for i in range(2)]
            rpg, rpu, rpd = Rot(2), Rot(2), Rot(2)
            hT = self.dr["hT"]; rhT = self.rr["hT"]
            for (t0, n) in self.chunks(with_ctx):
                s = 0 if t0 < self.TL else 1
                P.dma("sp", hc[:, :, :n], hT[:, :, t0:t0 + n].rearrange("k p t -> p k t"), reads=[rhT], writes=[rh])
                self.norm_chunk(nt, hc, rh, n, s, A, rA, mi0, nT, rnT)
                for m in range(22):
                    ig, rg_ = rpg.next(); iu, ru_ = rpu.next(); isg, rs_ = rsg.next()
                    for k in range(8):
                        P.op("pe", lambda e, ig=ig, m=m, k=k: e.matmul(pg[ig][:, :n], lhsT=wg[:, k, m * 128:(m + 1) * 128], rhs=nT[:, k, :n], start=(k == 0), stop=(k == 7)),
                             reads=[rwg, rnT], writes=[rg_], chain=True)
                    for k in range(8):
                        P.op("pe", lambda e, iu=iu, m=m, k=k: e.matmul(pu[iu][:, :n], lhsT=wu[:, k, m * 128:(m + 1) * 128], rhs=nT[:, k, :n], start=(k == 0), stop=(k == 7)),
                             reads=[rwu, rnT], writes=[ru_], chain=True)
                    P.op("act", lambda e, ig=ig, isg=isg: e.activation(out=sg[isg][:, :n], in_=pg[ig][:, :n], func=AF.Silu), reads=[rg_], writes=[rs_])
                    P.op("dve", lambda e, iu=iu, isg=isg, m=m: e.tensor_tensor(out=hid[:, m, :n], in0=pu[iu][:, :n], in1=sg[isg][:, :n], op=ALU.mult),
                         reads=[ru_, rs_], writes=[rhid])
                for mo in range(8):
                    ip, rp_ = rpd.next()
                    for k in range(22):
                        P.op("pe", lambda e, ip=ip, mo=mo, k=k: e.matmul(pd[ip][:, :n], lhsT=wd[:, k, mo * 128:(mo + 1) * 128], rhs=hid[:, k, :n], start=(k == 0), stop=(k == 21)),
                             reads=[rwd, rhid], writes=[rp_], chain=True)
                    P.op("dve", lambda e, ip=ip, mo=mo, s=s, n=n: e.scalar_tensor_tensor(out=hc[:, mo, :n], in0=pd[ip][:, :n], scalar=G[:, mo, s:s + 1], in1=hc[:, mo, :n], op0=ALU.mult, op1=ALU.add),
                         reads=[rp_, rG, rh], writes=[rh])
                P.dma("sp", hT[:, :, t0:t0 + n].rearrange("k p t -> p k t"), hc[:, :, :n], reads=[rh], writes=[rhT])

    def phase_final(self):
        P, nc, TL = self.P, self.nc, self.TL
        with ExitStack() as es:
            fw, rfw = self.load_vec(es, "z_fw", self.dr["final_norm"], 8)
            ident = self.sb(es, "z_ident", [128, 128], F32)
            rid = Res()
            P.dma("sp", ident[:], self.dr["k_ident"][:, :], writes=[rid])
            hc = self.sb(es, "z_h", [128, 8, 512], F32)
            rh = Res()
            sq = self.sb(es, "z_sq", [128, 8, 512], BF16); rsq = Res()
            rstd = self.sb(es, "z_rstd", [128, 512], F32); rrs = Res()
            y = self.sb(es, "z_y", [128, 8, 512], F32); ry = Res()
            pss = self.ps(es, "z_pss", [128, 512]); rpss = Res()
            pt = [self.ps(es, "z_pt%d" % i, [128, 8, 128]) for i in range(2)]; rpt = Rot(2)
            o = [self.sb(es, "z_o%d" % i, [128, D], F32) for i in range(2)]; ro = Rot(2)
            hT = self.dr["hT"]; rhT = self.rr["hT"]
            for (t0, n) in self.chunks(False):
                P.dma("sp", hc[:, :, :n], hT[:, :, t0:t0 + n].rearrange("k p t -> p k t"), reads=[rhT], writes=[rh])
                P.op("act", lambda e: e.activation(out=sq[:, :, :n], in_=hc[:, :, :n], func=AF.Square), reads=[rh], writes=[rsq])
                for k in range(8):
                    P.op("pe", lambda e, k=k: e.matmul(pss[:, :n], lhsT=self.ones[:], rhs=sq[:, k, :n], start=(k == 0), stop=(k == 7)), reads=[rsq, self.rones], writes=[rpss], chain=True)
                P.op("act", lambda e: e.activation(out=rstd[:, :n], in_=pss[:, :n], func=AF.Sqrt, bias=self.epsb[:, 0:1], scale=1.0 / D), reads=[rpss, self.reps], writes=[rrs])
                P.op("dve", lambda e: e.reciprocal(out=rstd[:, :n], in_=rstd[:, :n]), reads=[rrs], writes=[rrs])
                for k in range(8):
                    P.op("dve", lambda e, k=k: e.scalar_tensor_tensor(out=y[:, k, :n], in0=hc[:, k, :n], scalar=fw[:, k:k + 1], in1=rstd[:, :n], op0=ALU.mult, op1=ALU.mult),
                         reads=[rh, rfw, rrs], writes=[ry])
                for tt in range(n // 128):
                    ip, rp_ = rpt.next(); io, ro_ = ro.next()
                    for k in range(8):
                        P.op("pe", lambda e, ip=ip, k=k, tt=tt: e.transpose(out=pt[ip][:, k, :], in_=y[:, k, tt * 128:(tt + 1) * 128], identity=ident[:]),
                             reads=[ry, rid], writes=[rp_], chain=True)
                    P.op("act", lambda e, ip=ip, io=io: e.activation(out=o[io][:, 0:512], in_=pt[ip][:, 0:4, :].rearrange("p k t -> p (k t)"), func=AF.Copy), reads=[rp_], writes=[ro_])
                    P.op("dve", lambda e, ip=ip, io=io: e.tensor_copy(out=o[io][:, 512:1024], in_=pt[ip][:, 4:8, :].rearrange("p k t -> p (k t)")), reads=[rp_], writes=[ro_])
                    P.dma("sp", self.dr["out"][t0 + tt * 128:t0 + (tt + 1) * 128, :], o[io][:], reads=[ro_], writes=[self.rr["out"]])

    def build(self):
        nc, P = self.nc, self.P
        self.declare()
        with ExitStack() as es:
            P.open()
            self.mod = self.sb(es, "g_mod", [128, NMOD, 8, 2], F32); self.rmod = Res()
            self.macc = self.sb(es, "g_macc", [128, 72, 2], F32); self.rmacc = Res()
            self.ones = self.sb(es, "g_ones", [128, 128], BF16); self.rones = Res()
            self.epsb = self.sb(es, "g_eps", [128, 1], F32); self.reps = Res()
            P.op("dve", lambda e: e.memset(self.ones[:], 1.0), writes=[self.rones])
            P.op("dve", lambda e: e.memset(self.epsb[:], EPS), writes=[self.reps])
            self.hpi = self.sb(es, "g_hpi", [128, 1], F32); self.rhpi = Res()
            P.op("dve", lambda e: e.memset(self.hpi[:], math.pi / 2), writes=[self.rhpi])
            ph = self.phases

            def on(name):
                return ph is None or name in ph
            if on("init"):
                self.phase_init()
                P.barrier()
            for l in range(self.nlayers):
                last = (l == self.depth - 1)
                if on("mod"):
                    self.phase_mod(l)
                    if self.debug:
                        P.dma("sp", self.dr["dbg_mod"][l], self.mod[:].rearrange("p a b c -> p (a b c)"), reads=[self.rmod], writes=[self.rr["dbg_mod"]])
                    P.barrier()
                if on("ffn1"):
                    self.phase_ffn(l, "ffn1", 0, True)
                    P.barrier()
                if on("mix"):
                    self.phase_mix(l, not last)
                    P.barrier()
                if on("ffn2"):
                    self.phase_ffn(l, "ffn2", 6, not last)
                    P.barrier()
            if on("final"):
                self.phase_final()
            P.wait_all("sp", list(self.rr.values()))
            P.emit()
            P.close()
        return nc

    def phase_mix(self, l, need_ctx):
        ph = self.phases

        def on(name):
            return ph is None or name in ph
        if on("proj"):
            self.phase_proj(l); self.P.barrier()
        if on("mla"):
            self.phase_mla(l, need_ctx); self.P.barrier()
        if on("swa"):
            self.phase_swa(l, need_ctx); self.P.barrier()
        if on("na"):
            self.phase_na(l, need_ctx); self.P.barrier()
        if on("s5"):
            self.phase_s5(l, need_ctx); self.P.barrier()
        if on("merge"):
            self.phase_merge(l, need_ctx)


def host_consts(TL):
    T = TL + CTX
    pos = np.arange(TL)
    rows = (pos // 64).astype(np.float32)
    cols = (pos % 64).astype(np.float32)

    def tab(dim):
        half = dim // 2
        nf = half // 2
        inv = (10000.0 ** (-np.arange(nf, dtype=np.float32) / nf)).astype(np.float32)
        cos = np.ones((dim, T), np.float32)
        sin = np.zeros((dim, T), np.float32)
        for dd in range(dim):
            p = rows if dd < half else cols
            j = dd % nf
            ang = (p * inv[j]).astype(np.float32)
            sign = -1.0 if (dd % half) < nf else 1.0
            cos[dd, :TL] = np.cos(ang)
            sin[dd, :TL] = sign * np.sin(ang)
        return cos, sin
    c64, s64 = tab(64)
    c32, s32 = tab(32)
    k = {}
    k["k_ident"] = np.eye(128, dtype=np.float32)
    k["k_cos64"] = np.concatenate([c64, c64], 0)
    k["k_sin64"] = np.concatenate([s64, s64], 0)
    k["k_cos64q"] = (k["k_cos64"] * np.float32(0.125)).astype(np.float32)
    k["k_sin64q"] = (k["k_sin64"] * np.float32(0.125)).astype(np.float32)
    sq = np.float32(96.0 ** -0.5)
    k["k_cos32q"] = np.concatenate([np.ones((64, T), np.float32), c32 * sq], 0).astype(np.float32)
    k["k_sin32q"] = np.concatenate([np.zeros((64, T), np.float32), s32 * sq], 0).astype(np.float32)
    k["k_cos32k"] = c32
    k["k_sin32k"] = s32
    jl = np.arange(128)[:, None]
    il = np.arange(128)[None, :]
    k["k_mprev"] = np.tile((jl >= il).astype(np.float32), (1, 4))
    k["k_mnext"] = np.tile((jl <= il).astype(np.float32), (1, 4))
    kc = np.arange(64)[:, None]
    qc = np.arange(64)[None, :]
    c0 = np.clip(qc - 8, 0, 48)
    win = ((kc >= c0) & (kc < c0 + 16))
    G = np.zeros((31, 64, 64), np.float32)
    for dc in range(31):
        G[dc] = ((kc - qc + 15) == dc) & win
    k["k_G"] = G.reshape(31, 4096)
    k["k_cmask"] = np.tile(win.astype(np.float32).reshape(1, 4096), (15, 1))
    return k


def perm_swap(n_heads, dim):
    half = dim // 2
    nf = half // 2
    idx = np.arange(n_heads * dim)
    out = idx.copy()
    for h in range(n_heads):
        for dd in range(dim):
            partner = dd + nf if (dd % half) < nf else dd - nf
            out[h * dim + dd] = h * dim + partner
    return out


def host_layout(inputs, b, TL):
    m = {}
    m["x"] = np.ascontiguousarray(inputs["x"][b, :TL])
    m["c"] = np.ascontiguousarray(inputs["c"][b])
    m["ctx"] = np.ascontiguousarray(inputs["ctx"][b])
    for n in ("c_ctx", "ada_w", "ada_b", "ffn1_norm", "ffn1_w_gate", "ffn1_w_up", "ffn1_w_down", "mix_norm", "w_in",
              "na_rpb", "swa_sink", "s5_lambda_re", "s5_lambda_im", "s5_log_dt", "s5_b_re", "s5_b_im", "s5_c_re", "s5_c_im",
              "s5_d", "s5_glu_w", "s5_glu_b", "mla_q_norm", "mla_w_uq", "mla_kv_norm", "mla_w_ukv", "w_branch", "w_out",
              "ffn2_norm", "ffn2_w_gate", "ffn2_w_up", "ffn2_w_down", "final_norm"):
        m[n] = np.ascontiguousarray(inputs[n])
    w_in = inputs["w_in"]
    p64q = perm_swap(8, 64)
    p64k = perm_swap(2, 64)
    p32 = perm_swap(1, 32)
    m["w_in_sw"] = np.ascontiguousarray(np.concatenate([w_in[:, :, 1536:2048][:, :, p64q], w_in[:, :, 2048:2176][:, :, p64k],
                                                        w_in[:, :, 3200:3232][:, :, p32]], axis=2))
    wuq = inputs["mla_w_uq"]
    pq = np.arange(768)
    for h in range(8):
        pq[h * 96 + 64:h * 96 + 96] = h * 96 + 64 + p32
    m["mla_w_uq_sw"] = np.ascontiguousarray(wuq[:, :, pq])
    return m


_CACHE = {}


def kernel(**inputs):
    TL = inputs["x"].shape[1]
    depth = inputs["ada_w"].shape[0]
    B = inputs["x"].shape[0]
    inputs = {k: np.asarray(v) for k, v in inputs.items()}
    kb = K(TL, depth)
    nc = kb.build()
    consts = host_consts(TL)
    in_maps = []
    for core in range(8):
        m = host_layout(inputs, core % B, TL)
        m.update(consts)
        in_maps.append(m)
    res = run_bass_kernel_spmd(nc, in_maps, core_ids=list(range(8)))
    out = np.stack([res.results[b]["out"] for b in range(B)], axis=0)
    return out.astype(np.float32)


L = 256
CH = 256


def phase_s5(self, l, need_ctx):
    P, T, TL = self.P, self.T, self.TL
    dr, rr = self.dr, self.rr
    NCH = TL // L
    with ExitStack() as es:
        ident = self.sb(es, "s_id", [128, 128], F32); rid = Res()
        P.dma("sp", ident[:], dr["k_ident"][:, :], writes=[rid])
        dvec, rdv = self.load_vec(es, "s_d", dr["s5_d"][l], 4)
        uT = self.sb(es, "s_u", [128, T], BF16); ru = Res()
        names = ["lr", "li", "ldt", "dt", "a", "th", "r", "c", "s", "cc", "ss", "cs", "nr", "ni", "den", "kr", "ki", "t0", "t1"]
        pr = {nm: self.sb(es, "s_p_" + nm, [128, 4], F32) for nm in names}
        rp = Res()
        wc = self.sb(es, "s_wc", [128, 9, 4], F32); ws = self.sb(es, "s_ws", [128, 9, 4], F32)
        BDr = self.sb(es, "s_BDr", [128, 4, 128], F32); BDi = self.sb(es, "s_BDi", [128, 4, 128], F32); rBD = Res()
        CDr = self.sb(es, "s_CDr", [128, 4, 128], F32); CDi = self.sb(es, "s_CDi", [128, 4, 128], F32); rCD = Res()
        BBr = self.sb(es, "s_BBr", [128, 4, 128], F32); BBi = self.sb(es, "s_BBi", [128, 4, 128], F32); rBB = Res()
        tB = self.sb(es, "s_tB", [128, 128], F32); rtB = Res()
        WBr = self.sb(es, "s_WBr", [128, 4, 128], BF16); WBi = self.sb(es, "s_WBi", [128, 4, 128], BF16)
        WCr = self.sb(es, "s_WCr", [128, 4, 128], BF16); WCi = self.sb(es, "s_WCi", [128, 4, 128], BF16); rWt = Res()
        cE = self.sb(es, "s_cE", [128, 4, L], F32); sE = self.sb(es, "s_sE", [128, 4, L], F32); rE = Res()
        tE = self.sb(es, "s_tE", [128, L], F32); rtE = Res()
        init_re = self.sb(es, "s_ire", [128, 4], F32); init_im = self.sb(es, "s_iim", [128, 4], F32); rinit = Res()
        tI = self.sb(es, "s_tI", [128, 2], F32); rtI = Res()
        tt = [[self.sb(es, "s_t%d_%d" % (j, i), [128, L], F32) for i in range(2)] for j in range(4)]
        rtt = [Rot(2) for j in range(4)]
        vv = [[self.sb(es, "s_v%d_%d" % (j, i), [128, L], F32) for i in range(2)] for j in range(2)]
        rvv = [Rot(2) for j in range(2)]
        gg = [[self.sb(es, "s_g%d_%d" % (j, i), [128, L], F32) for i in range(2)] for j in range(2)]
        rgg = [Rot(2) for j in range(2)]
        mm_ = [[self.sb(es, "s_m%d_%d" % (j, i), [128, L], F32) for i in range(2)] for j in range(4)]
        rmm = [Rot(2) for j in range(4)]
        hh = [[self.sb(es, "s_h%d_%d" % (j, i), [128, L], BF16) for i in range(2)] for j in range(2)]
        rhh = [Rot(2) for j in range(2)]
        ych = [self.sb(es, "s_y%d" % i, [128, L], F32) for i in range(2)]; rych = Rot(2)
        yprev = [self.sb(es, "s_yp%d" % i, [128, L], F32) for i in range(2)]; ryp = Rot(2)
        pbu = [self.ps(es, "s_pbu%d" % i, [128, 2, L]) for i in range(2)]; rpbu = Rot(2)
        py = [self.ps(es, "s_py%d" % i, [128, 512]) for i in range(2)]; rpy = Rot(2)
        ptr = [self.ps(es, "s_ptr%d" % i, [128, 512]) for i in range(2)]; rptr = Rot(2)

        P.op("pool", lambda e: e.memset(BDr[:], 0.0), writes=[rBD]); P.op("pool", lambda e: e.memset(BDi[:], 0.0), writes=[rBD])
        P.op("pool", lambda e: e.memset(CDr[:], 0.0), writes=[rCD]); P.op("pool", lambda e: e.memset(CDi[:], 0.0), writes=[rCD])

        def tt_op(eng, out, a, b, op, reads, writes):
            P.op(eng, lambda e: e.tensor_tensor(out=out, in0=a, in1=b, op=op), reads=reads, writes=writes)

        for d in range(2):
            for fc in range(4):
                for ti in range(4):
                    for gl in range(2):
                        g = (fc * 4 + ti) * 2 + gl
                        ps_ = slice(gl * 64, gl * 64 + 64)
                        P.dma("sp", pr["lr"][ps_, ti:ti + 1], dr["s5_lambda_re"][l, d, g, :].rearrange("(p o) -> p o", o=1), writes=[rp], accum=True)
                        P.dma("sp", pr["li"][ps_, ti:ti + 1], dr["s5_lambda_im"][l, d, g, :].rearrange("(p o) -> p o", o=1), writes=[rp], accum=True)
                        P.dma("sp", pr["ldt"][ps_, ti:ti + 1], dr["s5_log_dt"][l, d, g:g + 1].partition_broadcast(64), writes=[rp], accum=True)
                        fo = (ti * 2 + gl) * 16
                        P.dma("sp", BDr[ps_, ti, fo:fo + 16], dr["s5_b_re"][l, d, g], writes=[rBD], accum=True)
                        P.dma("sp", BDi[ps_, ti, fo:fo + 16], dr["s5_b_im"][l, d, g], writes=[rBD], accum=True)
                        P.dma("sp", CDr[fo:fo + 16, ti, ps_], dr["s5_c_re"][l, d, g], writes=[rCD], accum=True)
                        P.dma("sp", CDi[fo:fo + 16, ti, ps_], dr["s5_c_im"][l, d, g], writes=[rCD], accum=True)
                R_ = [rp]
                p = pr
                P.op("act", lambda e: e.activation(out=p["dt"][:], in_=p["ldt"][:], func=AF.Exp), reads=R_, writes=R_)
                tt_op("dve", p["a"][:], p["lr"][:], p["dt"][:], ALU.mult, R_, R_)
                tt_op("dve", p["th"][:], p["li"][:], p["dt"][:], ALU.mult, R_, R_)
                P.op("act", lambda e: e.activation(out=p["r"][:], in_=p["a"][:], func=AF.Exp), reads=R_, writes=R_)
                P.op("act", lambda e: e.activation(out=p["s"][:], in_=p["th"][:], func=AF.Sin, scale=1.0 / 32), reads=R_, writes=R_)
                P.op("act", lambda e: e.activation(out=p["c"][:], in_=p["th"][:], func=AF.Sin, scale=1.0 / 32, bias=self.hpi[:, 0:1]), reads=R_ + [self.rhpi], writes=R_)
                for _ in range(5):
                    tt_op("dve", p["cc"][:], p["c"][:], p["c"][:], ALU.mult, R_, R_)
                    tt_op("dve", p["ss"][:], p["s"][:], p["s"][:], ALU.mult, R_, R_)
                    tt_op("dve", p["cs"][:], p["c"][:], p["s"][:], ALU.mult, R_, R_)
                    tt_op("dve", p["c"][:], p["cc"][:], p["ss"][:], ALU.subtract, R_, R_)
                    P.op("dve", lambda e: e.tensor_scalar(out=p["s"][:], in0=p["cs"][:], scalar1=2.0, scalar2=None, op0=ALU.mult), reads=R_, writes=R_)
                P.op("dve", lambda e: e.tensor_copy(out=wc[:, 0, :], in_=p["c"][:]), reads=R_, writes=R_)
                P.op("dve", lambda e: e.tensor_copy(out=ws[:, 0, :], in_=p["s"][:]), reads=R_, writes=R_)
                for k in range(8):
                    tt_op("dve", p["cc"][:], wc[:, k, :], wc[:, k, :], ALU.mult, R_, R_)
                    tt_op("dve", p["ss"][:], ws[:, k, :], ws[:, k, :], ALU.mult, R_, R_)
                    tt_op("dve", p["cs"][:], wc[:, k, :], ws[:, k, :], ALU.mult, R_, R_)
                    tt_op("dve", wc[:, k + 1, :], p["cc"][:], p["ss"][:], ALU.subtract, R_, R_)
                    P.op("dve", lambda e, k=k: e.tensor_scalar(out=ws[:, k + 1, :], in0=p["cs"][:], scalar1=2.0, scalar2=None, op0=ALU.mult), reads=R_, writes=R_)
                tt_op("dve", p["nr"][:], p["r"][:], p["c"][:], ALU.mult, R_, R_)
                P.op("dve", lambda e: e.tensor_scalar(out=p["nr"][:], in0=p["nr"][:], scalar1=-1.0, scalar2=None, op0=ALU.add), reads=R_, writes=R_)
                tt_op("dve", p["ni"][:], p["r"][:], p["s"][:], ALU.mult, R_, R_)
                tt_op("dve", p["cc"][:], p["lr"][:], p["lr"][:], ALU.mult, R_, R_)
                tt_op("dve", p["ss"][:], p["li"][:], p["li"][:], ALU.mult, R_, R_)
                tt_op("dve", p["den"][:], p["cc"][:], p["ss"][:], ALU.add, R_, R_)
                P.op("dve", lambda e: e.reciprocal(out=p["den"][:], in_=p["den"][:]), reads=R_, writes=R_)
                tt_op("dve", p["t0"][:], p["nr"][:], p["lr"][:], ALU.mult, R_, R_)
                tt_op("dve", p["t1"][:], p["ni"][:], p["li"][:], ALU.mult, R_, R_)
                tt_op("dve", p["kr"][:], p["t0"][:], p["t1"][:], ALU.add, R_, R_)
                tt_op("dve", p["kr"][:], p["kr"][:], p["den"][:], ALU.mult, R_, R_)
                tt_op("dve", p["t0"][:], p["ni"][:], p["lr"][:], ALU.mult, R_, R_)
                tt_op("dve", p["t1"][:], p["nr"][:], p["li"][:], ALU.mult, R_, R_)
                tt_op("dve", p["ki"][:], p["t0"][:], p["t1"][:], ALU.subtract, R_, R_)
                tt_op("dve", p["ki"][:], p["ki"][:], p["den"][:], ALU.mult, R_, R_)
                for ti in range(4):
                    P.op("dve", lambda e, ti=ti: e.tensor_scalar(out=tB[:], in0=BDi[:, ti, :], scalar1=p["ki"][:, ti:ti + 1], scalar2=None, op0=ALU.mult), reads=[rBD, rp], writes=[rtB])
                    P.op("dve", lambda e, ti=ti: e.scalar_tensor_tensor(out=BBr[:, ti, :], in0=BDr[:, ti, :], scalar=p["kr"][:, ti:ti + 1], in1=tB[:], op0=ALU.mult, op1=ALU.subtract),
                         reads=[rBD, rp, rtB], writes=[rBB])
                    P.op("dve", lambda e, ti=ti: e.tensor_scalar(out=tB[:], in0=BDr[:, ti, :], scalar1=p["ki"][:, ti:ti + 1], scalar2=None, op0=ALU.mult), reads=[rBD, rp, rBB], writes=[rtB])
                    P.op("dve", lambda e, ti=ti: e.scalar_tensor_tensor(out=BBi[:, ti, :], in0=BDi[:, ti, :], scalar=p["kr"][:, ti:ti + 1], in1=tB[:], op0=ALU.mult, op1=ALU.add),
                         reads=[rBD, rp, rtB], writes=[rBB])
                for ti in range(4):
                    for (src, rsrc, dst, sc) in ((BBr, rBB, WBr, 1.0), (BBi, rBB, WBi, 1.0), (CDr, rCD, WCr, 1.0), (CDi, rCD, WCi, -1.0)):
                        ip, rpt = rptr.next()
                        P.op("pe", lambda e, ip=ip, src=src, ti=ti: e.transpose(out=ptr[ip][:, 0:128], in_=src[:, ti, :], identity=ident[:]), reads=[rsrc, rid], writes=[rpt], chain=True)
                        P.op("act", lambda e, ip=ip, dst=dst, ti=ti, sc=sc: e.activation(out=dst[:, ti, :], in_=ptr[ip][:, 0:128], func=AF.Copy, scale=sc), reads=[rpt], writes=[rWt])
                P.op("pool", lambda e: e.memset(cE[:, :, 0:1], 1.0), writes=[rE])
                P.op("pool", lambda e: e.memset(sE[:, :, 0:1], 0.0), writes=[rE])
                for k in range(8):
                    m = 1 << k
                    for ti in range(4):
                        P.op("dve", lambda e, ti=ti, m=m, k=k: e.tensor_scalar(out=tE[:, 0:m], in0=sE[:, ti, 0:m], scalar1=ws[:, k, ti:ti + 1], scalar2=None, op0=ALU.mult), reads=[rE, rp], writes=[rtE])
                        P.op("dve", lambda e, ti=ti, m=m, k=k: e.scalar_tensor_tensor(out=cE[:, ti, m:2 * m], in0=cE[:, ti, 0:m], scalar=wc[:, k, ti:ti + 1], in1=tE[:, 0:m], op0=ALU.mult, op1=ALU.subtract),
                             reads=[rE, rp, rtE], writes=[rE])
                        P.op("dve", lambda e, ti=ti, m=m, k=k: e.tensor_scalar(out=tE[:, 0:m], in0=sE[:, ti, 0:m], scalar1=wc[:, k, ti:ti + 1], scalar2=None, op0=ALU.mult), reads=[rE, rp], writes=[rtE])
                        P.op("dve", lambda e, ti=ti, m=m, k=k: e.scalar_tensor_tensor(out=sE[:, ti, m:2 * m], in0=cE[:, ti, 0:m], scalar=ws[:, k, ti:ti + 1], in1=tE[:, 0:m], op0=ALU.mult, op1=ALU.add),
                             reads=[rE, rp, rtE], writes=[rE])
                P.dma("sp", uT[:], dr["u16"][fc], reads=[rr["u16"]], writes=[ru])
                P.op("dve", lambda e: e.memset(init_re[:], 0.0), writes=[rinit])
                P.op("dve", lambda e: e.memset(init_im[:], 0.0), writes=[rinit])
                order = [NCH] + (list(range(NCH)) if d == 0 else list(range(NCH - 1, -1, -1)))
                for ci in order:
                    t0 = ci * L
                    iy, ry = rpy.next()
                    for ti in range(4):
                        ib, rb = rpbu.next()
                        P.op("pe", lambda e, ib=ib, ti=ti, t0=t0: e.matmul(pbu[ib][:, 0, :], lhsT=WBr[:, ti, :], rhs=uT[:, t0:t0 + L], start=True, stop=True), reads=[rWt, ru], writes=[rb], chain=True)
                        P.op("pe", lambda e, ib=ib, ti=ti, t0=t0: e.matmul(pbu[ib][:, 1, :], lhsT=WBi[:, ti, :], rhs=uT[:, t0:t0 + L], start=True, stop=True), reads=[rWt, ru], writes=[rb], chain=True)
                        if d == 0:
                            cv, sv = cE[:, ti, :], sE[:, ti, :]
                        else:
                            cv, sv = cE[:, ti, ::-1], sE[:, ti, ::-1]
                        bre, bim = pbu[ib][:, 0, :], pbu[ib][:, 1, :]
                        ids = [rtt[j].next() for j in range(4)]
                        tl = [tt[j][ids[j][0]] for j in range(4)]
                        rl = [ids[j][1] for j in range(4)]
                        tt_op("dve", tl[0][:], bre, cv, ALU.mult, [rb, rE], [rl[0]])
                        tt_op("dve", tl[1][:], bim, sv, ALU.mult, [rb, rE], [rl[1]])
                        tt_op("dve", tl[2][:], bim, cv, ALU.mult, [rb, rE], [rl[2]])
                        tt_op("dve", tl[3][:], bre, sv, ALU.mult, [rb, rE], [rl[3]])
                        (i0, rv0), (i1, rv1) = rvv[0].next(), rvv[1].next()
                        vre, vim = vv[0][i0], vv[1][i1]
                        tt_op("pool", vre[:], tl[0][:], tl[1][:], ALU.add, [rl[0], rl[1]], [rv0])
                        tt_op("pool", vim[:], tl[2][:], tl[3][:], ALU.subtract, [rl[2], rl[3]], [rv1])
                        (j0, rg0), (j1, rg1) = rgg[0].next(), rgg[1].next()
                        gre, gim = gg[0][j0], gg[1][j1]
                        rbc = pr["r"][:, ti:ti + 1].to_broadcast([128, L])
                        if d == 0:
                            P.op("dve", lambda e, gre=gre, vre=vre, ti=ti, rbc=rbc: e.tensor_tensor_scan(out=gre[:], data0=rbc, data1=vre[:], initial=init_re[:, ti:ti + 1], op0=ALU.mult, op1=ALU.add),
                                 reads=[rv0, rp, rinit], writes=[rg0])
                            P.op("dve", lambda e, gim=gim, vim=vim, ti=ti, rbc=rbc: e.tensor_tensor_scan(out=gim[:], data0=rbc, data1=vim[:], initial=init_im[:, ti:ti + 1], op0=ALU.mult, op1=ALU.add),
                                 reads=[rv1, rp, rinit], writes=[rg1])
                            last = L - 1
                        else:
                            P.op("dve", lambda e, gre=gre, vre=vre, ti=ti, rbc=rbc: e.tensor_tensor_scan(out=gre[:, ::-1], data0=rbc, data1=vre[:, ::-1], initial=init_re[:, ti:ti + 1], op0=ALU.mult, op1=ALU.add),
                                 reads=[rv0, rp, rinit], writes=[rg0])
                            P.op("dve", lambda e, gim=gim, vim=vim, ti=ti, rbc=rbc: e.tensor_tensor_scan(out=gim[:, ::-1], data0=rbc, data1=vim[:, ::-1], initial=init_im[:, ti:ti + 1], op0=ALU.mult, op1=ALU.add),
                                 reads=[rv1, rp, rinit], writes=[rg1])
                            last = 0
                        P.op("dve", lambda e, gim=gim, ti=ti, last=last: e.tensor_scalar(out=tI[:, 0:1], in0=gim[:, last:last + 1], scalar1=ws[:, 8, ti:ti + 1], scalar2=None, op0=ALU.mult), reads=[rg1, rp], writes=[rtI])
                        P.op("dve", lambda e, gim=gim, ti=ti, last=last: e.tensor_scalar(out=tI[:, 1:2], in0=gim[:, last:last + 1], scalar1=wc[:, 8, ti:ti + 1], scalar2=None, op0=ALU.mult), reads=[rg1, rp], writes=[rtI])
                        P.op("dve", lambda e, gre=gre, ti=ti, last=last: e.scalar_tensor_tensor(out=init_re[:, ti:ti + 1], in0=gre[:, last:last + 1], scalar=wc[:, 8, ti:ti + 1], in1=tI[:, 0:1], op0=ALU.mult, op1=ALU.subtract),
                             reads=[rg0, rp, rtI], writes=[rinit])
                        P.op("dve", lambda e, gre=gre, ti=ti, last=last: e.scalar_tensor_tensor(out=init_im[:, ti:ti + 1], in0=gre[:, last:last + 1], scalar=ws[:, 8, ti:ti + 1], in1=tI[:, 1:2], op0=ALU.mult, op1=ALU.add),
                             reads=[rg0, rp, rtI], writes=[rinit])
                        mids = [rmm[j].next() for j in range(4)]
                        ml = [mm_[j][mids[j][0]] for j in range(4)]
                        rml = [mids[j][1] for j in range(4)]
                        tt_op("pool", ml[0][:], gre[:], cv, ALU.mult, [rg0, rE], [rml[0]])
                        tt_op("pool", ml[1][:], gim[:], sv, ALU.mult, [rg1, rE], [rml[1]])
                        tt_op("pool", ml[2][:], gre[:], sv, ALU.mult, [rg0, rE], [rml[2]])
                        tt_op("pool", ml[3][:], gim[:], cv, ALU.mult, [rg1, rE], [rml[3]])
                        (k0, rh0), (k1, rh1) = rhh[0].next(), rhh[1].next()
                        hre, him = hh[0][k0], hh[1][k1]
                        tt_op("dve", hre[:], ml[0][:], ml[1][:], ALU.subtract, [rml[0], rml[1]], [rh0])
                        tt_op("dve", him[:], ml[2][:], ml[3][:], ALU.add, [rml[2], rml[3]], [rh1])
                        P.op("pe", lambda e, iy=iy, ti=ti, hre=hre: e.matmul(py[iy][:, :L], lhsT=WCr[:, ti, :], rhs=hre[:], start=(ti == 0), stop=False), reads=[rWt, rh0], writes=[ry], chain=True)
                        P.op("pe", lambda e, iy=iy, ti=ti, him=him: e.matmul(py[iy][:, :L], lhsT=WCi[:, ti, :], rhs=him[:], start=False, stop=(ti == 3)), reads=[rWt, rh1], writes=[ry], chain=True)
                    io, ro = rych.next()
                    if d == 0:
                        P.op("dve", lambda e, io=io, iy=iy, t0=t0, fc=fc: e.scalar_tensor_tensor(out=ych[io][:], in0=uT[:, t0:t0 + L], scalar=dvec[:, fc:fc + 1], in1=py[iy][:, :L], op0=ALU.mult, op1=ALU.add),
                             reads=[ru, rdv, ry], writes=[ro])
                    else:
                        ipv, rpv = ryp.next()
                        P.dma("sp", yprev[ipv][:], dr["ys5"][fc, :, t0:t0 + L], reads=[rr["ys5"]], writes=[rpv])
                        tt_op("dve", ych[io][:], py[iy][:, :L], yprev[ipv][:], ALU.add, [ry, rpv], [ro])
                    P.dma("sp", dr["ys5"][fc, :, t0:t0 + L], ych[io][:], reads=[ro], writes=[rr["ys5"]])
    self.P.barrier()
    with ExitStack() as es:
        Wg = self.sb(es, "sg_w", [128, 4, 512], BF16); rWg = Res()
        for k in range(4):
            P.dma("pool", Wg[:, k, :], dr["s5_glu_w"][l, k * 128:(k + 1) * 128, :], writes=[rWg], accum=True)
        gb, rgb = self.load_vec(es, "sg_b", dr["s5_glu_b"][l], 4)
        y = self.sb(es, "sg_y", [128, 4, L], F32); ry_ = Res()
        x2 = self.sb(es, "sg_x2", [128, 4, L], F32); rx2 = Res()
        g32 = self.sb(es, "sg_g32", [128, 4, L], F32); rg32 = Res()
        g16 = self.sb(es, "sg_g16", [128, 4, L], BF16); rg16 = Res()
        sz = [self.sb(es, "sg_sz%d" % i, [128, L], F32) for i in range(2)]; rsz = Rot(2)
        st = [self.sb(es, "sg_st%d" % i, [128, L], BF16) for i in range(2)]; rst = Rot(2)
        pz = [self.ps(es, "sg_pz%d" % i, [128, 512]) for i in range(2)]; rpz = Rot(2)
        for (t0, n) in self.chunks(need_ctx):
            P.dma("sp", y[:, :, :n], dr["ys5"][:, :, t0:t0 + n].rearrange("k p t -> p k t"), reads=[rr["ys5"]], writes=[ry_])
            P.op("dve", lambda e: e.tensor_tensor(out=x2[:], in0=y[:], in1=y[:], op=ALU.mult), reads=[ry_], writes=[rx2])
            P.op("dve", lambda e: e.tensor_scalar(out=x2[:], in0=x2[:], scalar1=0.044715, scalar2=1.0, op0=ALU.mult, op1=ALU.add), reads=[rx2], writes=[rx2])
            P.op("dve", lambda e: e.tensor_tensor(out=x2[:], in0=x2[:], in1=y[:], op=ALU.mult), reads=[rx2, ry_], writes=[rx2])
            P.op("act", lambda e: e.activation(out=x2[:], in_=x2[:], func=AF.Sigmoid, scale=2.0 * math.sqrt(2.0 / math.pi)), reads=[rx2], writes=[rx2])
            P.op("dve", lambda e: e.tensor_tensor(out=g32[:], in0=x2[:], in1=y[:], op=ALU.mult), reads=[rx2, ry_], writes=[rg32])
            P.op("act", lambda e: e.activation(out=g16[:], in_=g32[:], func=AF.Copy), reads=[rg32], writes=[rg16])
            for m in range(4):
                ip, rp_ = rpz.next(); isz, rs_ = rsz.next(); ist, rt_ = rst.next()
                for k in range(4):
                    P.op("pe", lambda e, ip=ip, m=m, k=k: e.matmul(pz[ip][:, :n], lhsT=Wg[:, k, m * 128:(m + 1) * 128], rhs=g16[:, k, :n], start=(k == 0), stop=(k == 3)),
                         reads=[rWg, rg16], writes=[rp_], chain=True)
                P.op("act", lambda e, ip=ip, isz=isz, m=m: e.activation(out=sz[isz][:, :n], in_=pz[ip][:, :n], func=AF.Sigmoid, bias=gb[:, m:m + 1], scale=1.0), reads=[rp_, rgb], writes=[rs_])
                P.op("dve", lambda e, isz=isz, ist=ist, m=m: e.tensor_tensor(out=st[ist][:, :n], in0=sz[isz][:, :n], in1=g32[:, m, :n], op=ALU.mult), reads=[rs_, rg32], writes=[rt_])
                P.dma("sp", dr["ymx"][2, m, :, t0:t0 + n], st[ist][:, :n], reads=[rt_], writes=[rr["ymx"]])


def install(K):

    def phase_proj(self, l):
        P, nc, T, TL = self.P, self.nc, self.T, self.TL
        dr, rr = self.dr, self.rr
        with ExitStack() as es:
            W = self.sb(es, "p_w", [128, 8, 7328], BF16); rW = Res()
            Wsw = self.sb(es, "p_wsw", [128, 8, 672], BF16); rWsw = Res()
            for k in range(8):
                P.dma("pool", W[:, k, :], dr["w_in"][l, k * 128:(k + 1) * 128, :], writes=[rW], accum=True)
                P.dma("pool", Wsw[:, k, :], dr["w_in_sw"][l, k * 128:(k + 1) * 128, :], writes=[rWsw], accum=True)
            wuq = self.sb(es, "p_wuq", [128, 2, 768], BF16); wuqs = self.sb(es, "p_wuqs", [128, 2, 768], BF16); rwuq = Res()
            for j in range(2):
                P.dma("pool", wuq[:, j, :], dr["mla_w_uq"][l, j * 128:(j + 1) * 128, :], writes=[rwuq], accum=True)
                P.dma("pool", wuqs[:, j, :], dr["mla_w_uq_sw"][l, j * 128:(j + 1) * 128, :], writes=[rwuq], accum=True)
            wukv = self.sb(es, "p_wukv", [128, 1024], BF16); rwukv = Res()
            P.dma("pool", wukv[:], dr["mla_w_ukv"][l], writes=[rwukv])
            normw, rn = self.load_vec(es, "p_nw", dr["mix_norm"][l], 8)
            qn, rqn = self.load_vec(es, "p_qn", dr["mla_q_norm"][l], 2)
            kvn, rkvn = self.load_vec(es, "p_kvn", dr["mla_kv_norm"][l], 1)
            A, rA = self.make_AS(es, "p", normw, rn, 3, 4)
            hc = self.sb(es, "p_h", [128, 8, 512], F32); rh = Res()
            nT = self.sb(es, "p_nT", [128, 8, 512], BF16); rnT = Res()
            nt = self.norm_tiles(es, "p")
            (sq, rsq, pss, rpss, rstd, rrstd, ones, rones, tmpn, rtmpn) = nt
            c64 = self.sb(es, "p_c64", [128, CH], F32); s64 = self.sb(es, "p_s64", [128, CH], F32)
            c64q = self.sb(es, "p_c64q", [128, CH], F32); s64q = self.sb(es, "p_s64q", [128, CH], F32)
            c32q = self.sb(es, "p_c32q", [96, CH], F32); s32q = self.sb(es, "p_s32q", [96, CH], F32)
            c32k = self.sb(es, "p_c32k", [32, CH], F32); s32k = self.sb(es, "p_s32k", [32, CH], F32)
            rtab = Res()
            cq = self.sb(es, "p_cq", [128, 2, CH], F32); rcq = Res()
            cqn = self.sb(es, "p_cqn", [128, 2, CH], BF16); rcqn = Res()
            ckv = self.sb(es, "p_ckv", [128, CH], F32); rckv = Res()
            ckvn = self.sb(es, "p_ckvn", [128, CH], BF16); rckvn = Res()
            NST = 6
            stg = [self.sb(es, "p_st%d" % i, [128, 512], BF16) for i in range(NST)]; rstg = Rot(NST)
            t1 = [self.sb(es, "p_t1%d" % i, [128, CH], F32) for i in range(2)]; rt1 = Rot(2)
            t2 = [self.sb(es, "p_t2%d" % i, [128, CH], F32) for i in range(2)]; rt2 = Rot(2)
            NPS = 6
            pp = [self.ps(es, "p_ps%d" % i, [128, 512]) for i in range(NPS)]; rpp = Rot(NPS)
            tog = [0]

            def mm(ps_ap, terms, rps, reads):
                nterm = len(terms)
                for i, (lt, rh_) in enumerate(terms):
                    P.op("pe", lambda e, lt=lt, rh_=rh_, i=i: e.matmul(ps_ap, lhsT=lt, rhs=rh_, start=(i == 0), stop=(i == nterm - 1)),
                         reads=reads, writes=[rps], chain=True)

            def evac(out_ap, in_ap, rin, rout, scale=1.0, func=None):
                tog[0] ^= 1
                if func is not None or tog[0]:
                    P.op("act", lambda e: e.activation(out=out_ap, in_=in_ap, func=(func or AF.Copy), scale=scale), reads=[rin], writes=[rout])
                else:
                    P.op("dve", lambda e: e.tensor_scalar(out=out_ap, in0=in_ap, scalar1=scale, scalar2=None, op0=ALU.mult), reads=[rin], writes=[rout])

            for (t0, n) in self.chunks(True):
                s = 0 if t0 < TL else 1
                P.dma("sp", hc[:, :, :n], dr["hT"][:, :, t0:t0 + n].rearrange("k p t -> p k t"), reads=[rr["hT"]], writes=[rh])
                for (tt, nm) in ((c64, "k_cos64"), (s64, "k_sin64"), (c64q, "k_cos64q"), (s64q, "k_sin64q"), (c32q, "k_cos32q"), (s32q, "k_sin32q"),
                                 (c32k, "k_cos32k"), (s32k, "k_sin32k")):
                    P.dma("sp", tt[:, :n], dr[nm][:, t0:t0 + n], writes=[rtab], accum=True)
                self.norm_chunk(nt, hc, rh, n, s, A, rA, 3, nT, rnT)

                def fm(col0, M, Wt=W, rWt=rW):
                    ip, rp = rpp.next()
                    mm(pp[ip][:M, :n], [(Wt[:, k, col0:col0 + M], nT[:, k, :n]) for k in range(8)], rp, [rWt, rnT])
                    return pp[ip], rp

                def out_fm(dst_ap, rdst, ps, rp, M, scale=1.0, func=None):
                    i, rs = rstg.next()
                    evac(stg[i][:M, :n], ps[:M, :n], rp, rs, scale, func)
                    P.dma("sp", dst_ap, stg[i][:M, :n], reads=[rs], writes=[rdst])

                def rope_out(dst_ap, rdst, ps1, rp1, ps2, rp2, cs, sn, lo, hi):
                    i1, r1 = rt1.next(); i2, r2 = rt2.next(); i, rs = rstg.next()
                    P.op("dve", lambda e: e.tensor_tensor(out=t1[i1][lo:hi, :n], in0=ps1[lo:hi, :n], in1=cs[lo:hi, :n], op=ALU.mult), reads=[rp1, rtab], writes=[r1])
                    P.op("dve", lambda e: e.tensor_tensor(out=t2[i2][lo:hi, :n], in0=ps2[lo:hi, :n], in1=sn[lo:hi, :n], op=ALU.mult), reads=[rp2, rtab], writes=[r2])
                    P.op("pool", lambda e: e.tensor_tensor(out=stg[i][lo:hi, :n], in0=t1[i1][lo:hi, :n], in1=t2[i2][lo:hi, :n], op=ALU.add), reads=[r1, r2], writes=[rs])
                    return i, rs

                for j in range(4):
                    ps, rp = fm(128 * j, 128); out_fm(dr["qna"][j, :, t0:t0 + n], rr["qna"], ps, rp, 128, 0.125)
                    ps, rp = fm(512 + 128 * j, 128); out_fm(dr["kna"][j, :, t0:t0 + n], rr["kna"], ps, rp, 128)
                for j in range(4):
                    ps1, rp1 = fm(1536 + 128 * j, 128); ps2, rp2 = fm(128 * j, 128, Wsw, rWsw)
                    i, rs = rope_out(None, None, ps1, rp1, ps2, rp2, c64q, s64q, 0, 128)
                    P.dma("sp", dr["qsw"][j, :, t0:t0 + n], stg[i][:, :n], reads=[rs], writes=[rr["qsw"]])
                ps1, rp1 = fm(2048, 128); ps2, rp2 = fm(512, 128, Wsw, rWsw)
                i, rs = rope_out(None, None, ps1, rp1, ps2, rp2, c64, s64, 0, 128)
                P.dma("sp", dr["ksw"][:, t0:t0 + n], stg[i][:, :n], reads=[rs], writes=[rr["ksw"]])
                for j in range(4):
                    ps, rp = fm(2304 + 128 * j, 128); out_fm(dr["u16"][j, :, t0:t0 + n], rr["u16"], ps, rp, 128)
                for j in range(2):
                    ps, rp = fm(2816 + 128 * j, 128)
                    P.op("act", lambda e, j=j, ps=ps: e.activation(out=cq[:, j, :n], in_=ps[:, :n], func=AF.Copy), reads=[rp], writes=[rcq])
                ps, rp = fm(3072, 128)
                P.op("dve", lambda e, ps=ps: e.tensor_copy(out=ckv[:, :n], in_=ps[:, :n]), reads=[rp], writes=[rckv])
                ps1, rp1 = fm(3200, 32); ps2, rp2 = fm(640, 32, Wsw, rWsw)
                i, rs = rope_out(None, None, ps1, rp1, ps2, rp2, c32k, s32k, 0, 32)
                for h in range(8):
                    P.dma("sp", dr["kml"][h, 64:96, t0:t0 + n], stg[i][0:32, :n], reads=[rs], writes=[rr["kml"]])
                for j in range(32):
                    ps, rp = fm(3232 + 128 * j, 128)
                    out_fm(dr["gat"][j // 8, j % 8, :, t0:t0 + n], rr["gat"], ps, rp, 128, 1.0, AF.Sigmoid)
                for sub in range(n // 128):
                    tk = slice(sub * 128, sub * 128 + 128)
                    i, rs = rstg.next()
                    for hf in range(2):
                        ip, rp = rpp.next()
                        mm(pp[ip][:, :256], [(nT[:, k, tk], W[:, k, 1024 + 256 * hf:1024 + 256 * hf + 256]) for k in range(8)], rp, [rW, rnT])
                        evac(stg[i][:, 256 * hf:256 * hf + 256], pp[ip][:, :256], rp, rs)
                    P.dma("sp", dr["vna"][t0 + sub * 128:t0 + sub * 128 + 128, :], stg[i][:, :], reads=[rs], writes=[rr["vna"]])
                    i, rs = rstg.next(); ip, rp = rpp.next()
                    mm(pp[ip][:, :128], [(nT[:, k, tk], W[:, k, 2176:2304]) for k in range(8)], rp, [rW, rnT])
                    evac(stg[i][:, :128], pp[ip][:, :128], rp, rs)
                    P.dma("sp", dr["vsw"][t0 + sub * 128:t0 + sub * 128 + 128, :], stg[i][:, :128], reads=[rs], writes=[rr["vsw"]])
                P.op("act", lambda e: e.activation(out=sq[:, 0:2, :n], in_=cq[:, :, :n], func=AF.Square), reads=[rcq], writes=[rsq])
                mm(pss[:, :n], [(ones[:], sq[:, j, :n]) for j in range(2)], rpss, [rsq, rones])
                P.op("act", lambda e: e.activation(out=rstd[:, :n], in_=pss[:, :n], func=AF.Sqrt, bias=self.epsb[:, 0:1], scale=1.0 / 256), reads=[rpss, self.reps], writes=[rrstd])
                P.op("dve", lambda e: e.reciprocal(out=rstd[:, :n], in_=rstd[:, :n]), reads=[rrstd], writes=[rrstd])
                for j in range(2):
                    P.op("dve", lambda e, j=j: e.scalar_tensor_tensor(out=cqn[:, j, :n], in0=cq[:, j, :n], scalar=qn[:, j:j + 1], in1=rstd[:, :n], op0=ALU.mult, op1=ALU.mult),
                         reads=[rcq, rqn, rrstd], writes=[rcqn])
                for h in range(8):
                    ip1, rp1 = rpp.next(); ip2, rp2 = rpp.next()
                    mm(pp[ip1][:96, :n], [(wuq[:, j, 96 * h:96 * h + 96], cqn[:, j, :n]) for j in range(2)], rp1, [rwuq, rcqn])
                    mm(pp[ip2][:96, :n], [(wuqs[:, j, 96 * h:96 * h + 96], cqn[:, j, :n]) for j in range(2)], rp2, [rwuq, rcqn])
                    i, rs = rope_out(None, None, pp[ip1], rp1, pp[ip2], rp2, c32q, s32q, 64, 96)
                    P.op("act", lambda e, i=i, ip1=ip1: e.activation(out=stg[i][0:64, :n], in_=pp[ip1][0:64, :n], func=AF.Copy, scale=96.0 ** -0.5), reads=[rp1], writes=[rs])
                    P.dma("sp", dr["qml"][h, :, t0:t0 + n], stg[i][0:96, :n], reads=[rs], writes=[rr["qml"]])
                P.op("act", lambda e: e.activation(out=sq[:, 0, :n], in_=ckv[:, :n], func=AF.Square), reads=[rckv], writes=[rsq])
                mm(pss[:, :n], [(ones[:], sq[:, 0, :n])], rpss, [rsq, rones])
                P.op("act", lambda e: e.activation(out=rstd[:, :n], in_=pss[:, :n], func=AF.Sqrt, bias=self.epsb[:, 0:1], scale=1.0 / 128), reads=[rpss, self.reps], writes=[rrstd])
                P.op("dve", lambda e: e.reciprocal(out=rstd[:, :n], in_=rstd[:, :n]), reads=[rrstd], writes=[rrstd])
                P.op("dve", lambda e: e.scalar_tensor_tensor(out=ckvn[:, :n], in0=ckv[:, :n], scalar=kvn[:, 0:1], in1=rstd[:, :n], op0=ALU.mult, op1=ALU.mult),
                     reads=[rckv, rkvn, rrstd], writes=[rckvn])
                for h in range(8):
                    ip, rp = rpp.next()
                    mm(pp[ip][:64, :n], [(wukv[:, 128 * h:128 * h + 64], ckvn[:, :n])], rp, [rwukv, rckvn])
                    out_fm(dr["kml"][h, 0:64, t0:t0 + n], rr["kml"], pp[ip], rp, 64)
                wv = wukv[:].rearrange("p (h c) -> p h c", c=128)
                for sub in range(n // 128):
                    tk = slice(sub * 128, sub * 128 + 128)
                    i, rs = rstg.next()
                    for hf in range(2):
                        ip, rp = rpp.next()
                        mm(pp[ip][:, :256].rearrange("p (h c) -> p h c", c=64), [(ckvn[:, tk], wv[:, 4 * hf:4 * hf + 4, 64:128])], rp, [rwukv, rckvn])
                        evac(stg[i][:, 256 * hf:256 * hf + 256], pp[ip][:, :256], rp, rs)
                    P.dma("sp", dr["vml"][t0 + sub * 128:t0 + sub * 128 + 128, :], stg[i][:, :], reads=[rs], writes=[rr["vml"]])

    def attn_tiles(self, es, pref):
        st = {}
        st["ps"] = [self.ps(es, pref + "_s%d" % i, [128, 512]) for i in range(3)]; st["rps"] = Rot(3)
        st["po"] = [self.ps(es, pref + "_o%d" % i, [128, 512]) for i in range(2)]; st["rpo"] = Rot(2)
        st["e"] = [self.sb(es, pref + "_e%d" % i, [128, 512], BF16) for i in range(4)]; st["re"] = Rot(4)
        st["rec"] = [self.sb(es, pref + "_r%d" % i, [128, 512], F32) for i in range(2)]; st["rrec"] = Rot(2)
        st["y"] = [self.sb(es, pref + "_y%d" % i, [64, 512], BF16) for i in range(3)]; st["ry"] = Rot(3)
        return st

    def attn_job(self, st, qap, N, tiles, rin, sink=None):
        P = self.P
        io, ro = st["rpo"].next()
        po = st["po"][io]
        nt = len(tiles)
        LA = 2
        sc = []

        def issue_score(tj):
            kap_ = tiles[tj][0]
            ip_, rp_ = st["rps"].next()
            ps_ = st["ps"][ip_]
            P.op("pe", lambda e, ps_=ps_, kap_=kap_: e.matmul(ps_[:, :N], lhsT=kap_, rhs=qap, start=True, stop=True), reads=rin, writes=[rp_], chain=True)
            sc.append((ps_, rp_))
        for tj in range(min(LA, nt)):
            issue_score(tj)
        for ti, (kap, vap, mask, rmask) in enumerate(tiles):
            if ti + LA < nt:
                issue_score(ti + LA)
            ps, rp = sc[ti]
            ie, re_ = st["re"].next()
            et = st["e"][ie]
            P.op("act", lambda e, ps=ps, et=et: e.activation(out=et[:, :N], in_=ps[:, :N], func=AF.Exp), reads=[rp], writes=[re_])
            if mask is not None:
                P.op("dve", lambda e, et=et, mask=mask: e.tensor_tensor(out=et[:, :N], in0=et[:, :N], in1=mask, op=ALU.mult), reads=[re_, rmask], writes=[re_])
            P.op("pe", lambda e, po=po, vap=vap, et=et, ti=ti: e.matmul(po[:, :N], lhsT=vap, rhs=et[:, :N], start=(ti == 0), stop=(ti == nt - 1)),
                 reads=rin + [re_], writes=[ro], chain=True)
        ir, rrc = st["rrec"].next(); iy, ry = st["ry"].next()
        rec = st["rec"][ir]; y = st["y"][iy]
        if sink is not None:
            esink, rsink, blocks = sink
            for (c0, c1, sc) in blocks:
                P.op("dve", lambda e, c0=c0, c1=c1, sc=sc: e.tensor_scalar(out=rec[64:128, c0:c1], in0=po[64:128, c0:c1], scalar1=esink[64:128, sc:sc + 1], scalar2=None, op0=ALU.add),
                     reads=[ro, rsink], writes=[rrc])
            P.op("dve", lambda e: e.reciprocal(out=rec[0:64, :N], in_=rec[64:128, :N]), reads=[rrc], writes=[rrc])
        else:
            P.op("dve", lambda e: e.reciprocal(out=rec[0:64, :N], in_=po[64:128, :N]), reads=[ro], writes=[rrc])
        P.op("dve", lambda e: e.tensor_tensor(out=y[0:64, :N], in0=po[0:64, :N], in1=rec[0:64, :N], op=ALU.mult), reads=[ro, rrc], writes=[ry])
        return y, ry

    def load_vaug(self, vaug, rv, src, h, dv_off, ntile, first):
        P = self.P
        if first:
            P.op("pool", lambda e: e.memset(vaug[:, :, 64:128], 1.0), writes=[rv])
        P.dma("sp", vaug[:, :, 0:64], src[:, dv_off:dv_off + 64].rearrange("(t p) c -> p t c", p=128), writes=[rv])

    def phase_mla(self, l, need_ctx):
        P, T, TL = self.P, self.T, self.TL
        dr, rr = self.dr, self.rr
        NT = T // 128
        with ExitStack() as es:
            st = attn_tiles(self, es, "ml")
            kT = self.sb(es, "ml_k", [96, T], BF16); qT = self.sb(es, "ml_q", [96, T], BF16)
            vaug = self.sb(es, "ml_v", [128, NT, 128], BF16)
            rk, rq, rv = Res(), Res(), Res()
            for h in range(8):
                P.dma("sp", kT[:], dr["kml"][h], reads=[rr["kml"]], writes=[rk])
                P.dma("sp", qT[:], dr["qml"][h], reads=[rr["qml"]], writes=[rq])
                load_vaug(self, vaug, rv, dr["vml"], h, 64 * h, NT, h == 0)
                qch = [(c * 512, 512) for c in range(TL // 512)] + ([(TL, CTX)] if need_ctx else [])
                for (t0, n) in qch:
                    kt = range(NT) if t0 < TL else range(TL // 128, NT)
                    tiles = [(kT[:, j * 128:(j + 1) * 128], vaug[:, j, :], None, None) for j in kt]
                    y, ry = attn_job(self, st, qT[:, t0:t0 + n], n, tiles, [rk, rq, rv])
                    P.dma("sp", dr["ymx"][3, h // 2, (h % 2) * 64:(h % 2) * 64 + 64, t0:t0 + n], y[0:64, :n], reads=[ry], writes=[rr["ymx"]])

    def phase_swa(self, l, need_ctx):
        P, T, TL = self.P, self.T, self.TL
        dr, rr = self.dr, self.rr
        NT = T // 128
        NB = TL // 128
        with ExitStack() as es:
            st = attn_tiles(self, es, "sw")
            kT = self.sb(es, "sw_k", [64, T], BF16)
            qT = self.sb(es, "sw_q", [64, 4, T], BF16)
            vaug = self.sb(es, "sw_v", [128, NT, 128], BF16)
            mp = self.sb(es, "sw_mp", [128, 512], F32); mn = self.sb(es, "sw_mn", [128, 512], F32); rm = Res()
            sk = self.sb(es, "sw_sk", [128, 8], F32); rsk = Res()
            P.dma("sp", mp[:], dr["k_mprev"][:, :], writes=[rm]); P.dma("sp", mn[:], dr["k_mnext"][:, :], writes=[rm])
            P.dma("sp", sk[:], dr["swa_sink"][l, :].partition_broadcast(128), writes=[rsk])
            P.op("act", lambda e: e.activation(out=sk[:], in_=sk[:], func=AF.Exp), reads=[rsk], writes=[rsk])
            rk, rq, rv = Res(), Res(), Res()
            for g in range(2):
                P.dma("sp", kT[:], dr["ksw"][64 * g:64 * g + 64, :], reads=[rr["ksw"]], writes=[rk])
                for hh in range(4):
                    h = 4 * g + hh
                    P.dma("sp", qT[:, hh, :], dr["qsw"][h // 2, (h % 2) * 64:(h % 2) * 64 + 64, :], reads=[rr["qsw"]], writes=[rq])
                load_vaug(self, vaug, rv, dr["vsw"], g, 64 * g, NT, g == 0)
                ctx_tiles = [(kT[:, j * 128:(j + 1) * 128], vaug[:, j, :], None, None) for j in range(NB, NT)]
                for hp in range(2):
                    h0 = 4 * g + 2 * hp
                    blocks = [(0, 128, h0), (128, 256, h0 + 1)]
                    jobs = [(nb * 128, nb) for nb in range(NB)]
                    if need_ctx:
                        jobs += [(TL, -1), (TL + 128, -1)]
                    for (q0, nb) in jobs:
                        tiles = []
                        if nb >= 0:
                            if nb > 0:
                                tiles.append((kT[:, (nb - 1) * 128:nb * 128], vaug[:, nb - 1, :], mp[:, 0:256], rm))
                            tiles.append((kT[:, nb * 128:(nb + 1) * 128], vaug[:, nb, :], None, None))
                            if nb < NB - 1:
                                tiles.append((kT[:, (nb + 1) * 128:(nb + 2) * 128], vaug[:, nb + 1, :], mn[:, 0:256], rm))
                        tiles += ctx_tiles
                        y, ry = attn_job(self, st, qT[:, 2 * hp:2 * hp + 2, q0:q0 + 128], 256, tiles, [rk, rq, rv], sink=(sk, rsk, blocks))
                        for hb in range(2):
                            h = h0 + hb
                            P.dma("sp", dr["ymx"][1, h // 2, (h % 2) * 64:(h % 2) * 64 + 64, q0:q0 + 128], y[0:64, hb * 128:hb * 128 + 128], reads=[ry], writes=[rr["ymx"]])

    def phase_na(self, l, need_ctx):
        P, nc, T, TL = self.P, self.nc, self.T, self.TL
        dr, rr = self.dr, self.rr
        NT = T // 128
        R = TL // 64
        assert R >= 10
        with ExitStack() as es:
            with ExitStack() as es2:
                rpT = self.sb(es2, "na_rpT", [31, 8, 15], F32); rrp = Res()
                Gt = self.sb(es2, "na_G", [31, 4096], F32); cm = self.sb(es2, "na_cm", [15, 4096], F32); rG = Res()
                eb = self.sb(es2, "na_eb", [15, 4096], F32); reb = Res()
                pb = [self.ps(es2, "na_pb%d" % i, [128, 512]) for i in range(2)]; rpb = Rot(2)
                P.dma("sp", rpT[:], dr["na_rpb"][l].rearrange("h r c -> c h r"), writes=[rrp])
                P.dma("sp", Gt[:], dr["k_G"][:, :], writes=[rG]); P.dma("sp", cm[:], dr["k_cmask"][:, :], writes=[rG])
                for h in range(8):
                    for cc in range(16):
                        ip, rp = rpb.next()
                        P.op("pe", lambda e, ip=ip, h=h, cc=cc: e.matmul(pb[ip][:15, :256], lhsT=rpT[:, h, :], rhs=Gt[:, cc * 256:(cc + 1) * 256], start=True, stop=True),
                             reads=[rrp, rG], writes=[rp], chain=True)
                        P.op("act", lambda e, ip=ip, cc=cc: e.activation(out=eb[:, cc * 256:(cc + 1) * 256], in_=pb[ip][:15, :256], func=AF.Exp), reads=[rp], writes=[reb])
                    P.op("dve", lambda e: e.tensor_tensor(out=eb[:], in0=eb[:], in1=cm[:], op=ALU.mult), reads=[reb, rG], writes=[reb])
                    P.dma("sp", dr["ebd"][h].rearrange("r a b -> r (a b)"), eb[:], reads=[reb], writes=[rr["ebd"]])
            P.barrier()
            st = attn_tiles(self, es, "na")
            kT = self.sb(es, "na_k", [64, T], BF16); qT = self.sb(es, "na_q", [64, T], BF16)
            vaug = self.sb(es, "na_v", [128, NT, 128], BF16)
            EB = self.sb(es, "na_EB", [128, 5, 5, 128], F32); rEB = Res()
            P.op("pool", lambda e: e.memset(EB[:], 0.0), writes=[rEB])
            rk, rq, rv = Res(), Res(), Res()
            types = {0: (0, 0), 2: (1, 0), 4: (2, None), 6: (3, 2), 8: (4, 2)}
            for h in range(8):
                P.dma("sp", kT[:], dr["kna"][h // 2, (h % 2) * 64:(h % 2) * 64 + 64, :], reads=[rr["kna"]], writes=[rk])
                P.dma("sp", qT[:], dr["qna"][h // 2, (h % 2) * 64:(h % 2) * 64 + 64, :], reads=[rr["qna"]], writes=[rq])
                load_vaug(self, vaug, rv, dr["vna"], h, 64 * h, NT, h == 0)
                for qrel, (ty, r0r) in types.items():
                    for kt in range(5):
                        for jl in range(2):
                            for ql in range(2):
                                j = 2 * kt + jl
                                r0rel = ql if r0r is None else r0r
                                if not (r0rel <= j <= r0rel + 7):
                                    continue
                                drr = j - (qrel + ql) + 7
                                assert 0 <= drr <= 14
                                P.dma("sp", EB[jl * 64:(jl + 1) * 64, ty, kt, ql * 64:(ql + 1) * 64], dr["ebd"][h, drr], reads=[rr["ebd"]], writes=[rEB], accum=True)
                ctx_tiles = [(kT[:, j * 128:(j + 1) * 128], vaug[:, j, :], None, None) for j in range(TL // 128, NT)]
                for r in range(0, R, 2):
                    ks = min(max(r - 4, 0), R - 10)
                    ty = types[r - ks][0]
                    tiles = [(kT[:, (ks + 2 * kt) * 64:(ks + 2 * kt) * 64 + 128], vaug[:, (ks + 2 * kt) // 2, :], EB[:, ty, kt, :], rEB) for kt in range(5)]
                    tiles += ctx_tiles
                    y, ry = attn_job(self, st, qT[:, r * 64:r * 64 + 128], 128, tiles, [rk, rq, rv])
                    P.dma("sp", dr["ymx"][0, h // 2, (h % 2) * 64:(h % 2) * 64 + 64, r * 64:r * 64 + 128], y[0:64, :128], reads=[ry], writes=[rr["ymx"]])
                if need_ctx:
                    y, ry = attn_job(self, st, qT[:, TL:TL + 256], 256, ctx_tiles, [rk, rq, rv])
                    P.dma("sp", dr["ymx"][0, h // 2, (h % 2) * 64:(h % 2) * 64 + 64, TL:TL + 256], y[0:64, :256], reads=[ry], writes=[rr["ymx"]])

    def phase_merge(self, l, need_ctx):
        P, T, TL = self.P, self.T, self.TL
        dr, rr = self.dr, self.rr
        with ExitStack() as es:
            wb = self.sb(es, "g_wb", [128, 4, 4, 1024], BF16); rwb = Res()
            wo = self.sb(es, "g_wo", [128, 8, 1024], BF16); rwo = Res()
            for nb in range(4):
                for k in range(4):
                    P.dma("pool", wb[:, nb, k, :], dr["w_branch"][l, nb, k * 128:(k + 1) * 128, :], writes=[rwb], accum=True)
            for k in range(8):
                P.dma("pool", wo[:, k, :], dr["w_out"][l, k * 128:(k + 1) * 128, :], writes=[rwo], accum=True)
            yT = self.sb(es, "g_y", [128, 4, 4, CH], BF16); ryT = Res()
            gt = self.sb(es, "g_g", [128, 4, 8, CH], BF16); rgt = Res()
            hc = self.sb(es, "g_h", [128, 8, CH], F32); rh = Res()
            mg = self.sb(es, "g_m", [128, 8, CH], BF16); rmg = Res()
            acc = [self.sb(es, "g_a%d" % i, [128, CH], F32) for i in range(2)]; racc = Rot(2)
            tmp = [self.sb(es, "g_t%d" % i, [128, CH], F32) for i in range(2)]; rtmp = Rot(2)
            pp = [self.ps(es, "g_p%d" % i, [128, 512]) for i in range(4)]; rpp = Rot(4)
            po = [self.ps(es, "g_po%d" % i, [128, 512]) for i in range(2)]; rpo = Rot(2)
            for (t0, n) in self.chunks(need_ctx):
                s = 0 if t0 < TL else 1
                P.dma("sp", hc[:, :, :n], dr["hT"][:, :, t0:t0 + n].rearrange("k p t -> p k t"), reads=[rr["hT"]], writes=[rh])
                for nb in range(4):
                    P.dma("sp", yT[:, nb, :, :n], dr["ymx"][nb, :, :, t0:t0 + n].rearrange("k p t -> p k t"), reads=[rr["ymx"]], writes=[ryT], accum=True)
                    P.dma("sp", gt[:, nb, :, :n], dr["gat"][nb, :, :, t0:t0 + n].rearrange("k p t -> p k t"), reads=[rr["gat"]], writes=[rgt], accum=True)
                for m in range(8):
                    ia, ra = racc.next()
                    for nb in range(4):
                        ip, rp = rpp.next()
                        for k in range(4):
                            P.op("pe", lambda e, ip=ip, nb=nb, k=k, m=m: e.matmul(pp[ip][:, :n], lhsT=wb[:, nb, k, m * 128:(m + 1) * 128], rhs=yT[:, nb, k, :n], start=(k == 0), stop=(k == 3)),
                                 reads=[rwb, ryT], writes=[rp], chain=True)
                        if nb == 0:
                            P.op("dve", lambda e, ip=ip, ia=ia, nb=nb, m=m: e.tensor_tensor(out=acc[ia][:, :n], in0=pp[ip][:, :n], in1=gt[:, nb, m, :n], op=ALU.mult), reads=[rp, rgt], writes=[ra])
                        else:
                            it, rt = rtmp.next()
                            P.op("dve", lambda e, ip=ip, it=it, nb=nb, m=m: e.tensor_tensor(out=tmp[it][:, :n], in0=pp[ip][:, :n], in1=gt[:, nb, m, :n], op=ALU.mult), reads=[rp, rgt], writes=[rt])
                            if nb < 3:
                                P.op("pool", lambda e, ia=ia, it=it: e.tensor_tensor(out=acc[ia][:, :n], in0=acc[ia][:, :n], in1=tmp[it][:, :n], op=ALU.add), reads=[ra, rt], writes=[ra])
                            else:
                                P.op("pool", lambda e, ia=ia, it=it, m=m: e.tensor_tensor(out=mg[:, m, :n], in0=acc[ia][:, :n], in1=tmp[it][:, :n], op=ALU.add), reads=[ra, rt], writes=[rmg])
                for mo in range(8):
                    ip, rp = rpo.next()
                    for k in range(8):
                        P.op("pe", lambda e, ip=ip, mo=mo, k=k: e.matmul(po[ip][:, :n], lhsT=wo[:, k, mo * 128:(mo + 1) * 128], rhs=mg[:, k, :n], start=(k == 0), stop=(k == 7)),
                             reads=[rwo, rmg], writes=[rp], chain=True)
                    P.op("dve", lambda e, ip=ip, mo=mo, s=s, n=n: e.scalar_tensor_tensor(out=hc[:, mo, :n], in0=po[ip][:, :n], scalar=self.mod[:, 5, mo, s:s + 1], in1=hc[:, mo, :n], op0=ALU.mult, op1=ALU.add),
                         reads=[rp, self.rmod, rh], writes=[rh])
                P.dma("sp", dr["hT"][:, :, t0:t0 + n].rearrange("k p t -> p k t"), hc[:, :, :n], reads=[rh], writes=[rr["hT"]])

    K.phase_proj = phase_proj
    K.phase_mla = phase_mla
    K.phase_swa = phase_swa
    K.phase_na = phase_na
    K.phase_merge = phase_merge
    K.phase_s5 = phase_s5


install(K)
```

```python
import math
from contextlib import ExitStack
import numpy as np
import concourse.bass as bass
import concourse.mybir as mybir
from concourse.bass_utils import run_bass_kernel_spmd

F32 = mybir.dt.float32
BF16 = mybir.dt.bfloat16
AF = mybir.ActivationFunctionType
ALU = mybir.AluOpType

ENGS = ("pe", "act", "dve", "pool", "sp")
D = 1024
DFF = 2816
CTX = 256
NMOD = 9
INC = 7328
EPS = 1e-6


class Res:
    __slots__ = ("name", "writers", "readers")

    def __init__(self, name=""):
        self.name = name
        self.writers = {}
        self.readers = {}


class Prog:
    def __init__(self, nc, n_dma_sems=14):
        self.nc = nc
        self.q = {e: [] for e in ENGS}
        self.cnt = {e: 0 for e in ENGS}
        self.known = {}
        self.sems = {}
        self.n_dma_sems = n_dma_sems
        self.dma_n = {e: 0 for e in ENGS}
        self.dma_sems = {}
        self._ctx = []

    def open(self):
        nc = self.nc
        for e in ENGS:
            cm = nc.semaphore("s_" + e)
            self.sems[e] = cm.__enter__()
            self._ctx.append(cm)
        for e in ("sp", "pool", "act"):
            lst = []
            for i in range(self.n_dma_sems):
                cm = nc.semaphore("d_%s%d" % (e, i))
                lst.append(cm.__enter__())
                self._ctx.append(cm)
            self.dma_sems[e] = lst

    def close(self):
        for cm in reversed(self._ctx):
            cm.__exit__(None, None, None)
        self._ctx = []

    def _need(self, eng, deps):
        for key, (sem, val) in deps.items():
            k = (eng, key)
            if self.known.get(k, 0) >= val:
                continue
            self.known[k] = val
            self.q[eng].append(("wait", sem, val))

    def _collect(self, eng, reads, writes):
        deps = {}

        def add(d, same_ok):
            for key, (sem, val) in d.items():
                if key == eng and not same_ok:
                    continue
                if key not in deps or deps[key][1] < val:
                    deps[key] = (sem, val)
        for r in reads:
            add(r.writers, True)
        for w in writes:
            add(w.writers, False)
            add(w.readers, False)
        return deps

    def _commit(self, ev_key, ev, reads, writes):
        for w in writes:
            w.writers = {ev_key: ev}
            w.readers = {}
        for r in reads:
            if r in writes:
                continue
            old = r.readers.get(ev_key)
            if old is None or old[1] < ev[1]:
                r.readers[ev_key] = ev

    def op(self, eng, fn, reads=(), writes=(), chain=False):
        deps = self._collect(eng, reads, writes)
        if chain and eng in deps:
            del deps[eng]
        self._need(eng, deps)
        self.cnt[eng] += 1
        ev = (self.sems[eng], self.cnt[eng])
        self.q[eng].append(("op", fn, self.sems[eng]))
        self._commit(eng, ev, reads, writes)

    def dma(self, eng, out, in_, reads=(), writes=(), accum=False, **kw):
        if accum:
            deps = self._collect(eng, reads, ())
            for w in writes:
                for d_ in (w.writers, w.readers):
                    for key_, (sem_, val_) in d_.items():
                        if d_ is w.writers and key_.startswith("dma_"):
                            continue
                        if key_ not in deps or deps[key_][1] < val_:
                            deps[key_] = (sem_, val_)
        else:
            deps = self._collect(eng, reads, writes)
        n = self.dma_n[eng]
        self.dma_n[eng] += 1
        slot = n % self.n_dma_sems
        sem = self.dma_sems[eng][slot]
        rnd = n // self.n_dma_sems
        key = "dma_%s_%d" % (eng, slot)
        if rnd > 0:
            deps[key] = (sem, 16 * rnd)
        if eng in deps:
            del deps[eng]
        self._need(eng, deps)
        ev = (sem, 16 * (rnd + 1))
        self.q[eng].append(("dma", out, in_, sem, kw))
        if accum:
            for w in writes:
                w.writers[key] = ev
            self._commit(key, ev, reads, ())
        else:
            self._commit(key, ev, reads, writes)

    def barrier(self):
        deps = {}
        for e in ENGS:
            if self.cnt[e] > 0:
                deps[e] = (self.sems[e], self.cnt[e])
        for e in ("sp", "pool", "act"):
            n = self.dma_n[e]
            for slot in range(min(n, self.n_dma_sems)):
                last = ((n - 1 - slot) // self.n_dma_sems)
                deps["dma_%s_%d" % (e, slot)] = (self.dma_sems[e][slot], 16 * (last + 1))
        for e in ENGS:
            d = {k: v for k, v in deps.items() if k != e}
            self._need(e, d)

    def wait_all(self, eng, ress):
        deps = {}
        for r in ress:
            for d in (r.writers, r.readers):
                for key, (sem, val) in d.items():
                    if key == eng:
                        continue
                    if key not in deps or deps[key][1] < val:
                        deps[key] = (sem, val)
        self._need(eng, deps)

    def emit(self):
        nc = self.nc
        with nc.allow_non_contiguous_dma(reason="small strided vector loads"), nc.Block() as block:
            def make(ename):
                def body(e):
                    for it in self.q[ename]:
                        if it[0] == "wait":
                            e.wait_ge(it[1], it[2])
                        elif it[0] == "op":
                            it[1](e).then_inc(it[2], 1)
                        else:
                            _, out, in_, sem, kw = it
                            e.dma_start(out=out, in_=in_, **kw).then_inc(sem, 16)
                return body
            block.tensor(make("pe"))
            block.scalar(make("act"))
            block.vector(make("dve"))
            block.gpsimd(make("pool"))
            block.sync(make("sp"))


class Rot:
    def __init__(self, n, name=""):
        self.n = n
        self.i = 0
        self.res = [Res("%s%d" % (name, j)) for j in range(n)]

    def next(self):
        j = self.i % self.n
        self.i += 1
        return j, self.res[j]


class K:
    def __init__(self, TL, depth, debug=False, phases=None, nlayers=None):
        self.TL = TL
        self.T = TL + CTX
        self.depth = depth
        self.debug = debug
        self.phases = phases
        self.nlayers = depth if nlayers is None else nlayers
        self.nc = bass.Bass("TRN2", target_bir_lowering=False)
        self.P = Prog(self.nc)
        self.dr = {}
        self.rr = {}

    def din(self, name, shape, dt=F32):
        t = self.nc.dram_tensor(name, list(shape), dt, kind="ExternalInput").ap()
        self.dr[name] = t
        self.rr[name] = Res(name)
        return t

    def dscr(self, name, shape, dt, out=False):
        kind = "ExternalOutput" if (out or self.debug) else "Internal"
        t = self.nc.dram_tensor(name, list(shape), dt, kind=kind).ap()
        self.dr[name] = t
        self.rr[name] = Res(name)
        return t

    def chunks(self, with_ctx=True):
        out = [(c * 256, 256) for c in range(self.TL // 256)]
        if with_ctx:
            out.append((self.TL, CTX))
        return out

    def declare(self):
        L, T, TL = self.depth, self.T, self.TL
        d = self.din
        d("x", [TL, D]); d("c", [D]); d("ctx", [CTX, D]); d("c_ctx", [D])
        d("ada_w", [L, D, NMOD * D]); d("ada_b", [L, NMOD * D])
        for f in ("ffn1", "ffn2"):
            d(f + "_norm", [L, D]); d(f + "_w_gate", [L, D, DFF]); d(f + "_w_up", [L, D, DFF]); d(f + "_w_down", [L, DFF, D])
        d("mix_norm", [L, D]); d("w_in", [L, D, INC]); d("w_in_sw", [L, D, 672])
        d("na_rpb", [L, 8, 15, 31]); d("swa_sink", [L, 8])
        d("s5_lambda_re", [L, 2, 32, 64]); d("s5_lambda_im", [L, 2, 32, 64]); d("s5_log_dt", [L, 2, 32])
        d("s5_b_re", [L, 2, 32, 64, 16]); d("s5_b_im", [L, 2, 32, 64, 16])
        d("s5_c_re", [L, 2, 32, 16, 64]); d("s5_c_im", [L, 2, 32, 16, 64])
        d("s5_d", [L, 512]); d("s5_glu_w", [L, 512, 512]); d("s5_glu_b", [L, 512])
        d("mla_q_norm", [L, 256]); d("mla_w_uq", [L, 256, 768]); d("mla_w_uq_sw", [L, 256, 768])
        d("mla_kv_norm", [L, 128]); d("mla_w_ukv", [L, 128, 1024])
        d("w_branch", [L, 4, 512, D]); d("w_out", [L, D, D]); d("final_norm", [D])
        d("k_ident", [128, 128]); d("k_cos64", [128, T]); d("k_sin64", [128, T])
        d("k_cos64q", [128, T]); d("k_sin64q", [128, T])
        d("k_cos32q", [96, T]); d("k_sin32q", [96, T]); d("k_cos32k", [32, T]); d("k_sin32k", [32, T])
        d("k_mprev", [128, 512]); d("k_mnext", [128, 512])
        d("k_G", [31, 4096]); d("k_cmask", [15, 4096])
        s = self.dscr
        s("hT", [8, 128, T], F32)
        s("qna", [4, 128, T], BF16); s("kna", [4, 128, T], BF16); s("vna", [T, 512], BF16)
        s("qsw", [4, 128, T], BF16); s("ksw", [128, T], BF16); s("vsw", [T, 128], BF16)
        s("u16", [4, 128, T], BF16)
        s("qml", [8, 96, T], BF16); s("kml", [8, 96, T], BF16); s("vml", [T, 512], BF16)
        s("gat", [4, 8, 128, T], BF16)
        s("ymx", [4, 4, 128, T], BF16)
        s("ys5", [4, 128, T], F32)
        s("ebd", [8, 15, 64, 64], F32)
        s("out", [TL, D], F32, out=True)
        if self.debug:
            s("dbg_mod", [self.depth, 128, NMOD * 16], F32)

    def sb(self, es, name, shape, dt):
        self._uid = getattr(self, "_uid", 0) + 1
        return es.enter_context(self.nc.sbuf_tensor("%s_%d" % (name, self._uid), list(shape), dt))

    def ps(self, es, name, shape, dt=F32):
        self._uid = getattr(self, "_uid", 0) + 1
        return es.enter_context(self.nc.psum_tensor("%s_%d" % (name, self._uid), list(shape), dt))

    def phase_init(self):
        P, nc, T, TL = self.P, self.nc, self.T, self.TL
        with ExitStack() as es:
            ident = self.sb(es, "i_ident", [128, 128], F32)
            xin = [self.sb(es, "i_x%d" % i, [128, D], F32) for i in range(2)]
            xo = [self.sb(es, "i_o%d" % i, [128, 8, 128], F32) for i in range(2)]
            pst = [self.ps(es, "i_ps%d" % i, [128, 8, 128]) for i in range(2)]
            r_id = Res()
            r_in = Rot(2); r_o = Rot(2); r_ps = Rot(2)
            P.dma("sp", ident[:], self.dr["k_ident"][:, :], writes=[r_id])
            for tt in range(T // 128):
                t0 = tt * 128
                src = self.dr["x"][t0:t0 + 128, :] if t0 < TL else self.dr["ctx"][t0 - TL:t0 - TL + 128, :]
                i, ri = r_in.next(); j, ro = r_o.next(); p, rp = r_ps.next()
                P.dma("sp", xin[i][:], src, writes=[ri])
                for k in range(8):
                    P.op("pe", lambda e, i=i, p=p, k=k: e.transpose(out=pst[p][:, k, :], in_=xin[i][:, k * 128:(k + 1) * 128], identity=ident[:]),
                         reads=[ri, r_id], writes=[rp], chain=True)
                P.op("act", lambda e, j=j, p=p: e.activation(out=xo[j][:, 0:4, :], in_=pst[p][:, 0:4, :], func=AF.Copy), reads=[rp], writes=[ro])
                P.op("dve", lambda e, j=j, p=p: e.tensor_copy(out=xo[j][:, 4:8, :], in_=pst[p][:, 4:8, :]), reads=[rp], writes=[ro])
                P.dma("sp", self.dr["hT"][:, :, t0:t0 + 128].rearrange("k p t -> p k t"), xo[j][:], reads=[ro], writes=[self.rr["hT"]])

    def phase_mod(self, l):
        P, nc = self.P, self.nc
        mod, rmod = self.mod, self.rmod
        with ExitStack() as es:
            sc = self.sb(es, "m_sc", [128, 8, 2], F32)
            wt = [self.sb(es, "m_w%d" % i, [128, NMOD * D], F32) for i in range(2)]
            bt = self.sb(es, "m_b", [128, 72], F32)
            pm = self.ps(es, "m_ps", [128, 72, 2])
            rsc, rb, rpm = Res(), Res(), Res()
            rw = Rot(2)
            c0, rc0 = self.load_vec(es, "m_c0", self.dr["c"], 8)
            c1, rc1 = self.load_vec(es, "m_c1", self.dr["c_ctx"], 8)
            P.op("dve", lambda e: e.tensor_copy(out=sc[:, :, 0], in_=c0[:]), reads=[rc0], writes=[rsc])
            P.op("dve", lambda e: e.tensor_copy(out=sc[:, :, 1], in_=c1[:]), reads=[rc1], writes=[rsc])
            P.dma("sp", bt[:], self.dr["ada_b"][l].rearrange("(j p) -> p j", p=128), writes=[rb])
            P.op("act", lambda e: e.activation(out=sc[:], in_=sc[:], func=AF.Silu), reads=[rsc], writes=[rsc])
            wi = []
            for k in range(8):
                i, r = rw.next()
                P.dma("sp", wt[i][:], self.dr["ada_w"][l, k * 128:(k + 1) * 128, :], writes=[r])
                wi.append((i, r))
                for j in range(72):
                    P.op("pe", lambda e, i=i, j=j, k=k: e.matmul(pm[:, j, :], lhsT=wt[i][:, j * 128:(j + 1) * 128], rhs=sc[:, k, :], start=True, stop=True),
                         reads=[r, rsc], writes=[rpm], chain=True)
                if k == 0:
                    P.op("dve", lambda e: e.tensor_copy(out=self.macc[:], in_=pm[:]), reads=[rpm], writes=[self.rmacc])
                else:
                    P.op("dve", lambda e: e.tensor_tensor(out=self.macc[:], in0=pm[:], in1=self.macc[:], op=ALU.add), reads=[rpm, self.rmacc], writes=[self.rmacc])
            for s in range(2):
                P.op("dve", lambda e, s=s: e.tensor_tensor(out=mod[:, :, :, s], in0=self.macc[:, :, s].rearrange("p (m k) -> p m k", k=8),
                                                           in1=bt[:].rearrange("p (m k) -> p m k", k=8), op=ALU.add),
                     reads=[self.rmacc, rb], writes=[rmod])

    def load_vec(self, es, name, src_ap, nk):
        t = self.sb(es, name, [128, nk], F32)
        r = Res(name)
        self.P.dma("sp", t[:], src_ap.rearrange("(k p) -> p k", p=128), writes=[r])
        return t, r

    def make_AS(self, es, pref, normw, rn, mi_shift, mi_scale):
        P = self.P
        A = self.sb(es, pref + "_A", [128, 8, 2], F32)
        rA = Res()
        for s in range(2):
            P.op("dve", lambda e, s=s: e.scalar_tensor_tensor(out=A[:, :, s], in0=self.mod[:, mi_scale, :, s], scalar=1.0, in1=normw[:], op0=ALU.add, op1=ALU.mult),
                 reads=[self.rmod, rn], writes=[rA])
        return A, rA

    def norm_chunk(self, es_tiles, hc, rh, n, s, A, rA, mi_shift, nT, rnT):
        P = self.P
        sq, rsq, pss, rpss, rstd, rrstd, ones, rones, tmp, rtmp = es_tiles
        P.op("act", lambda e: e.activation(out=sq[:, :, :n], in_=hc[:, :, :n], func=AF.Square), reads=[rh], writes=[rsq])
        for k in range(8):
            P.op("pe", lambda e, k=k: e.matmul(pss[:, :n], lhsT=ones[:], rhs=sq[:, k, :n], start=(k == 0), stop=(k == 7)),
                 reads=[rsq, rones], writes=[rpss], chain=True)
        P.op("act", lambda e: e.activation(out=rstd[:, :n], in_=pss[:, :n], func=AF.Sqrt, bias=self.epsb[:, 0:1], scale=1.0 / D), reads=[rpss, self.reps], writes=[rrstd])
        P.op("dve", lambda e: e.reciprocal(out=rstd[:, :n], in_=rstd[:, :n]), reads=[rrstd], writes=[rrstd])
        for k in range(8):
            j, rt = rtmp.next()
            P.op("dve", lambda e, k=k, j=j: e.scalar_tensor_tensor(out=tmp[j][:, :n], in0=hc[:, k, :n], scalar=A[:, k, s:s + 1], in1=rstd[:, :n], op0=ALU.mult, op1=ALU.mult),
                 reads=[rh, rA, rrstd], writes=[rt])
            P.op("act", lambda e, k=k, j=j: e.activation(out=nT[:, k, :n], in_=tmp[j][:, :n], func=AF.Identity, bias=self.mod[:, mi_shift, k, s:s + 1], scale=1.0),
                 reads=[rt, self.rmod], writes=[rnT])

    def norm_tiles(self, es, pref):
        sq = self.sb(es, pref + "_sq", [128, 8, 512], BF16)
        pss = self.ps(es, pref + "_pss", [128, 512])
        rstd = self.sb(es, pref + "_rstd", [128, 512], F32)
        tmp = [self.sb(es, pref + "_tmp%d" % i, [128, 512], F32) for i in range(2)]
        return (sq, Res(), pss, Res(), rstd, Res(), self.ones, self.rones, tmp, Rot(2))

    def phase_ffn(self, l, which, mi0, with_ctx):
        P, nc = self.P, self.nc
        with ExitStack() as es:
            wg = self.sb(es, "f_wg", [128, 8, DFF], BF16)
            wu = self.sb(es, "f_wu", [128, 8, DFF], BF16)
            wd = self.sb(es, "f_wd", [128, 22, D], BF16)
            rwg, rwu, rwd = Res(), Res(), Res()
            for k in range(8):
                P.dma("pool", wg[:, k, :], self.dr[which + "_w_gate"][l, k * 128:(k + 1) * 128, :], writes=[rwg], accum=True)
                P.dma("pool", wu[:, k, :], self.dr[which + "_w_up"][l, k * 128:(k + 1) * 128, :], writes=[rwu], accum=True)
            for k in range(22):
                P.dma("pool", wd[:, k, :], self.dr[which + "_w_down"][l, k * 128:(k + 1) * 128, :], writes=[rwd], accum=True)
            normw, rn = self.load_vec(es, "f_nw", self.dr[which + "_norm"][l], 8)
            A, rA = self.make_AS(es, "f", normw, rn, mi0, mi0 + 1)
            G = self.sb(es, "f_G", [128, 8, 2], F32)
            rG = Res()
            P.op("dve", lambda e: e.tensor_scalar(out=G[:], in0=self.mod[:, mi0 + 2, :, :], scalar1=0.5, scalar2=None, op0=ALU.mult), reads=[self.rmod], writes=[rG])
            hc = self.sb(es, "f_h", [128, 8, 512], F32)
            rh = Res()
            nT = self.sb(es, "f_nT", [128, 8, 512], BF16)
            rnT = Res()
            hid = self.sb(es, "f_hid", [128, 22, 512], BF16)
            rhid = Res()
            sg = [self.sb(es, "f_sg%d" % i, [128, 512], F32) for i in range(2)]
            rsg = Rot(2)
            nt = self.norm_tiles(es, "f")
            pg = [self.ps(es, "f_pg%d" % i, [128, 512]) for i in range(2)]
            pu = [self.ps(es, "f_pu%d" % i, [128, 512]) for i in range(2)]
            pd = [self.ps(es, "f_pd%d" % i, [128, 512]) for i in range(2)]
            rpg, rpu, rpd = Rot(2), Rot(2), Rot(2)
            hT = self.dr["hT"]; rhT = self.rr["hT"]
            for (t0, n) in self.chunks(with_ctx):
                s = 0 if t0 < self.TL else 1
                P.dma("sp", hc[:, :, :n], hT[:, :, t0:t0 + n].rearrange("k p t -> p k t"), reads=[rhT], writes=[rh])
                self.norm_chunk(nt, hc, rh, n, s, A, rA, mi0, nT, rnT)
                for m in range(22):
                    ig, rg_ = rpg.next(); iu, ru_ = rpu.next(); isg, rs_ = rsg.next()
                    for k in range(8):
                        P.op("pe", lambda e, ig=ig, m=m, k=k: e.matmul(pg[ig][:, :n], lhsT=wg[:, k, m * 128:(m + 1) * 128], rhs=nT[:, k, :n], start=(k == 0), stop=(k == 7)),
                             reads=[rwg, rnT], writes=[rg_], chain=True)
                    for k in range(8):
                        P.op("pe", lambda e, iu=iu, m=m, k=k: e.matmul(pu[iu][:, :n], lhsT=wu[:, k, m * 128:(m + 1) * 128], rhs=nT[:, k, :n], start=(k == 0), stop=(k == 7)),
                             reads=[rwu, rnT], writes=[ru_], chain=True)
                    P.op("act", lambda e, ig=ig, isg=isg: e.activation(out=sg[isg][:, :n], in_=pg[ig][:, :n], func=AF.Silu), reads=[rg_], writes=[rs_])
                    P.op("dve", lambda e, iu=iu, isg=isg, m=m: e.tensor_tensor(out=hid[:, m, :n], in0=pu[iu][:, :n], in1=sg[isg][:, :n], op=ALU.mult),
                         reads=[ru_, rs_], writes=[rhid])
                for mo in range(8):
                    ip, rp_ = rpd.next()
                    for k in range(22):
                        P.op("pe", lambda e, ip=ip, mo=mo, k=k: e.matmul(pd[ip][:, :n], lhsT=wd[:, k, mo * 128:(mo + 1) * 128], rhs=hid[:, k, :n], start=(k == 0), stop=(k == 21)),
                             reads=[rwd, rhid], writes=[rp_], chain=True)
                    P.op("dve", lambda e, ip=ip, mo=mo, s=s, n=n: e.scalar_tensor_tensor(out=hc[:, mo, :n], in0=pd[ip][:, :n], scalar=G[:, mo, s:s + 1], in1=hc[:, mo, :n], op0=ALU.mult, op1=ALU.add),
                         reads=[rp_, rG, rh], writes=[rh])
                P.dma("sp", hT[:, :, t0:t0 + n].rearrange("k p t -> p k t"), hc[:, :, :n], reads=[rh], writes=[rhT])

    def phase_final(self):
        P, nc, TL = self.P, self.nc, self.TL
        with ExitStack() as es:
            fw, rfw = self.load_vec(es, "z_fw", self.dr["final_norm"], 8)
            ident = self.sb(es, "z_ident", [128, 128], F32)
            rid = Res()
            P.dma("sp", ident[:], self.dr["k_ident"][:, :], writes=[rid])
            hc = self.sb(es, "z_h", [128, 8, 512], F32)
            rh = Res()
            sq = self.sb(es, "z_sq", [128, 8, 512], BF16); rsq = Res()
            rstd = self.sb(es, "z_rstd", [128, 512], F32); rrs = Res()
            y = self.sb(es, "z_y", [128, 8, 512], F32); ry = Res()
            pss = self.ps(es, "z_pss", [128, 512]); rpss = Res()
            pt = [self.ps(es, "z_pt%d" % i, [128, 8, 128]) for i in range(2)]; rpt = Rot(2)
            o = [self.sb(es, "z_o%d" % i, [128, D], F32) for i in range(2)]; ro = Rot(2)
            hT = self.dr["hT"]; rhT = self.rr["hT"]
            for (t0, n) in self.chunks(False):
                P.dma("sp", hc[:, :, :n], hT[:, :, t0:t0 + n].rearrange("k p t -> p k t"), reads=[rhT], writes=[rh])
                P.op("act", lambda e: e.activation(out=sq[:, :, :n], in_=hc[:, :, :n], func=AF.Square), reads=[rh], writes=[rsq])
                for k in range(8):
                    P.op("pe", lambda e, k=k: e.matmul(pss[:, :n], lhsT=self.ones[:], rhs=sq[:, k, :n], start=(k == 0), stop=(k == 7)), reads=[rsq, self.rones], writes=[rpss], chain=True)
                P.op("act", lambda e: e.activation(out=rstd[:, :n], in_=pss[:, :n], func=AF.Sqrt, bias=self.epsb[:, 0:1], scale=1.0 / D), reads=[rpss, self.reps], writes=[rrs])
                P.op("dve", lambda e: e.reciprocal(out=rstd[:, :n], in_=rstd[:, :n]), reads=[rrs], writes=[rrs])
                for k in range(8):
                    P.op("dve", lambda e, k=k: e.scalar_tensor_tensor(out=y[:, k, :n], in0=hc[:, k, :n], scalar=fw[:, k:k + 1], in1=rstd[:, :n], op0=ALU.mult, op1=ALU.mult),
                         reads=[rh, rfw, rrs], writes=[ry])
                for tt in range(n // 128):
                    ip, rp_ = rpt.next(); io, ro_ = ro.next()
                    for k in range(8):
                        P.op("pe", lambda e, ip=ip, k=k, tt=tt: e.transpose(out=pt[ip][:, k, :], in_=y[:, k, tt * 128:(tt + 1) * 128], identity=ident[:]),
                             reads=[ry, rid], writes=[rp_], chain=True)
                    P.op("act", lambda e, ip=ip, io=io: e.activation(out=o[io][:, 0:512], in_=pt[ip][:, 0:4, :].rearrange("p k t -> p (k t)"), func=AF.Copy), reads=[rp_], writes=[ro_])
                    P.op("dve", lambda e, ip=ip, io=io: e.tensor_copy(out=o[io][:, 512:1024], in_=pt[ip][:, 4:8, :].rearrange("p k t -> p (k t)")), reads=[rp_], writes=[ro_])
                    P.dma("sp", self.dr["out"][t0 + tt * 128:t0 + (tt + 1) * 128, :], o[io][:], reads=[ro_], writes=[self.rr["out"]])

    def build(self):
        nc, P = self.nc, self.P
        self.declare()
        with ExitStack() as es:
            P.open()
            self.mod = self.sb(es, "g_mod", [128, NMOD, 8, 2], F32); self.rmod = Res()
            self.macc = self.sb(es, "g_macc", [128, 72, 2], F32); self.rmacc = Res()
            self.ones = self.sb(es, "g_ones", [128, 128], BF16); self.rones = Res()
            self.epsb = self.sb(es, "g_eps", [128, 1], F32); self.reps = Res()
            P.op("dve", lambda e: e.memset(self.ones[:], 1.0), writes=[self.rones])
            P.op("dve", lambda e: e.memset(self.epsb[:], EPS), writes=[self.reps])
            self.hpi = self.sb(es, "g_hpi", [128, 1], F32); self.rhpi = Res()
            P.op("dve", lambda e: e.memset(self.hpi[:], math.pi / 2), writes=[self.rhpi])
            ph = self.phases

            def on(name):
                return ph is None or name in ph
            if on("init"):
                self.phase_init()
                P.barrier()
            for l in range(self.nlayers):
                last = (l == self.depth - 1)
                if on("mod"):
                    self.phase_mod(l)
                    if self.debug:
                        P.dma("sp", self.dr["dbg_mod"][l], self.mod[:].rearrange("p a b c -> p (a b c)"), reads=[self.rmod], writes=[self.rr["dbg_mod"]])
                    P.barrier()
                if on("ffn1"):
                    self.phase_ffn(l, "ffn1", 0, True)
                    P.barrier()
                if on("mix"):
                    self.phase_mix(l, not last)
                    P.barrier()
                if on("ffn2"):
                    self.phase_ffn(l, "ffn2", 6, not last)
                    P.barrier()
            if on("final"):
                self.phase_final()
            P.wait_all("sp", list(self.rr.values()))
            P.emit()
            P.close()
        return nc

    def phase_mix(self, l, need_ctx):
        ph = self.phases

        def on(name):
            return ph is None or name in ph
        if on("proj"):
            self.phase_proj(l); self.P.barrier()
        if on("mla"):
            self.phase_mla(l, need_ctx); self.P.barrier()
        if on("swa"):
            self.phase_swa(l, need_ctx); self.P.barrier()
        if on("na"):
            self.phase_na(l, need_ctx); self.P.barrier()
        if on("s5"):
            self.phase_s5(l, need_ctx); self.P.barrier()
        if on("merge"):
            self.phase_merge(l, need_ctx)


def host_consts(TL):
    T = TL + CTX
    pos = np.arange(TL)
    rows = (pos // 64).astype(np.float32)
    cols = (pos % 64).astype(np.float32)

    def tab(dim):
        half = dim // 2
        nf = half // 2
        inv = (10000.0 ** (-np.arange(nf, dtype=np.float32) / nf)).astype(np.float32)
        cos = np.ones((dim, T), np.float32)
        sin = np.zeros((dim, T), np.float32)
        for dd in range(dim):
            p = rows if dd < half else cols
            j = dd % nf
            ang = (p * inv[j]).astype(np.float32)
            sign = -1.0 if (dd % half) < nf else 1.0
            cos[dd, :TL] = np.cos(ang)
            sin[dd, :TL] = sign * np.sin(ang)
        return cos, sin
    c64, s64 = tab(64)
    c32, s32 = tab(32)
    k = {}
    k["k_ident"] = np.eye(128, dtype=np.float32)
    k["k_cos64"] = np.concatenate([c64, c64], 0)
    k["k_sin64"] = np.concatenate([s64, s64], 0)
    k["k_cos64q"] = (k["k_cos64"] * np.float32(0.125)).astype(np.float32)
    k["k_sin64q"] = (k["k_sin64"] * np.float32(0.125)).astype(np.float32)
    sq = np.float32(96.0 ** -0.5)
    k["k_cos32q"] = np.concatenate([np.ones((64, T), np.float32), c32 * sq], 0).astype(np.float32)
    k["k_sin32q"] = np.concatenate([np.zeros((64, T), np.float32), s32 * sq], 0).astype(np.float32)
    k["k_cos32k"] = c32
    k["k_sin32k"] = s32
    jl = np.arange(128)[:, None]
    il = np.arange(128)[None, :]
    k["k_mprev"] = np.tile((jl >= il).astype(np.float32), (1, 4))
    k["k_mnext"] = np.tile((jl <= il).astype(np.float32), (1, 4))
    kc = np.arange(64)[:, None]
    qc = np.arange(64)[None, :]
    c0 = np.clip(qc - 8, 0, 48)
    win = ((kc >= c0) & (kc < c0 + 16))
    G = np.zeros((31, 64, 64), np.float32)
    for dc in range(31):
        G[dc] = ((kc - qc + 15) == dc) & win
    k["k_G"] = G.reshape(31, 4096)
    k["k_cmask"] = np.tile(win.astype(np.float32).reshape(1, 4096), (15, 1))
    return k


def perm_swap(n_heads, dim):
    half = dim // 2
    nf = half // 2
    idx = np.arange(n_heads * dim)
    out = idx.copy()
    for h in range(n_heads):
        for dd in range(dim):
            partner = dd + nf if (dd % half) < nf else dd - nf
            out[h * dim + dd] = h * dim + partner
    return out


def host_layout(inputs, b, TL):
    m = {}
    m["x"] = np.ascontiguousarray(inputs["x"][b, :TL])
    m["c"] = np.ascontiguousarray(inputs["c"][b])
    m["ctx"] = np.ascontiguousarray(inputs["ctx"][b])
    for n in ("c_ctx", "ada_w", "ada_b", "ffn1_norm", "ffn1_w_gate", "ffn1_w_up", "ffn1_w_down", "mix_norm", "w_in",
              "na_rpb", "swa_sink", "s5_lambda_re", "s5_lambda_im", "s5_log_dt", "s5_b_re", "s5_b_im", "s5_c_re", "s5_c_im",
              "s5_d", "s5_glu_w", "s5_glu_b", "mla_q_norm", "mla_w_uq", "mla_kv_norm", "mla_w_ukv", "w_branch", "w_out",
              "ffn2_norm", "ffn2_w_gate", "ffn2_w_up", "ffn2_w_down", "final_norm"):
        m[n] = np.ascontiguousarray(inputs[n])
    w_in = inputs["w_in"]
    p64q = perm_swap(8, 64)
    p64k = perm_swap(2, 64)
    p32 = perm_swap(1, 32)
    m["w_in_sw"] = np.ascontiguousarray(np.concatenate([w_in[:, :, 1536:2048][:, :, p64q], w_in[:, :, 2048:2176][:, :, p64k],
                                                        w_in[:, :, 3200:3232][:, :, p32]], axis=2))
    wuq = inputs["mla_w_uq"]
    pq = np.arange(768)
    for h in range(8):
        pq[h * 96 + 64:h * 96 + 96] = h * 96 + 64 + p32
    m["mla_w_uq_sw"] = np.ascontiguousarray(wuq[:, :, pq])
    return m


_CACHE = {}


def kernel(**inputs):
    TL = inputs["x"].shape[1]
    depth = inputs["ada_w"].shape[0]
    B = inputs["x"].shape[0]
    inputs = {k: np.asarray(v) for k, v in inputs.items()}
    kb = K(TL, depth)
    nc = kb.build()
    consts = host_consts(TL)
    in_maps = []
    for core in range(8):
        m = host_layout(inputs, core % B, TL)
        m.update(consts)
        in_maps.append(m)
    res = run_bass_kernel_spmd(nc, in_maps, core_ids=list(range(8)))
    out = np.stack([res.results[b]["out"] for b in range(B)], axis=0)
    return out.astype(np.float32)


L = 256
CH = 256


def phase_s5(self, l, need_ctx):
    P, T, TL = self.P, self.T, self.TL
    dr, rr = self.dr, self.rr
    NCH = TL // L
    with ExitStack() as es:
        ident = self.sb(es, "s_id", [128, 128], F32); rid = Res()
        P.dma("sp", ident[:], dr["k_ident"][:, :], writes=[rid])
        dvec, rdv = self.load_vec(es, "s_d", dr["s5_d"][l], 4)
        uT = self.sb(es, "s_u", [128, T], BF16); ru = Res()
        names = ["lr", "li", "ldt", "dt", "a", "th", "r", "c", "s", "cc", "ss", "cs", "nr", "ni", "den", "kr", "ki", "t0", "t1"]
        pr = {nm: self.sb(es, "s_p_" + nm, [128, 4], F32) for nm in names}
        rp = Res()
        wc = self.sb(es, "s_wc", [128, 9, 4], F32); ws = self.sb(es, "s_ws", [128, 9, 4], F32)
        BDr = self.sb(es, "s_BDr", [128, 4, 128], F32); BDi = self.sb(es, "s_BDi", [128, 4, 128], F32); rBD = Res()
        CDr = self.sb(es, "s_CDr", [128, 4, 128], F32); CDi = self.sb(es, "s_CDi", [128, 4, 128], F32); rCD = Res()
        BBr = self.sb(es, "s_BBr", [128, 4, 128], F32); BBi = self.sb(es, "s_BBi", [128, 4, 128], F32); rBB = Res()
        tB = self.sb(es, "s_tB", [128, 128], F32); rtB = Res()
        WBr = self.sb(es, "s_WBr", [128, 4, 128], BF16); WBi = self.sb(es, "s_WBi", [128, 4, 128], BF16)
        WCr = self.sb(es, "s_WCr", [128, 4, 128], BF16); WCi = self.sb(es, "s_WCi", [128, 4, 128], BF16); rWt = Res()
        cE = self.sb(es, "s_cE", [128, 4, L], F32); sE = self.sb(es, "s_sE", [128, 4, L], F32); rE = Res()
        tE = self.sb(es, "s_tE", [128, L], F32); rtE = Res()
        init_re = self.sb(es, "s_ire", [128, 4], F32); init_im = self.sb(es, "s_iim", [128, 4], F32); rinit = Res()
        tI = self.sb(es, "s_tI", [128, 2], F32); rtI = Res()
        tt = [[self.sb(es, "s_t%d_%d" % (j, i), [128, L], F32) for i in range(2)] for j in range(4)]
        rtt = [Rot(2) for j in range(4)]
        vv = [[self.sb(es, "s_v%d_%d" % (j, i), [128, L], F32) for i in range(2)] for j in range(2)]
        rvv = [Rot(2) for j in range(2)]
        gg = [[self.sb(es, "s_g%d_%d" % (j, i), [128, L], F32) for i in range(2)] for j in range(2)]
        rgg = [Rot(2) for j in range(2)]
        mm_ = [[self.sb(es, "s_m%d_%d" % (j, i), [128, L], F32) for i in range(2)] for j in range(4)]
        rmm = [Rot(2) for j in range(4)]
        hh = [[self.sb(es, "s_h%d_%d" % (j, i), [128, L], BF16) for i in range(2)] for j in range(2)]
        rhh = [Rot(2) for j in range(2)]
        ych = [self.sb(es, "s_y%d" % i, [128, L], F32) for i in range(2)]; rych = Rot(2)
        yprev = [self.sb(es, "s_yp%d" % i, [128, L], F32) for i in range(2)]; ryp = Rot(2)
        pbu = [self.ps(es, "s_pbu%d" % i, [128, 2, L]) for i in range(2)]; rpbu = Rot(2)
        py = [self.ps(es, "s_py%d" % i, [128, 512]) for i in range(2)]; rpy = Rot(2)
        ptr = [self.ps(es, "s_ptr%d" % i, [128, 512]) for i in range(2)]; rptr = Rot(2)

        P.op("pool", lambda e: e.memset(BDr[:], 0.0), writes=[rBD]); P.op("pool", lambda e: e.memset(BDi[:], 0.0), writes=[rBD])
        P.op("pool", lambda e: e.memset(CDr[:], 0.0), writes=[rCD]); P.op("pool", lambda e: e.memset(CDi[:], 0.0), writes=[rCD])

        def tt_op(eng, out, a, b, op, reads, writes):
            P.op(eng, lambda e: e.tensor_tensor(out=out, in0=a, in1=b, op=op), reads=reads, writes=writes)

        for d in range(2):
            for fc in range(4):
                for ti in range(4):
                    for gl in range(2):
                        g = (fc * 4 + ti) * 2 + gl
                        ps_ = slice(gl * 64, gl * 64 + 64)
                        P.dma("sp", pr["lr"][ps_, ti:ti + 1], dr["s5_lambda_re"][l, d, g, :].rearrange("(p o) -> p o", o=1), writes=[rp], accum=True)
                        P.dma("sp", pr["li"][ps_, ti:ti + 1], dr["s5_lambda_im"][l, d, g, :].rearrange("(p o) -> p o", o=1), writes=[rp], accum=True)
                        P.dma("sp", pr["ldt"][ps_, ti:ti + 1], dr["s5_log_dt"][l, d, g:g + 1].partition_broadcast(64), writes=[rp], accum=True)
                        fo = (ti * 2 + gl) * 16
                        P.dma("sp", BDr[ps_, ti, fo:fo + 16], dr["s5_b_re"][l, d, g], writes=[rBD], accum=True)
                        P.dma("sp", BDi[ps_, ti, fo:fo + 16], dr["s5_b_im"][l, d, g], writes=[rBD], accum=True)
                        P.dma("sp", CDr[fo:fo + 16, ti, ps_], dr["s5_c_re"][l, d, g], writes=[rCD], accum=True)
                        P.dma("sp", CDi[fo:fo + 16, ti, ps_], dr["s5_c_im"][l, d, g], writes=[rCD], accum=True)
                R_ = [rp]
                p = pr
                P.op("act", lambda e: e.activation(out=p["dt"][:], in_=p["ldt"][:], func=AF.Exp), reads=R_, writes=R_)
                tt_op("dve", p["a"][:], p["lr"][:], p["dt"][:], ALU.mult, R_, R_)
                tt_op("dve", p["th"][:], p["li"][:], p["dt"][:], ALU.mult, R_, R_)
                P.op("act", lambda e: e.activation(out=p["r"][:], in_=p["a"][:], func=AF.Exp), reads=R_, writes=R_)
                P.op("act", lambda e: e.activation(out=p["s"][:], in_=p["th"][:], func=AF.Sin, scale=1.0 / 32), reads=R_, writes=R_)
                P.op("act", lambda e: e.activation(out=p["c"][:], in_=p["th"][:], func=AF.Sin, scale=1.0 / 32, bias=self.hpi[:, 0:1]), reads=R_ + [self.rhpi], writes=R_)
                for _ in range(5):
                    tt_op("dve", p["cc"][:], p["c"][:], p["c"][:], ALU.mult, R_, R_)
                    tt_op("dve", p["ss"][:], p["s"][:], p["s"][:], ALU.mult, R_, R_)
                    tt_op("dve", p["cs"][:], p["c"][:], p["s"][:], ALU.mult, R_, R_)
                    tt_op("dve", p["c"][:], p["cc"][:], p["ss"][:], ALU.subtract, R_, R_)
                    P.op("dve", lambda e: e.tensor_scalar(out=p["s"][:], in0=p["cs"][:], scalar1=2.0, scalar2=None, op0=ALU.mult), reads=R_, writes=R_)
                P.op("dve", lambda e: e.tensor_copy(out=wc[:, 0, :], in_=p["c"][:]), reads=R_, writes=R_)
                P.op("dve", lambda e: e.tensor_copy(out=ws[:, 0, :], in_=p["s"][:]), reads=R_, writes=R_)
                for k in range(8):
                    tt_op("dve", p["cc"][:], wc[:, k, :], wc[:, k, :], ALU.mult, R_, R_)
                    tt_op("dve", p["ss"][:], ws[:, k, :], ws[:, k, :], ALU.mult, R_, R_)
                    tt_op("dve", p["cs"][:], wc[:, k, :], ws[:, k, :], ALU.mult, R_, R_)
                    tt_op("dve", wc[:, k + 1, :], p["cc"][:], p["ss"][:], ALU.subtract, R_, R_)
                    P.op("dve", lambda e, k=k: e.tensor_scalar(out=ws[:, k + 1, :], in0=p["cs"][:], scalar1=2.0, scalar2=None, op0=ALU.mult), reads=R_, writes=R_)
                tt_op("dve", p["nr"][:], p["r"][:], p["c"][:], ALU.mult, R_, R_)
                P.op("dve", lambda e: e.tensor_scalar(out=p["nr"][:], in0=p["nr"][:], scalar1=-1.0, scalar2=None, op0=ALU.add), reads=R_, writes=R_)
                tt_op("dve", p["ni"][:], p["r"][:], p["s"][:], ALU.mult, R_, R_)
                tt_op("dve", p["cc"][:], p["lr"][:], p["lr"][:], ALU.mult, R_, R_)
                tt_op("dve", p["ss"][:], p["li"][:], p["li"][:], ALU.mult, R_, R_)
                tt_op("dve", p["den"][:], p["cc"][:], p["ss"][:], ALU.add, R_, R_)
                P.op("dve", lambda e: e.reciprocal(out=p["den"][:], in_=p["den"][:]), reads=R_, writes=R_)
                tt_op("dve", p["t0"][:], p["nr"][:], p["lr"][:], ALU.mult, R_, R_)
                tt_op("dve", p["t1"][:], p["ni"][:], p["li"][:], ALU.mult, R_, R_)
                tt_op("dve", p["kr"][:], p["t0"][:], p["t1"][:], ALU.add, R_, R_)
                tt_op("dve", p["kr"][:], p["kr"][:], p["den"][:], ALU.mult, R_, R_)
                tt_op("dve", p["t0"][:], p["ni"][:], p["lr"][:], ALU.mult, R_, R_)
                tt_op("dve", p["t1"][:], p["nr"][:], p["li"][:], ALU.mult, R_, R_)
                tt_op("dve", p["ki"][:], p["t0"][:], p["t1"][:], ALU.subtract, R_, R_)
                tt_op("dve", p["ki"][:], p["ki"][:], p["den"][:], ALU.mult, R_, R_)
                for ti in range(4):
                    P.op("dve", lambda e, ti=ti: e.tensor_scalar(out=tB[:], in0=BDi[:, ti, :], scalar1=p["ki"][:, ti:ti + 1], scalar2=None, op0=ALU.mult), reads=[rBD, rp], writes=[rtB])
                    P.op("dve", lambda e, ti=ti: e.scalar_tensor_tensor(out=BBr[:, ti, :], in0=BDr[:, ti, :], scalar=p["kr"][:, ti:ti + 1], in1=tB[:], op0=ALU.mult, op1=ALU.subtract),
                         reads=[rBD, rp, rtB], writes=[rBB])
                    P.op("dve", lambda e, ti=ti: e.tensor_scalar(out=tB[:], in0=BDr[:, ti, :], scalar1=p["ki"][:, ti:ti + 1], scalar2=None, op0=ALU.mult), reads=[rBD, rp, rBB], writes=[rtB])
                    P.op("dve", lambda e, ti=ti: e.scalar_tensor_tensor(out=BBi[:, ti, :], in0=BDi[:, ti, :], scalar=p["kr"][:, ti:ti + 1], in1=tB[:], op0=ALU.mult, op1=ALU.add),
                         reads=[rBD, rp, rtB], writes=[rBB])
                for ti in range(4):
                    for (src, rsrc, dst, sc) in ((BBr, rBB, WBr, 1.0), (BBi, rBB, WBi, 1.0), (CDr, rCD, WCr, 1.0), (CDi, rCD, WCi, -1.0)):
                        ip, rpt = rptr.next()
                        P.op("pe", lambda e, ip=ip, src=src, ti=ti: e.transpose(out=ptr[ip][:, 0:128], in_=src[:, ti, :], identity=ident[:]), reads=[rsrc, rid], writes=[rpt], chain=True)
                        P.op("act", lambda e, ip=ip, dst=dst, ti=ti, sc=sc: e.activation(out=dst[:, ti, :], in_=ptr[ip][:, 0:128], func=AF.Copy, scale=sc), reads=[rpt], writes=[rWt])
                P.op("pool", lambda e: e.memset(cE[:, :, 0:1], 1.0), writes=[rE])
                P.op("pool", lambda e: e.memset(sE[:, :, 0:1], 0.0), writes=[rE])
                for k in range(8):
                    m = 1 << k
                    for ti in range(4):
                        P.op("dve", lambda e, ti=ti, m=m, k=k: e.tensor_scalar(out=tE[:, 0:m], in0=sE[:, ti, 0:m], scalar1=ws[:, k, ti:ti + 1], scalar2=None, op0=ALU.mult), reads=[rE, rp], writes=[rtE])
                        P.op("dve", lambda e, ti=ti, m=m, k=k: e.scalar_tensor_tensor(out=cE[:, ti, m:2 * m], in0=cE[:, ti, 0:m], scalar=wc[:, k, ti:ti + 1], in1=tE[:, 0:m], op0=ALU.mult, op1=ALU.subtract),
                             reads=[rE, rp, rtE], writes=[rE])
                        P.op("dve", lambda e, ti=ti, m=m, k=k: e.tensor_scalar(out=tE[:, 0:m], in0=sE[:, ti, 0:m], scalar1=wc[:, k, ti:ti + 1], scalar2=None, op0=ALU.mult), reads=[rE, rp], writes=[rtE])
                        P.op("dve", lambda e, ti=ti, m=m, k=k: e.scalar_tensor_tensor(out=sE[:, ti, m:2 * m], in0=cE[:, ti, 0:m], scalar=ws[:, k, ti:ti + 1], in1=tE[:, 0:m], op0=ALU.mult, op1=ALU.add),
                             reads=[rE, rp, rtE], writes=[rE])
                P.dma("sp", uT[:], dr["u16"][fc], reads=[rr["u16"]], writes=[ru])
                P.op("dve", lambda e: e.memset(init_re[:], 0.0), writes=[rinit])
                P.op("dve", lambda e: e.memset(init_im[:], 0.0), writes=[rinit])
                order = [NCH] + (list(range(NCH)) if d == 0 else list(range(NCH - 1, -1, -1)))
                items = [(ci, ti) for ci in order for ti in range(4)]
                stt_ = {}
                last = (L - 1) if d == 0 else 0

                def views(ti):
                    if d == 0:
                        return cE[:, ti, :], sE[:, ti, :]
                    return cE[:, ti, ::-1], sE[:, ti, ::-1]

                def stage_a(it):
                    ci, ti = items[it]
                    t0 = ci * L
                    S = stt_[it] = {}
                    ib, rb = rpbu.next()
                    P.op("pe", lambda e, ib=ib, ti=ti, t0=t0: e.matmul(pbu[ib][:, 0, :], lhsT=WBr[:, ti, :], rhs=uT[:, t0:t0 + L], start=True, stop=True), reads=[rWt, ru], writes=[rb], chain=True)
                    P.op("pe", lambda e, ib=ib, ti=ti, t0=t0: e.matmul(pbu[ib][:, 1, :], lhsT=WBi[:, ti, :], rhs=uT[:, t0:t0 + L], start=True, stop=True), reads=[rWt, ru], writes=[rb], chain=True)
                    cv, sv = views(ti)
                    bre, bim = pbu[ib][:, 0, :], pbu[ib][:, 1, :]
                    ids = [rtt[j].next() for j in range(4)]
                    tl = [tt[j][ids[j][0]] for j in range(4)]
                    rl = [ids[j][1] for j in range(4)]
                    tt_op("dve", tl[0][:], bre, cv, ALU.mult, [rb, rE], [rl[0]])
                    tt_op("dve", tl[1][:], bim, sv, ALU.mult, [rb, rE], [rl[1]])
                    tt_op("dve", tl[2][:], bim, cv, ALU.mult, [rb, rE], [rl[2]])
                    tt_op("dve", tl[3][:], bre, sv, ALU.mult, [rb, rE], [rl[3]])
                    (i0_, rv0), (i1_, rv1) = rvv[0].next(), rvv[1].next()
                    vre, vim = vv[0][i0_], vv[1][i1_]
                    tt_op("pool", vre[:], tl[0][:], tl[1][:], ALU.add, [rl[0], rl[1]], [rv0])
                    tt_op("pool", vim[:], tl[2][:], tl[3][:], ALU.subtract, [rl[2], rl[3]], [rv1])
                    S.update(vre=vre, vim=vim, rv0=rv0, rv1=rv1)

                def stage_b(it):
                    ci, ti = items[it]
                    S = stt_[it]
                    vre, vim, rv0, rv1 = S["vre"], S["vim"], S["rv0"], S["rv1"]
                    cv, sv = views(ti)
                    (j0, rg0), (j1, rg1) = rgg[0].next(), rgg[1].next()
                    gre, gim = gg[0][j0], gg[1][j1]
                    rbc = pr["r"][:, ti:ti + 1].to_broadcast([128, L])
                    if d == 0:
                        go_r, go_i, vi_r, vi_i = gre[:], gim[:], vre[:], vim[:]
                    else:
                        go_r, go_i, vi_r, vi_i = gre[:, ::-1], gim[:, ::-1], vre[:, ::-1], vim[:, ::-1]
                    P.op("dve", lambda e, go_r=go_r, vi_r=vi_r, ti=ti, rbc=rbc: e.tensor_tensor_scan(out=go_r, data0=rbc, data1=vi_r, initial=init_re[:, ti:ti + 1], op0=ALU.mult, op1=ALU.add),
                         reads=[rv0, rp, rinit], writes=[rg0])
                    P.op("dve", lambda e, go_i=go_i, vi_i=vi_i, ti=ti, rbc=rbc: e.tensor_tensor_scan(out=go_i, data0=rbc, data1=vi_i, initial=init_im[:, ti:ti + 1], op0=ALU.mult, op1=ALU.add),
                         reads=[rv1, rp, rinit], writes=[rg1])
                    P.op("dve", lambda e, gim=gim, ti=ti, last=last: e.tensor_scalar(out=tI[:, 0:1], in0=gim[:, last:last + 1], scalar1=ws[:, 8, ti:ti + 1], scalar2=None, op0=ALU.mult), reads=[rg1, rp], writes=[rtI])
                    P.op("dve", lambda e, gim=gim, ti=ti, last=last: e.tensor_scalar(out=tI[:, 1:2], in0=gim[:, last:last + 1], scalar1=wc[:, 8, ti:ti + 1], scalar2=None, op0=ALU.mult), reads=[rg1, rp], writes=[rtI])
                    P.op("dve", lambda e, gre=gre, ti=ti, last=last: e.scalar_tensor_tensor(out=init_re[:, ti:ti + 1], in0=gre[:, last:last + 1], scalar=wc[:, 8, ti:ti + 1], in1=tI[:, 0:1], op0=ALU.mult, op1=ALU.subtract),
                         reads=[rg0, rp, rtI], writes=[rinit])
                    P.op("dve", lambda e, gre=gre, ti=ti, last=last: e.scalar_tensor_tensor(out=init_im[:, ti:ti + 1], in0=gre[:, last:last + 1], scalar=ws[:, 8, ti:ti + 1], in1=tI[:, 1:2], op0=ALU.mult, op1=ALU.add),
                         reads=[rg0, rp, rtI], writes=[rinit])
                    mids = [rmm[j].next() for j in range(4)]
                    ml = [mm_[j][mids[j][0]] for j in range(4)]
                    rml = [mids[j][1] for j in range(4)]
                    tt_op("pool", ml[0][:], gre[:], cv, ALU.mult, [rg0, rE], [rml[0]])
                    tt_op("pool", ml[1][:], gim[:], sv, ALU.mult, [rg1, rE], [rml[1]])
                    tt_op("pool", ml[2][:], gre[:], sv, ALU.mult, [rg0, rE], [rml[2]])
                    tt_op("pool", ml[3][:], gim[:], cv, ALU.mult, [rg1, rE], [rml[3]])
                    S.update(ml=ml, rml=rml)

                cur = {}

                def stage_c(it):
                    ci, ti = items[it]
                    t0 = ci * L
                    S = stt_.pop(it)
                    ml, rml = S["ml"], S["rml"]
                    if ti == 0:
                        cur["iy"], cur["ry"] = rpy.next()
                    iy, ry = cur["iy"], cur["ry"]
                    (k0, rh0), (k1, rh1) = rhh[0].next(), rhh[1].next()
                    hre, him = hh[0][k0], hh[1][k1]
                    tt_op("dve", hre[:], ml[0][:], ml[1][:], ALU.subtract, [rml[0], rml[1]], [rh0])
                    tt_op("dve", him[:], ml[2][:], ml[3][:], ALU.add, [rml[2], rml[3]], [rh1])
                    P.op("pe", lambda e, iy=iy, ti=ti, hre=hre: e.matmul(py[iy][:, :L], lhsT=WCr[:, ti, :], rhs=hre[:], start=(ti == 0), stop=False), reads=[rWt, rh0], writes=[ry], chain=True)
                    P.op("pe", lambda e, iy=iy, ti=ti, him=him: e.matmul(py[iy][:, :L], lhsT=WCi[:, ti, :], rhs=him[:], start=False, stop=(ti == 3)), reads=[rWt, rh1], writes=[ry], chain=True)
                    if ti == 3:
                        io, ro = rych.next()
                        if d == 0:
                            P.op("dve", lambda e, io=io, iy=iy, t0=t0, fc=fc: e.scalar_tensor_tensor(out=ych[io][:], in0=uT[:, t0:t0 + L], scalar=dvec[:, fc:fc + 1], in1=py[iy][:, :L], op0=ALU.mult, op1=ALU.add),
                                 reads=[ru, rdv, ry], writes=[ro])
                        else:
                            ipv, rpv = ryp.next()
                            P.dma("sp", yprev[ipv][:], dr["ys5"][fc, :, t0:t0 + L], reads=[rr["ys5"]], writes=[rpv])
                            tt_op("dve", ych[io][:], py[iy][:, :L], yprev[ipv][:], ALU.add, [ry, rpv], [ro])
                        P.dma("sp", dr["ys5"][fc, :, t0:t0 + L], ych[io][:], reads=[ro], writes=[rr["ys5"]])

                nit = len(items)
                for step in range(nit + 2):
                    if step < nit:
                        stage_a(step)
                    if 0 <= step - 1 < nit:
                        stage_b(step - 1)
                    if 0 <= step - 2 < nit:
                        stage_c(step - 2)
    self.P.barrier()
    with ExitStack() as es:
        Wg = self.sb(es, "sg_w", [128, 4, 512], BF16); rWg = Res()
        for k in range(4):
            P.dma("pool", Wg[:, k, :], dr["s5_glu_w"][l, k * 128:(k + 1) * 128, :], writes=[rWg], accum=True)
        gb, rgb = self.load_vec(es, "sg_b", dr["s5_glu_b"][l], 4)
        y = self.sb(es, "sg_y", [128, 4, L], F32); ry_ = Res()
        x2 = self.sb(es, "sg_x2", [128, 4, L], F32); rx2 = Res()
        g32 = self.sb(es, "sg_g32", [128, 4, L], F32); rg32 = Res()
        g16 = self.sb(es, "sg_g16", [128, 4, L], BF16); rg16 = Res()
        sz = [self.sb(es, "sg_sz%d" % i, [128, L], F32) for i in range(2)]; rsz = Rot(2)
        st = [self.sb(es, "sg_st%d" % i, [128, L], BF16) for i in range(2)]; rst = Rot(2)
        pz = [self.ps(es, "sg_pz%d" % i, [128, 512]) for i in range(2)]; rpz = Rot(2)
        for (t0, n) in self.chunks(need_ctx):
            P.dma("sp", y[:, :, :n], dr["ys5"][:, :, t0:t0 + n].rearrange("k p t -> p k t"), reads=[rr["ys5"]], writes=[ry_])
            P.op("dve", lambda e: e.tensor_tensor(out=x2[:], in0=y[:], in1=y[:], op=ALU.mult), reads=[ry_], writes=[rx2])
            P.op("dve", lambda e: e.tensor_scalar(out=x2[:], in0=x2[:], scalar1=0.044715, scalar2=1.0, op0=ALU.mult, op1=ALU.add), reads=[rx2], writes=[rx2])
            P.op("dve", lambda e: e.tensor_tensor(out=x2[:], in0=x2[:], in1=y[:], op=ALU.mult), reads=[rx2, ry_], writes=[rx2])
            P.op("act", lambda e: e.activation(out=x2[:], in_=x2[:], func=AF.Sigmoid, scale=2.0 * math.sqrt(2.0 / math.pi)), reads=[rx2], writes=[rx2])
            P.op("dve", lambda e: e.tensor_tensor(out=g32[:], in0=x2[:], in1=y[:], op=ALU.mult), reads=[rx2, ry_], writes=[rg32])
            P.op("act", lambda e: e.activation(out=g16[:], in_=g32[:], func=AF.Copy), reads=[rg32], writes=[rg16])
            for m in range(4):
                ip, rp_ = rpz.next(); isz, rs_ = rsz.next(); ist, rt_ = rst.next()
                for k in range(4):
                    P.op("pe", lambda e, ip=ip, m=m, k=k: e.matmul(pz[ip][:, :n], lhsT=Wg[:, k, m * 128:(m + 1) * 128], rhs=g16[:, k, :n], start=(k == 0), stop=(k == 3)),
                         reads=[rWg, rg16], writes=[rp_], chain=True)
                P.op("act", lambda e, ip=ip, isz=isz, m=m: e.activation(out=sz[isz][:, :n], in_=pz[ip][:, :n], func=AF.Sigmoid, bias=gb[:, m:m + 1], scale=1.0), reads=[rp_, rgb], writes=[rs_])
                P.op("dve", lambda e, isz=isz, ist=ist, m=m: e.tensor_tensor(out=st[ist][:, :n], in0=sz[isz][:, :n], in1=g32[:, m, :n], op=ALU.mult), reads=[rs_, rg32], writes=[rt_])
                P.dma("sp", dr["ymx"][2, m, :, t0:t0 + n], st[ist][:, :n], reads=[rt_], writes=[rr["ymx"]])


def install(K):

    def phase_proj(self, l):
        P, nc, T, TL = self.P, self.nc, self.T, self.TL
        dr, rr = self.dr, self.rr
        with ExitStack() as es:
            W = self.sb(es, "p_w", [128, 8, 7328], BF16); rW = Res()
            Wsw = self.sb(es, "p_wsw", [128, 8, 672], BF16); rWsw = Res()
            for k in range(8):
                P.dma("pool", W[:, k, :], dr["w_in"][l, k * 128:(k + 1) * 128, :], writes=[rW], accum=True)
                P.dma("pool", Wsw[:, k, :], dr["w_in_sw"][l, k * 128:(k + 1) * 128, :], writes=[rWsw], accum=True)
            wuq = self.sb(es, "p_wuq", [128, 2, 768], BF16); wuqs = self.sb(es, "p_wuqs", [128, 2, 768], BF16); rwuq = Res()
            for j in range(2):
                P.dma("pool", wuq[:, j, :], dr["mla_w_uq"][l, j * 128:(j + 1) * 128, :], writes=[rwuq], accum=True)
                P.dma("pool", wuqs[:, j, :], dr["mla_w_uq_sw"][l, j * 128:(j + 1) * 128, :], writes=[rwuq], accum=True)
            wukv = self.sb(es, "p_wukv", [128, 1024], BF16); rwukv = Res()
            P.dma("pool", wukv[:], dr["mla_w_ukv"][l], writes=[rwukv])
            normw, rn = self.load_vec(es, "p_nw", dr["mix_norm"][l], 8)
            qn, rqn = self.load_vec(es, "p_qn", dr["mla_q_norm"][l], 2)
            kvn, rkvn = self.load_vec(es, "p_kvn", dr["mla_kv_norm"][l], 1)
            A, rA = self.make_AS(es, "p", normw, rn, 3, 4)
            hc = self.sb(es, "p_h", [128, 8, 512], F32); rh = Res()
            nT = self.sb(es, "p_nT", [128, 8, 512], BF16); rnT = Res()
            nt = self.norm_tiles(es, "p")
            (sq, rsq, pss, rpss, rstd, rrstd, ones, rones, tmpn, rtmpn) = nt
            c64 = self.sb(es, "p_c64", [128, CH], F32); s64 = self.sb(es, "p_s64", [128, CH], F32)
            c64q = self.sb(es, "p_c64q", [128, CH], F32); s64q = self.sb(es, "p_s64q", [128, CH], F32)
            c32q = self.sb(es, "p_c32q", [96, CH], F32); s32q = self.sb(es, "p_s32q", [96, CH], F32)
            c32k = self.sb(es, "p_c32k", [32, CH], F32); s32k = self.sb(es, "p_s32k", [32, CH], F32)
            rtab = Res()
            cq = self.sb(es, "p_cq", [128, 2, CH], F32); rcq = Res()
            cqn = self.sb(es, "p_cqn", [128, 2, CH], BF16); rcqn = Res()
            ckv = self.sb(es, "p_ckv", [128, CH], F32); rckv = Res()
            ckvn = self.sb(es, "p_ckvn", [128, CH], BF16); rckvn = Res()
            NST = 6
            stg = [self.sb(es, "p_st%d" % i, [128, 512], BF16) for i in range(NST)]; rstg = Rot(NST)
            t1 = [self.sb(es, "p_t1%d" % i, [128, CH], F32) for i in range(2)]; rt1 = Rot(2)
            t2 = [self.sb(es, "p_t2%d" % i, [128, CH], F32) for i in range(2)]; rt2 = Rot(2)
            NPS = 6
            pp = [self.ps(es, "p_ps%d" % i, [128, 512]) for i in range(NPS)]; rpp = Rot(NPS)
            tog = [0]

            def mm(ps_ap, terms, rps, reads):
                nterm = len(terms)
                for i, (lt, rh_) in enumerate(terms):
                    P.op("pe", lambda e, lt=lt, rh_=rh_, i=i: e.matmul(ps_ap, lhsT=lt, rhs=rh_, start=(i == 0), stop=(i == nterm - 1)),
                         reads=reads, writes=[rps], chain=True)

            def evac(out_ap, in_ap, rin, rout, scale=1.0, func=None):
                tog[0] ^= 1
                if func is not None or tog[0]:
                    P.op("act", lambda e: e.activation(out=out_ap, in_=in_ap, func=(func or AF.Copy), scale=scale), reads=[rin], writes=[rout])
                else:
                    P.op("dve", lambda e: e.tensor_scalar(out=out_ap, in0=in_ap, scalar1=scale, scalar2=None, op0=ALU.mult), reads=[rin], writes=[rout])

            for (t0, n) in self.chunks(True):
                s = 0 if t0 < TL else 1
                P.dma("sp", hc[:, :, :n], dr["hT"][:, :, t0:t0 + n].rearrange("k p t -> p k t"), reads=[rr["hT"]], writes=[rh])
                for (tt, nm) in ((c64, "k_cos64"), (s64, "k_sin64"), (c64q, "k_cos64q"), (s64q, "k_sin64q"), (c32q, "k_cos32q"), (s32q, "k_sin32q"),
                                 (c32k, "k_cos32k"), (s32k, "k_sin32k")):
                    P.dma("sp", tt[:, :n], dr[nm][:, t0:t0 + n], writes=[rtab], accum=True)
                self.norm_chunk(nt, hc, rh, n, s, A, rA, 3, nT, rnT)

                def fm(col0, M, Wt=W, rWt=rW):
                    ip, rp = rpp.next()
                    mm(pp[ip][:M, :n], [(Wt[:, k, col0:col0 + M], nT[:, k, :n]) for k in range(8)], rp, [rWt, rnT])
                    return pp[ip], rp

                def out_fm(dst_ap, rdst, ps, rp, M, scale=1.0, func=None):
                    i, rs = rstg.next()
                    evac(stg[i][:M, :n], ps[:M, :n], rp, rs, scale, func)
                    P.dma("sp", dst_ap, stg[i][:M, :n], reads=[rs], writes=[rdst])

                def rope_out(dst_ap, rdst, ps1, rp1, ps2, rp2, cs, sn, lo, hi):
                    i1, r1 = rt1.next(); i2, r2 = rt2.next(); i, rs = rstg.next()
                    P.op("dve", lambda e: e.tensor_tensor(out=t1[i1][lo:hi, :n], in0=ps1[lo:hi, :n], in1=cs[lo:hi, :n], op=ALU.mult), reads=[rp1, rtab], writes=[r1])
                    P.op("dve", lambda e: e.tensor_tensor(out=t2[i2][lo:hi, :n], in0=ps2[lo:hi, :n], in1=sn[lo:hi, :n], op=ALU.mult), reads=[rp2, rtab], writes=[r2])
                    P.op("pool", lambda e: e.tensor_tensor(out=stg[i][lo:hi, :n], in0=t1[i1][lo:hi, :n], in1=t2[i2][lo:hi, :n], op=ALU.add), reads=[r1, r2], writes=[rs])
                    return i, rs

                for j in range(4):
                    ps, rp = fm(128 * j, 128); out_fm(dr["qna"][j, :, t0:t0 + n], rr["qna"], ps, rp, 128, 0.125)
                    ps, rp = fm(512 + 128 * j, 128); out_fm(dr["kna"][j, :, t0:t0 + n], rr["kna"], ps, rp, 128)
                for j in range(4):
                    ps1, rp1 = fm(1536 + 128 * j, 128); ps2, rp2 = fm(128 * j, 128, Wsw, rWsw)
                    i, rs = rope_out(None, None, ps1, rp1, ps2, rp2, c64q, s64q, 0, 128)
                    P.dma("sp", dr["qsw"][j, :, t0:t0 + n], stg[i][:, :n], reads=[rs], writes=[rr["qsw"]])
                ps1, rp1 = fm(2048, 128); ps2, rp2 = fm(512, 128, Wsw, rWsw)
                i, rs = rope_out(None, None, ps1, rp1, ps2, rp2, c64, s64, 0, 128)
                P.dma("sp", dr["ksw"][:, t0:t0 + n], stg[i][:, :n], reads=[rs], writes=[rr["ksw"]])
                for j in range(4):
                    ps, rp = fm(2304 + 128 * j, 128); out_fm(dr["u16"][j, :, t0:t0 + n], rr["u16"], ps, rp, 128)
                for j in range(2):
                    ps, rp = fm(2816 + 128 * j, 128)
                    P.op("act", lambda e, j=j, ps=ps: e.activation(out=cq[:, j, :n], in_=ps[:, :n], func=AF.Copy), reads=[rp], writes=[rcq])
                ps, rp = fm(3072, 128)
                P.op("dve", lambda e, ps=ps: e.tensor_copy(out=ckv[:, :n], in_=ps[:, :n]), reads=[rp], writes=[rckv])
                ps1, rp1 = fm(3200, 32); ps2, rp2 = fm(640, 32, Wsw, rWsw)
                i, rs = rope_out(None, None, ps1, rp1, ps2, rp2, c32k, s32k, 0, 32)
                for h in range(8):
                    P.dma("sp", dr["kml"][h, 64:96, t0:t0 + n], stg[i][0:32, :n], reads=[rs], writes=[rr["kml"]])
                for j in range(32):
                    ps, rp = fm(3232 + 128 * j, 128)
                    out_fm(dr["gat"][j // 8, j % 8, :, t0:t0 + n], rr["gat"], ps, rp, 128, 1.0, AF.Sigmoid)
                for sub in range(n // 128):
                    tk = slice(sub * 128, sub * 128 + 128)
                    i, rs = rstg.next()
                    for hf in range(2):
                        ip, rp = rpp.next()
                        mm(pp[ip][:, :256], [(nT[:, k, tk], W[:, k, 1024 + 256 * hf:1024 + 256 * hf + 256]) for k in range(8)], rp, [rW, rnT])
                        evac(stg[i][:, 256 * hf:256 * hf + 256], pp[ip][:, :256], rp, rs)
                    P.dma("sp", dr["vna"][t0 + sub * 128:t0 + sub * 128 + 128, :], stg[i][:, :], reads=[rs], writes=[rr["vna"]])
                    i, rs = rstg.next(); ip, rp = rpp.next()
                    mm(pp[ip][:, :128], [(nT[:, k, tk], W[:, k, 2176:2304]) for k in range(8)], rp, [rW, rnT])
                    evac(stg[i][:, :128], pp[ip][:, :128], rp, rs)
                    P.dma("sp", dr["vsw"][t0 + sub * 128:t0 + sub * 128 + 128, :], stg[i][:, :128], reads=[rs], writes=[rr["vsw"]])
                P.op("act", lambda e: e.activation(out=sq[:, 0:2, :n], in_=cq[:, :, :n], func=AF.Square), reads=[rcq], writes=[rsq])
                mm(pss[:, :n], [(ones[:], sq[:, j, :n]) for j in range(2)], rpss, [rsq, rones])
                P.op("act", lambda e: e.activation(out=rstd[:, :n], in_=pss[:, :n], func=AF.Sqrt, bias=self.epsb[:, 0:1], scale=1.0 / 256), reads=[rpss, self.reps], writes=[rrstd])
                P.op("dve", lambda e: e.reciprocal(out=rstd[:, :n], in_=rstd[:, :n]), reads=[rrstd], writes=[rrstd])
                for j in range(2):
                    P.op("dve", lambda e, j=j: e.scalar_tensor_tensor(out=cqn[:, j, :n], in0=cq[:, j, :n], scalar=qn[:, j:j + 1], in1=rstd[:, :n], op0=ALU.mult, op1=ALU.mult),
                         reads=[rcq, rqn, rrstd], writes=[rcqn])
                for h in range(8):
                    ip1, rp1 = rpp.next(); ip2, rp2 = rpp.next()
                    mm(pp[ip1][:96, :n], [(wuq[:, j, 96 * h:96 * h + 96], cqn[:, j, :n]) for j in range(2)], rp1, [rwuq, rcqn])
                    mm(pp[ip2][:96, :n], [(wuqs[:, j, 96 * h:96 * h + 96], cqn[:, j, :n]) for j in range(2)], rp2, [rwuq, rcqn])
                    i, rs = rope_out(None, None, pp[ip1], rp1, pp[ip2], rp2, c32q, s32q, 64, 96)
                    P.op("act", lambda e, i=i, ip1=ip1: e.activation(out=stg[i][0:64, :n], in_=pp[ip1][0:64, :n], func=AF.Copy, scale=96.0 ** -0.5), reads=[rp1], writes=[rs])
                    P.dma("sp", dr["qml"][h, :, t0:t0 + n], stg[i][0:96, :n], reads=[rs], writes=[rr["qml"]])
                P.op("act", lambda e: e.activation(out=sq[:, 0, :n], in_=ckv[:, :n], func=AF.Square), reads=[rckv], writes=[rsq])
                mm(pss[:, :n], [(ones[:], sq[:, 0, :n])], rpss, [rsq, rones])
                P.op("act", lambda e: e.activation(out=rstd[:, :n], in_=pss[:, :n], func=AF.Sqrt, bias=self.epsb[:, 0:1], scale=1.0 / 128), reads=[rpss, self.reps], writes=[rrstd])
                P.op("dve", lambda e: e.reciprocal(out=rstd[:, :n], in_=rstd[:, :n]), reads=[rrstd], writes=[rrstd])
                P.op("dve", lambda e: e.scalar_tensor_tensor(out=ckvn[:, :n], in0=ckv[:, :n], scalar=kvn[:, 0:1], in1=rstd[:, :n], op0=ALU.mult, op1=ALU.mult),
                     reads=[rckv, rkvn, rrstd], writes=[rckvn])
                for h in range(8):
                    ip, rp = rpp.next()
                    mm(pp[ip][:64, :n], [(wukv[:, 128 * h:128 * h + 64], ckvn[:, :n])], rp, [rwukv, rckvn])
                    out_fm(dr["kml"][h, 0:64, t0:t0 + n], rr["kml"], pp[ip], rp, 64)
                wv = wukv[:].rearrange("p (h c) -> p h c", c=128)
                for sub in range(n // 128):
                    tk = slice(sub * 128, sub * 128 + 128)
                    i, rs = rstg.next()
                    for hf in range(2):
                        ip, rp = rpp.next()
                        mm(pp[ip][:, :256].rearrange("p (h c) -> p h c", c=64), [(ckvn[:, tk], wv[:, 4 * hf:4 * hf + 4, 64:128])], rp, [rwukv, rckvn])
                        evac(stg[i][:, 256 * hf:256 * hf + 256], pp[ip][:, :256], rp, rs)
                    P.dma("sp", dr["vml"][t0 + sub * 128:t0 + sub * 128 + 128, :], stg[i][:, :], reads=[rs], writes=[rr["vml"]])

    def attn_tiles(self, es, pref):
        st = {}
        st["ps"] = [self.ps(es, pref + "_s%d" % i, [128, 512]) for i in range(3)]; st["rps"] = Rot(3)
        st["po"] = [self.ps(es, pref + "_o%d" % i, [128, 512]) for i in range(2)]; st["rpo"] = Rot(2)
        st["e"] = [self.sb(es, pref + "_e%d" % i, [128, 512], BF16) for i in range(4)]; st["re"] = Rot(4)
        st["rec"] = [self.sb(es, pref + "_r%d" % i, [128, 512], F32) for i in range(2)]; st["rrec"] = Rot(2)
        st["y"] = [self.sb(es, pref + "_y%d" % i, [64, 512], BF16) for i in range(3)]; st["ry"] = Rot(3)
        return st

    def attn_job(self, st, qap, N, tiles, rin, sink=None):
        P = self.P
        io, ro = st["rpo"].next()
        po = st["po"][io]
        nt = len(tiles)
        LA = 2
        sc = []

        def issue_score(tj):
            kap_ = tiles[tj][0]
            ip_, rp_ = st["rps"].next()
            ps_ = st["ps"][ip_]
            P.op("pe", lambda e, ps_=ps_, kap_=kap_: e.matmul(ps_[:, :N], lhsT=kap_, rhs=qap, start=True, stop=True), reads=rin, writes=[rp_], chain=True)
            sc.append((ps_, rp_))
        for tj in range(min(LA, nt)):
            issue_score(tj)
        for ti, (kap, vap, mask, rmask) in enumerate(tiles):
            if ti + LA < nt:
                issue_score(ti + LA)
            ps, rp = sc[ti]
            ie, re_ = st["re"].next()
            et = st["e"][ie]
            P.op("act", lambda e, ps=ps, et=et: e.activation(out=et[:, :N], in_=ps[:, :N], func=AF.Exp), reads=[rp], writes=[re_])
            if mask is not None:
                P.op("dve", lambda e, et=et, mask=mask: e.tensor_tensor(out=et[:, :N], in0=et[:, :N], in1=mask, op=ALU.mult), reads=[re_, rmask], writes=[re_])
            P.op("pe", lambda e, po=po, vap=vap, et=et, ti=ti: e.matmul(po[:, :N], lhsT=vap, rhs=et[:, :N], start=(ti == 0), stop=(ti == nt - 1)),
                 reads=rin + [re_], writes=[ro], chain=True)
        ir, rrc = st["rrec"].next(); iy, ry = st["ry"].next()
        rec = st["rec"][ir]; y = st["y"][iy]
        if sink is not None:
            esink, rsink, blocks = sink
            for (c0, c1, sc) in blocks:
                P.op("dve", lambda e, c0=c0, c1=c1, sc=sc: e.tensor_scalar(out=rec[64:128, c0:c1], in0=po[64:128, c0:c1], scalar1=esink[64:128, sc:sc + 1], scalar2=None, op0=ALU.add),
                     reads=[ro, rsink], writes=[rrc])
            P.op("dve", lambda e: e.reciprocal(out=rec[0:64, :N], in_=rec[64:128, :N]), reads=[rrc], writes=[rrc])
        else:
            P.op("dve", lambda e: e.reciprocal(out=rec[0:64, :N], in_=po[64:128, :N]), reads=[ro], writes=[rrc])
        P.op("dve", lambda e: e.tensor_tensor(out=y[0:64, :N], in0=po[0:64, :N], in1=rec[0:64, :N], op=ALU.mult), reads=[ro, rrc], writes=[ry])
        return y, ry

    def load_vaug(self, vaug, rv, src, h, dv_off, ntile, first):
        P = self.P
        if first:
            P.op("pool", lambda e: e.memset(vaug[:, :, 64:128], 1.0), writes=[rv])
        P.dma("sp", vaug[:, :, 0:64], src[:, dv_off:dv_off + 64].rearrange("(t p) c -> p t c", p=128), writes=[rv])

    def phase_mla(self, l, need_ctx):
        P, T, TL = self.P, self.T, self.TL
        dr, rr = self.dr, self.rr
        NT = T // 128
        with ExitStack() as es:
            st = attn_tiles(self, es, "ml")
            kT = self.sb(es, "ml_k", [96, T], BF16); qT = self.sb(es, "ml_q", [96, T], BF16)
            vaug = self.sb(es, "ml_v", [128, NT, 128], BF16)
            rk, rq, rv = Res(), Res(), Res()
            for h in range(8):
                P.dma("sp", kT[:], dr["kml"][h], reads=[rr["kml"]], writes=[rk])
                P.dma("sp", qT[:], dr["qml"][h], reads=[rr["qml"]], writes=[rq])
                load_vaug(self, vaug, rv, dr["vml"], h, 64 * h, NT, h == 0)
                qch = [(c * 512, 512) for c in range(TL // 512)] + ([(TL, CTX)] if need_ctx else [])
                for (t0, n) in qch:
                    kt = range(NT) if t0 < TL else range(TL // 128, NT)
                    tiles = [(kT[:, j * 128:(j + 1) * 128], vaug[:, j, :], None, None) for j in kt]
                    y, ry = attn_job(self, st, qT[:, t0:t0 + n], n, tiles, [rk, rq, rv])
                    P.dma("sp", dr["ymx"][3, h // 2, (h % 2) * 64:(h % 2) * 64 + 64, t0:t0 + n], y[0:64, :n], reads=[ry], writes=[rr["ymx"]])

    def phase_swa(self, l, need_ctx):
        P, T, TL = self.P, self.T, self.TL
        dr, rr = self.dr, self.rr
        NT = T // 128
        NB = TL // 128
        with ExitStack() as es:
            st = attn_tiles(self, es, "sw")
            kT = self.sb(es, "sw_k", [64, T], BF16)
            qT = self.sb(es, "sw_q", [64, 4, T], BF16)
            vaug = self.sb(es, "sw_v", [128, NT, 128], BF16)
            mp = self.sb(es, "sw_mp", [128, 512], F32); mn = self.sb(es, "sw_mn", [128, 512], F32); rm = Res()
            sk = self.sb(es, "sw_sk", [128, 8], F32); rsk = Res()
            P.dma("sp", mp[:], dr["k_mprev"][:, :], writes=[rm]); P.dma("sp", mn[:], dr["k_mnext"][:, :], writes=[rm])
            P.dma("sp", sk[:], dr["swa_sink"][l, :].partition_broadcast(128), writes=[rsk])
            P.op("act", lambda e: e.activation(out=sk[:], in_=sk[:], func=AF.Exp), reads=[rsk], writes=[rsk])
            rk, rq, rv = Res(), Res(), Res()
            for g in range(2):
                P.dma("sp", kT[:], dr["ksw"][64 * g:64 * g + 64, :], reads=[rr["ksw"]], writes=[rk])
                for hh in range(4):
                    h = 4 * g + hh
                    P.dma("sp", qT[:, hh, :], dr["qsw"][h // 2, (h % 2) * 64:(h % 2) * 64 + 64, :], reads=[rr["qsw"]], writes=[rq])
                load_vaug(self, vaug, rv, dr["vsw"], g, 64 * g, NT, g == 0)
                ctx_tiles = [(kT[:, j * 128:(j + 1) * 128], vaug[:, j, :], None, None) for j in range(NB, NT)]
                for hp in range(2):
                    h0 = 4 * g + 2 * hp
                    blocks = [(0, 128, h0), (128, 256, h0 + 1)]
                    jobs = [(nb * 128, nb) for nb in range(NB)]
                    if need_ctx:
                        jobs += [(TL, -1), (TL + 128, -1)]
                    for (q0, nb) in jobs:
                        tiles = []
                        if nb >= 0:
                            if nb > 0:
                                tiles.append((kT[:, (nb - 1) * 128:nb * 128], vaug[:, nb - 1, :], mp[:, 0:256], rm))
                            tiles.append((kT[:, nb * 128:(nb + 1) * 128], vaug[:, nb, :], None, None))
                            if nb < NB - 1:
                                tiles.append((kT[:, (nb + 1) * 128:(nb + 2) * 128], vaug[:, nb + 1, :], mn[:, 0:256], rm))
                        tiles += ctx_tiles
                        y, ry = attn_job(self, st, qT[:, 2 * hp:2 * hp + 2, q0:q0 + 128], 256, tiles, [rk, rq, rv], sink=(sk, rsk, blocks))
                        for hb in range(2):
                            h = h0 + hb
                            P.dma("sp", dr["ymx"][1, h // 2, (h % 2) * 64:(h % 2) * 64 + 64, q0:q0 + 128], y[0:64, hb * 128:hb * 128 + 128], reads=[ry], writes=[rr["ymx"]])

    def phase_na(self, l, need_ctx):
        P, nc, T, TL = self.P, self.nc, self.T, self.TL
        dr, rr = self.dr, self.rr
        NT = T // 128
        R = TL // 64
        assert R >= 10
        with ExitStack() as es:
            with ExitStack() as es2:
                rpT = self.sb(es2, "na_rpT", [31, 8, 15], F32); rrp = Res()
                Gt = self.sb(es2, "na_G", [31, 4096], F32); cm = self.sb(es2, "na_cm", [15, 4096], F32); rG = Res()
                eb = self.sb(es2, "na_eb", [15, 4096], F32); reb = Res()
                pb = [self.ps(es2, "na_pb%d" % i, [128, 512]) for i in range(2)]; rpb = Rot(2)
                P.dma("sp", rpT[:], dr["na_rpb"][l].rearrange("h r c -> c h r"), writes=[rrp])
                P.dma("sp", Gt[:], dr["k_G"][:, :], writes=[rG]); P.dma("sp", cm[:], dr["k_cmask"][:, :], writes=[rG])
                for h in range(8):
                    for cc in range(16):
                        ip, rp = rpb.next()
                        P.op("pe", lambda e, ip=ip, h=h, cc=cc: e.matmul(pb[ip][:15, :256], lhsT=rpT[:, h, :], rhs=Gt[:, cc * 256:(cc + 1) * 256], start=True, stop=True),
                             reads=[rrp, rG], writes=[rp], chain=True)
                        P.op("act", lambda e, ip=ip, cc=cc: e.activation(out=eb[:, cc * 256:(cc + 1) * 256], in_=pb[ip][:15, :256], func=AF.Exp), reads=[rp], writes=[reb])
                    P.op("dve", lambda e: e.tensor_tensor(out=eb[:], in0=eb[:], in1=cm[:], op=ALU.mult), reads=[reb, rG], writes=[reb])
                    P.dma("sp", dr["ebd"][h].rearrange("r a b -> r (a b)"), eb[:], reads=[reb], writes=[rr["ebd"]])
            P.barrier()
            st = attn_tiles(self, es, "na")
            kT = self.sb(es, "na_k", [64, T], BF16); qT = self.sb(es, "na_q", [64, T], BF16)
            vaug = self.sb(es, "na_v", [128, NT, 128], BF16)
            EB = self.sb(es, "na_EB", [128, 5, 5, 128], F32); rEB = Res()
            P.op("pool", lambda e: e.memset(EB[:], 0.0), writes=[rEB])
            rk, rq, rv = Res(), Res(), Res()
            types = {0: (0, 0), 2: (1, 0), 4: (2, None), 6: (3, 2), 8: (4, 2)}
            for h in range(8):
                P.dma("sp", kT[:], dr["kna"][h // 2, (h % 2) * 64:(h % 2) * 64 + 64, :], reads=[rr["kna"]], writes=[rk])
                P.dma("sp", qT[:], dr["qna"][h // 2, (h % 2) * 64:(h % 2) * 64 + 64, :], reads=[rr["qna"]], writes=[rq])
                load_vaug(self, vaug, rv, dr["vna"], h, 64 * h, NT, h == 0)
                for qrel, (ty, r0r) in types.items():
                    for kt in range(5):
                        for jl in range(2):
                            for ql in range(2):
                                j = 2 * kt + jl
                                r0rel = ql if r0r is None else r0r
                                if not (r0rel <= j <= r0rel + 7):
                                    continue
                                drr = j - (qrel + ql) + 7
                                assert 0 <= drr <= 14
                                P.dma("sp", EB[jl * 64:(jl + 1) * 64, ty, kt, ql * 64:(ql + 1) * 64], dr["ebd"][h, drr], reads=[rr["ebd"]], writes=[rEB], accum=True)
                ctx_tiles = [(kT[:, j * 128:(j + 1) * 128], vaug[:, j, :], None, None) for j in range(TL // 128, NT)]
                for r in range(0, R, 2):
                    ks = min(max(r - 4, 0), R - 10)
                    ty = types[r - ks][0]
                    tiles = [(kT[:, (ks + 2 * kt) * 64:(ks + 2 * kt) * 64 + 128], vaug[:, (ks + 2 * kt) // 2, :], EB[:, ty, kt, :], rEB) for kt in range(5)]
                    tiles += ctx_tiles
                    y, ry = attn_job(self, st, qT[:, r * 64:r * 64 + 128], 128, tiles, [rk, rq, rv])
                    P.dma("sp", dr["ymx"][0, h // 2, (h % 2) * 64:(h % 2) * 64 + 64, r * 64:r * 64 + 128], y[0:64, :128], reads=[ry], writes=[rr["ymx"]])
                if need_ctx:
                    y, ry = attn_job(self, st, qT[:, TL:TL + 256], 256, ctx_tiles, [rk, rq, rv])
                    P.dma("sp", dr["ymx"][0, h // 2, (h % 2) * 64:(h % 2) * 64 + 64, TL:TL + 256], y[0:64, :256], reads=[ry], writes=[rr["ymx"]])

    def phase_merge(self, l, need_ctx):
        P, T, TL = self.P, self.T, self.TL
        dr, rr = self.dr, self.rr
        with ExitStack() as es:
            wb = self.sb(es, "g_wb", [128, 4, 4, 1024], BF16); rwb = Res()
            wo = self.sb(es, "g_wo", [128, 8, 1024], BF16); rwo = Res()
            for nb in range(4):
                for k in range(4):
                    P.dma("pool", wb[:, nb, k, :], dr["w_branch"][l, nb, k * 128:(k + 1) * 128, :], writes=[rwb], accum=True)
            for k in range(8):
                P.dma("pool", wo[:, k, :], dr["w_out"][l, k * 128:(k + 1) * 128, :], writes=[rwo], accum=True)
            yT = self.sb(es, "g_y", [128, 4, 4, CH], BF16); ryT = Res()
            gt = self.sb(es, "g_g", [128, 4, 8, CH], BF16); rgt = Res()
            hc = self.sb(es, "g_h", [128, 8, CH], F32); rh = Res()
            mg = self.sb(es, "g_m", [128, 8, CH], BF16); rmg = Res()
            acc = [self.sb(es, "g_a%d" % i, [128, CH], F32) for i in range(2)]; racc = Rot(2)
            tmp = [self.sb(es, "g_t%d" % i, [128, CH], F32) for i in range(2)]; rtmp = Rot(2)
            pp = [self.ps(es, "g_p%d" % i, [128, 512]) for i in range(4)]; rpp = Rot(4)
            po = [self.ps(es, "g_po%d" % i, [128, 512]) for i in range(2)]; rpo = Rot(2)
            for (t0, n) in self.chunks(need_ctx):
                s = 0 if t0 < TL else 1
                P.dma("sp", hc[:, :, :n], dr["hT"][:, :, t0:t0 + n].rearrange("k p t -> p k t"), reads=[rr["hT"]], writes=[rh])
                for nb in range(4):
                    P.dma("sp", yT[:, nb, :, :n], dr["ymx"][nb, :, :, t0:t0 + n].rearrange("k p t -> p k t"), reads=[rr["ymx"]], writes=[ryT], accum=True)
                    P.dma("sp", gt[:, nb, :, :n], dr["gat"][nb, :, :, t0:t0 + n].rearrange("k p t -> p k t"), reads=[rr["gat"]], writes=[rgt], accum=True)
                for m in range(8):
                    ia, ra = racc.next()
                    for nb in range(4):
                        ip, rp = rpp.next()
                        for k in range(4):
                            P.op("pe", lambda e, ip=ip, nb=nb, k=k, m=m: e.matmul(pp[ip][:, :n], lhsT=wb[:, nb, k, m * 128:(m + 1) * 128], rhs=yT[:, nb, k, :n], start=(k == 0), stop=(k == 3)),
                                 reads=[rwb, ryT], writes=[rp], chain=True)
                        if nb == 0:
                            P.op("dve", lambda e, ip=ip, ia=ia, nb=nb, m=m: e.tensor_tensor(out=acc[ia][:, :n], in0=pp[ip][:, :n], in1=gt[:, nb, m, :n], op=ALU.mult), reads=[rp, rgt], writes=[ra])
                        else:
                            it, rt = rtmp.next()
                            P.op("dve", lambda e, ip=ip, it=it, nb=nb, m=m: e.tensor_tensor(out=tmp[it][:, :n], in0=pp[ip][:, :n], in1=gt[:, nb, m, :n], op=ALU.mult), reads=[rp, rgt], writes=[rt])
                            if nb < 3:
                                P.op("pool", lambda e, ia=ia, it=it: e.tensor_tensor(out=acc[ia][:, :n], in0=acc[ia][:, :n], in1=tmp[it][:, :n], op=ALU.add), reads=[ra, rt], writes=[ra])
                            else:
                                P.op("pool", lambda e, ia=ia, it=it, m=m: e.tensor_tensor(out=mg[:, m, :n], in0=acc[ia][:, :n], in1=tmp[it][:, :n], op=ALU.add), reads=[ra, rt], writes=[rmg])
                for mo in range(8):
                    ip, rp = rpo.next()
                    for k in range(8):
                        P.op("pe", lambda e, ip=ip, mo=mo, k=k: e.matmul(po[ip][:, :n], lhsT=wo[:, k, mo * 128:(mo + 1) * 128], rhs=mg[:, k, :n], start=(k == 0), stop=(k == 7)),
                             reads=[rwo, rmg], writes=[rp], chain=True)
                    P.op("dve", lambda e, ip=ip, mo=mo, s=s, n=n: e.scalar_tensor_tensor(out=hc[:, mo, :n], in0=po[ip][:, :n], scalar=self.mod[:, 5, mo, s:s + 1], in1=hc[:, mo, :n], op0=ALU.mult, op1=ALU.add),
                         reads=[rp, self.rmod, rh], writes=[rh])
                P.dma("sp", dr["hT"][:, :, t0:t0 + n].rearrange("k p t -> p k t"), hc[:, :, :n], reads=[rh], writes=[rr["hT"]])

    K.phase_proj = phase_proj
    K.phase_mla = phase_mla
    K.phase_swa = phase_swa
    K.phase_na = phase_na
    K.phase_merge = phase_merge
    K.phase_s5 = phase_s5


install(K)
```
